# Optimizing a Trainium2 kernel written in Bass

```python
import jax
import jax.numpy as jnp
from jax import lax
import numpy as np

D_MODEL = 1024
BATCH = 4
SEQ = 4096
DEPTH = 4

D_FF = 4 * D_MODEL
NORM_EPS = 1e-6
L2_EPS = 1e-6

RET_HEADS = 4
RET_DK = D_MODEL // 8
RET_DV = D_MODEL // 8
RET_CHUNK = 128
ROPE_BASE = 10000.0

SSD_HEADS = 8
SSD_HEAD_DIM = D_MODEL // 16
SSD_GROUPS = 2
SSD_STATE = 128
SSD_CONV = 4
SSD_CHUNK = 128
DT_MIN = 1e-3
DT_MAX = 1e-1

GDN_HEADS = 4
GDN_DK = D_MODEL // 8
GDN_DV = D_MODEL // 8
GDN_CONV = 4
GDN_CHUNK = 64

RET_QK = RET_HEADS * RET_DK
RET_V = RET_HEADS * RET_DV
SSD_D = SSD_HEADS * SSD_HEAD_DIM
SSD_BC = SSD_GROUPS * SSD_STATE
SSD_XBC = SSD_D + 2 * SSD_BC
GDN_QK = GDN_HEADS * GDN_DK
GDN_V = GDN_HEADS * GDN_DV
GDN_QKV = 2 * GDN_QK + GDN_V
MIX_WIDTH = RET_V + SSD_D + GDN_V
IN_SPLITS = (RET_QK, RET_QK, RET_V, RET_V, SSD_D, SSD_XBC, SSD_HEADS, GDN_QKV, GDN_V, GDN_HEADS, GDN_HEADS)
D_IN = sum(IN_SPLITS)

kernel_name = 'hybrid_retention_ssd_gdn_trunk'

F32 = jnp.float32


def _split_offsets():
    offs, acc = [], 0
    for size in IN_SPLITS[:-1]:
        acc += size
        offs.append(acc)
    return offs


def _rms(t, eps=NORM_EPS):
    t = t.astype(F32)
    return t * lax.rsqrt(jnp.mean(t * t, axis=-1, keepdims=True) + eps)


def _rms_norm(x, w):
    return (_rms(x) * w).astype(x.dtype)


def _l2norm(t):
    return t * lax.rsqrt(jnp.sum(t * t, axis=-1, keepdims=True) + L2_EPS)


def _causal_conv(t, w):
    k = w.shape[0]
    return lax.conv_general_dilated(
        t, w[:, None, :].astype(t.dtype), (1,), [(k - 1, 0)],
        dimension_numbers=('NWC', 'WIO', 'NWC'), feature_group_count=t.shape[-1])


def _rotary(t, positions):
    half = t.shape[-1] // 2
    inv_freq = ROPE_BASE ** (-jnp.arange(half, dtype=F32) / half)
    ang = positions.astype(F32)[:, :, None] * inv_freq
    cos = jnp.cos(ang)[:, :, None, :]
    sin = jnp.sin(ang)[:, :, None, :]
    t1, t2 = t[..., :half], t[..., half:]
    return jnp.concatenate([t1 * cos - t2 * sin, t2 * cos + t1 * sin], axis=-1)


def _to_chunks(t, c):
    b, s, h, d = t.shape
    return t.reshape(b, s // c, c, h, d).transpose(0, 3, 1, 2, 4)


def _from_chunks(t):
    b, h, n, c, d = t.shape
    return t.transpose(0, 2, 3, 1, 4).reshape(b, n * c, h, d)


def _prev_chunk_states(decay, inc):
    def step(state, xs):
        d, i = xs
        return d * state + i, state
    _, prev = lax.scan(step, jnp.zeros_like(inc[0]), (decay, inc))
    return prev


def _retention(q, k, v, positions):
    c = RET_CHUNK
    dk = q.shape[-1]
    q = _rotary(q.astype(F32), positions) * (dk ** -0.5)
    k = _rotary(k.astype(F32), positions)
    q, k, v = (_to_chunks(t, c) for t in (q, k, v.astype(F32)))
    h, n = q.shape[1], q.shape[2]
    log_gamma = jnp.log1p(-jnp.exp2(-5.0 - jnp.arange(h, dtype=F32)))
    idx = jnp.arange(c, dtype=F32)
    rel = idx[:, None] - idx[None, :]
    causal = rel >= 0
    d_intra = jnp.where(causal, jnp.exp(log_gamma[:, None, None] * jnp.where(causal, rel, 0.0)), 0.0)
    scores = jnp.einsum('bhncd,bhnmd->bhncm', q, k) * d_intra[None, :, None]
    y = jnp.einsum('bhncm,bhnme->bhnce', scores, v)
    zeta = jnp.exp(log_gamma[:, None] * (c - 1 - idx))
    kv = jnp.einsum('bhnmd,hm,bhnme->nbhde', k, zeta, v)
    chunk_decay = jnp.broadcast_to(jnp.exp(log_gamma * c)[None, None, :, None, None], (n, 1, h, 1, 1))
    prev = _prev_chunk_states(chunk_decay, kv)
    xi = jnp.exp(log_gamma[:, None] * (idx + 1))
    y = y + jnp.einsum('bhncd,nbhde,hc->bhnce', q, prev, xi)
    return _from_chunks(y)


def _retention_mixer(q, k, v, gate, positions, norm_w):
    b, s, _ = q.shape
    y = _retention(q.reshape(b, s, RET_HEADS, RET_DK), k.reshape(b, s, RET_HEADS, RET_DK),
                   v.reshape(b, s, RET_HEADS, RET_DV), positions)
    y = _rms(y) * norm_w.reshape(RET_HEADS, RET_DV)
    return y.reshape(b, s, RET_V) * jax.nn.silu(gate.astype(F32))


def _ssd_mixer(z, xbc, dt_raw, conv_w, conv_b, dt_bias, a_log, d_skip, norm_w):
    b, s, _ = xbc.shape
    l = SSD_CHUNK
    n = s // l
    g = SSD_GROUPS
    r = SSD_HEADS // g
    p = SSD_HEAD_DIM
    ks = SSD_STATE
    xbc = jax.nn.silu((_causal_conv(xbc, conv_w) + conv_b).astype(F32))
    xs, bm, cm = jnp.split(xbc, [SSD_D, SSD_D + SSD_BC], axis=-1)
    xs = xs.reshape(b, n, l, g, r, p)
    bm = bm.reshape(b, n, l, g, ks)
    cm = cm.reshape(b, n, l, g, ks)
    dt = jax.nn.softplus(dt_raw.astype(F32) + dt_bias).reshape(b, n, l, g, r)
    a = -jnp.exp(a_log.astype(F32)).reshape(g, r)
    a_cs = jnp.cumsum(dt * a, axis=2)
    causal = jnp.tril(jnp.ones((l, l), bool))[None, None, :, :, None, None]
    seg = a_cs[:, :, :, None] - a_cs[:, :, None, :]
    decay = jnp.where(causal, jnp.exp(jnp.where(causal, seg, 0.0)), 0.0)
    xdt = xs * dt[..., None]
    cb = jnp.einsum('bnlgk,bnmgk->bnlmg', cm, bm)
    y = jnp.einsum('bnlmg,bnlmgr,bnmgrp->bnlgrp', cb, decay, xdt)
    to_end = jnp.exp(a_cs[:, :, -1:] - a_cs)
    states = jnp.einsum('bnmgk,bnmgr,bnmgrp->nbgrpk', bm, to_end, xdt)
    chunk_decay = jnp.exp(a_cs[:, :, -1]).transpose(1, 0, 2, 3)[..., None, None]
    prev = _prev_chunk_states(chunk_decay, states)
    y = y + jnp.einsum('bnlgk,nbgrpk,bnlgr->bnlgrp', cm, prev, jnp.exp(a_cs))
    y = y + xs * d_skip.astype(F32).reshape(g, r)[:, :, None]
    gsz = SSD_D // g
    y = y.reshape(b, s, g, gsz) * jax.nn.silu(z.astype(F32)).reshape(b, s, g, gsz)
    return _rms(y).reshape(b, s, SSD_D) * norm_w


def _gated_delta_net(qkv, z, b_raw, a_raw, conv_w, dt_bias, a_log, norm_w):
    b, s, _ = qkv.shape
    h, dk, dv, c = GDN_HEADS, GDN_DK, GDN_DV, GDN_CHUNK
    qkv = jax.nn.silu(_causal_conv(qkv, conv_w).astype(F32))
    q, k, v = jnp.split(qkv, [GDN_QK, 2 * GDN_QK], axis=-1)
    q = _l2norm(q.reshape(b, s, h, dk)) * (dk ** -0.5)
    k = _l2norm(k.reshape(b, s, h, dk))
    v = v.reshape(b, s, h, dv)
    beta = jax.nn.sigmoid(b_raw.astype(F32))[..., None]
    g = -jnp.exp(a_log.astype(F32)) * jax.nn.softplus(a_raw.astype(F32) + dt_bias)
    q, k, v, kb, vb = (_to_chunks(t, c) for t in (q, k, v, k * beta, v * beta))
    g_cs = jnp.cumsum(_to_chunks(g[..., None], c)[..., 0], axis=-1)
    incl = jnp.tril(jnp.ones((c, c), bool))
    strict = jnp.tril(jnp.ones((c, c), bool), -1)
    diff = g_cs[..., :, None] - g_cs[..., None, :]
    decay = jnp.where(incl, jnp.exp(jnp.where(incl, diff, 0.0)), 0.0)
    lower = jnp.where(strict, jnp.einsum('bhncd,bhnmd->bhncm', kb, k) * decay, 0.0)
    rhs = jnp.concatenate([vb, kb * jnp.exp(g_cs)[..., None]], axis=-1)
    sol = lax.linalg.triangular_solve(lower + jnp.eye(c, dtype=F32), rhs,
                                      left_side=True, lower=True, unit_diagonal=True)
    u, w = sol[..., :dv], sol[..., dv:]
    attn = jnp.einsum('bhncd,bhnmd->bhncm', q, k) * decay
    g_last = g_cs[..., -1:]
    q_dec = q * jnp.exp(g_cs)[..., None]
    k_dec = k * jnp.exp(g_last - g_cs)[..., None]
    chunk_decay = jnp.exp(g_last)[..., None]

    def step(state, xs):
        qd, kd, u_n, w_n, a_n, dec = xs
        v_new = u_n - jnp.einsum('bhck,bhkv->bhcv', w_n, state)
        o = jnp.einsum('bhck,bhkv->bhcv', qd, state) + jnp.einsum('bhcm,bhmv->bhcv', a_n, v_new)
        state = state * dec + jnp.einsum('bhck,bhcv->bhkv', kd, v_new)
        return state, o

    xs = tuple(jnp.moveaxis(t, 2, 0) for t in (q_dec, k_dec, u, w, attn, chunk_decay))
    _, o = lax.scan(step, jnp.zeros((b, h, dk, dv), F32), xs)
    o = _from_chunks(jnp.moveaxis(o, 0, 2))
    o = _rms(o) * norm_w * jax.nn.silu(z.astype(F32)).reshape(b, s, h, dv)
    return o.reshape(b, s, GDN_V)


def setup_inputs(seed: int = 0) -> dict:
    key = jax.random.key(seed)
    ks = jax.random.split(key, 24)

    def nrm(k, shape, scale):
        return scale * jax.random.normal(k, shape, F32)

    def gain(k, shape):
        return 1.0 + 0.02 * jax.random.normal(k, shape, F32)

    def dt_bias(k, shape):
        dt = jnp.exp(jax.random.uniform(k, shape, F32, jnp.log(DT_MIN), jnp.log(DT_MAX)))
        return dt + jnp.log(-jnp.expm1(-dt))

    def a_log(k, shape):
        return jnp.log(jax.random.uniform(k, shape, F32, 1.0, 16.0))

    x = jax.random.normal(ks[0], (BATCH, SEQ, D_MODEL), F32)
    offset = jax.random.randint(ks[1], (BATCH, 1), 0, 2048, jnp.int32)
    positions = offset + jnp.arange(SEQ, dtype=jnp.int32)[None, :]
    return {
        'x': x,
        'positions': positions,
        'mix_norm_w': gain(ks[2], (DEPTH, D_MODEL)),
        'w_in': nrm(ks[3], (DEPTH, D_MODEL, D_IN), D_MODEL ** -0.5),
        'ret_norm_w': gain(ks[4], (DEPTH, RET_V)),
        'ssd_conv_w': nrm(ks[5], (DEPTH, SSD_CONV, SSD_XBC), SSD_CONV ** -0.5),
        'ssd_conv_b': nrm(ks[6], (DEPTH, SSD_XBC), 0.02),
        'ssd_dt_bias': dt_bias(ks[7], (DEPTH, SSD_HEADS)),
        'ssd_a_log': a_log(ks[8], (DEPTH, SSD_HEADS)),
        'ssd_d': 1.0 + nrm(ks[9], (DEPTH, SSD_HEADS), 0.1),
        'ssd_norm_w': gain(ks[10], (DEPTH, SSD_D)),
        'gdn_conv_w': nrm(ks[11], (DEPTH, GDN_CONV, GDN_QKV), GDN_CONV ** -0.5),
        'gdn_dt_bias': dt_bias(ks[12], (DEPTH, GDN_HEADS)),
        'gdn_a_log': a_log(ks[13], (DEPTH, GDN_HEADS)),
        'gdn_norm_w': gain(ks[14], (DEPTH, GDN_DV)),
        'w_out': nrm(ks[15], (DEPTH, MIX_WIDTH, D_MODEL), MIX_WIDTH ** -0.5),
        'mlp_norm_w': gain(ks[16], (DEPTH, D_MODEL)),
        'w_up': nrm(ks[17], (DEPTH, D_MODEL, D_FF), D_MODEL ** -0.5),
        'w_down': nrm(ks[18], (DEPTH, D_FF, D_MODEL), D_FF ** -0.5),
        'final_norm_w': gain(ks[19], (D_MODEL,)),
    }


def reference(x, positions, mix_norm_w, w_in, ret_norm_w, ssd_conv_w, ssd_conv_b, ssd_dt_bias,
              ssd_a_log, ssd_d, ssd_norm_w, gdn_conv_w, gdn_dt_bias, gdn_a_log, gdn_norm_w,
              w_out, mlp_norm_w, w_up, w_down, final_norm_w):
    offsets = _split_offsets()
    for l in range(DEPTH):
        h = _rms_norm(x, mix_norm_w[l])
        rq, rk, rv, rg, sz, sxbc, sdt, gqkv, gz, gb, ga = jnp.split(h @ w_in[l], offsets, axis=-1)
        y_ret = _retention_mixer(rq, rk, rv, rg, positions, ret_norm_w[l])
        y_ssd = _ssd_mixer(sz, sxbc, sdt, ssd_conv_w[l], ssd_conv_b[l], ssd_dt_bias[l],
                           ssd_a_log[l], ssd_d[l], ssd_norm_w[l])
        y_gdn = _gated_delta_net(gqkv, gz, gb, ga, gdn_conv_w[l], gdn_dt_bias[l],
                                 gdn_a_log[l], gdn_norm_w[l])
        y = jnp.concatenate([y_ret, y_ssd, y_gdn], axis=-1).astype(x.dtype)
        x = x + y @ w_out[l]
        hm = _rms_norm(x, mlp_norm_w[l]) @ w_up[l]
        x = x + jnp.square(jax.nn.relu(hm)) @ w_down[l]
    return _rms_norm(x, final_norm_w)
```

```python
import math
from contextlib import ExitStack

import numpy as np
import concourse.bass as bass
import concourse.mybir as mybir
from concourse.bass_utils import run_bass_kernel_spmd

F32 = mybir.dt.float32
BF16 = mybir.dt.bfloat16
I32 = mybir.dt.int32
ALU = mybir.AluOpType
AF = mybir.ActivationFunctionType
AX = mybir.AxisListType

D = 1024
DIN = 5648
DFF = 4096
MIXW = 1536
EPS = 1e-6
EPOCH_LEN = 12000
NEG = -30000.0


class Buf:
    def __init__(self, name):
        self.name = name
        self.w = None
        self.r = []


class Prod:
    def __init__(self, K, name, handle):
        self.K = K
        self.name = name
        self.h = handle
        self.sems = []
        self.epoch = -1
        self.cnt = 0
        self.pending = False
        self.new_epoch()

    def new_epoch(self):
        self.sems.append(self.K.new_sem(f"{self.name}_e{len(self.sems)}"))
        self.epoch += 1
        self.cnt = 0


class T:
    def __init__(self, t, name):
        self.t = t
        self.b = Buf(name)

    def __getitem__(self, idx):
        return self.t[idx]


def _bufs(lst):
    out = []
    for x in lst:
        if isinstance(x, (list, tuple)):
            out.extend(_bufs(x))
        elif isinstance(x, T):
            out.append(x.b)
        else:
            out.append(x)
    return out


class Kern:
    def __init__(self, nc, stack):
        self.nc = nc
        self.stack = stack
        self.nsem = 0
        self.prods = {}
        for nm, h in (("pe", nc.tensor), ("act", nc.scalar), ("dve", nc.vector),
                      ("pool", nc.gpsimd), ("sp", nc.sync)):
            self.prods[nm] = Prod(self, nm, h)
        self.engines = ["pe", "act", "dve", "pool", "sp"]
        self.waited = {}
        self.ninst = {k: 0 for k in self.engines}
        self.banks = []
        self.bank_i = 0

    def new_sem(self, name):
        self.nsem += 1
        return self.stack.enter_context(self.nc.semaphore(name))

    def dma_slot(self, name):
        p = Prod(self, "dma_" + name, None)
        self.prods[p.name] = p
        return p

    def sb(self, stack, name, shape, dtype):
        self.nsb = getattr(self, "nsb", 0) + 1
        name = f"s{self.nsb}_{name}"
        return T(stack.enter_context(self.nc.sbuf_tensor(name, list(shape), dtype)), name)

    def make_banks(self):
        for i in range(8):
            t = self.stack.enter_context(self.nc.psum_tensor(f"bank{i}", [128, 512], F32))
            self.banks.append(T(t, f"bank{i}"))

    def bank(self):
        b = self.banks[self.bank_i % 8]
        self.bank_i += 1
        return b

    def _need(self, eng, reads, writes):
        need = {}

        def add(tok):
            if tok is None:
                return
            p, ep, c = tok
            if p is eng and eng.name == "pe":
                return
            key = (p.name, ep)
            if need.get(key, (None, 0))[1] < c:
                need[key] = (p, c)

        for b in reads:
            add(b.w)
        for b in writes:
            add(b.w)
            for t in b.r:
                add(t)
        for (pname, ep), (p, c) in need.items():
            wk = (eng.name, pname, ep)
            if self.waited.get(wk, 0) >= c:
                continue
            if ep == p.epoch:
                assert c <= p.cnt, f"wait on un-issued inc: {eng.name} waits {pname} {c}>{p.cnt}"
            eng.h.wait_ge(p.sems[ep], c)
            self.waited[wk] = c

    def _mark(self, tok, reads, writes):
        for b in reads:
            b.r.append(tok)
            if len(b.r) > 48:
                best = {}
                for (p, ep, c) in b.r:
                    k = (p.name, ep)
                    if k not in best or best[k][2] < c:
                        best[k] = (p, ep, c)
                b.r = list(best.values())
        for b in writes:
            b.w = tok
            b.r = []

    def op(self, engname, fn, R=(), W=(), inc=True):
        reads, writes = _bufs(R), _bufs(W)
        eng = self.prods[engname]
        self._need(eng, reads, writes)
        inst = fn(eng.h)
        self.ninst[engname] += 1
        if inc:
            if eng.cnt + 1 > EPOCH_LEN:
                assert not eng.pending
                eng.new_epoch()
            eng.cnt += 1
            inst.then_inc(eng.sems[eng.epoch], 1)
            eng.pending = False
            tok = (eng, eng.epoch, eng.cnt)
        else:
            if eng.cnt + 1 > EPOCH_LEN and not eng.pending:
                eng.new_epoch()
            eng.pending = True
            tok = (eng, eng.epoch, eng.cnt + 1)
        self._mark(tok, reads, writes)
        return inst

    def dma(self, qname, slot, out, in_, R=(), W=(), **kw):
        reads, writes = _bufs(R), _bufs(W)
        q = self.prods[qname]
        self._need(q, reads, writes)
        inst = q.h.dma_start(out=out, in_=in_, **kw)
        self.ninst[qname] += 1
        if slot.cnt + 16 > EPOCH_LEN:
            slot.new_epoch()
        slot.cnt += 16
        inst.then_inc(slot.sems[slot.epoch], 16)
        tok = (slot, slot.epoch, slot.cnt)
        self._mark(tok, reads, writes)
        return inst

    def barrier(self):
        for en in self.engines:
            e = self.prods[en]
            for p in self.prods.values():
                assert not p.pending
                if p.cnt == 0:
                    continue
                wk = (e.name, p.name, p.epoch)
                if self.waited.get(wk, 0) >= p.cnt:
                    continue
                e.h.wait_ge(p.sems[p.epoch], p.cnt)
                self.waited[wk] = p.cnt

    def tt(self, eng, out, in0, in1, op, R, W):
        return self.op(eng, lambda h: h.tensor_tensor(out=out, in0=in0, in1=in1, op=op), R, W)

    def ts(self, eng, out, in0, s1, s2, op0, op1, R, W):
        if s2 is None:
            return self.op(eng, lambda h: h.tensor_scalar(out=out, in0=in0, scalar1=s1, scalar2=None, op0=op0), R, W)
        return self.op(eng, lambda h: h.tensor_scalar(out=out, in0=in0, scalar1=s1, scalar2=s2, op0=op0, op1=op1), R, W)

    def stt(self, eng, out, in0, scalar, in1, op0, op1, R, W):
        return self.op(eng, lambda h: h.scalar_tensor_tensor(out=out, in0=in0, scalar=scalar, in1=in1,
                                                             op0=op0, op1=op1), R, W)

    def act(self, out, in_, func, R, W, **kw):
        return self.op("act", lambda h: h.activation(out=out, in_=in_, func=func, **kw), R, W)

    def cp(self, eng, out, in_, R, W):
        if eng == "act":
            return self.op("act", lambda h: h.copy(out=out, in_=in_), R, W)
        return self.op(eng, lambda h: h.tensor_copy(out=out, in_=in_), R, W)

    def mm(self, out, lhsT, rhs, start, stop, R, W, inc):
        return self.op("pe", lambda h: h.matmul(out, lhsT=lhsT, rhs=rhs, start=start, stop=stop), R, W, inc=inc)

    def tr(self, out, in_, ident, R, W, inc):
        return self.op("pe", lambda h: h.transpose(out=out, in_=in_, identity=ident), R, W, inc=inc)

    def rsum(self, out, in_, R, W):
        return self.op("dve", lambda h: h.tensor_reduce(out=out, in_=in_, axis=AX.X, op=ALU.add), R, W)

    def recip(self, out, in_, R, W):
        return self.op("dve", lambda h: h.reciprocal(out=out, in_=in_), R, W)


C_OFF = {}


def _const_table():
    cols = []

    def add(name, arr):
        arr = np.asarray(arr, np.float32)
        full = np.zeros((128, arr.shape[1]), np.float32)
        full[:arr.shape[0]] = arr
        C_OFF[name] = (sum(c.shape[1] for c in cols), arr.shape[1])
        cols.append(full)

    i = np.arange(128)
    m, l = i[:, None], i[None, :]
    same = (m // 64) == (l // 64)
    add("ident", np.eye(128))
    add("tri", (m <= l))
    add("tribd", (m <= l) & same)
    add("ones", np.ones((128, 128)))
    add("blockones", same)
    add("bs0", np.broadcast_to(m < 64, (128, 128)))
    add("bs1", np.broadcast_to(m >= 64, (128, 128)))
    add("negssd", np.where(l >= m, 0.0, NEG))
    add("negI", np.where((l >= m) & same, 0.0, NEG))
    add("posS", np.where((m > l) & same, 0.0, -NEG))
    sel8 = np.zeros((8, 8 * 128))
    for h in range(8):
        sel8[h, h * 128:(h + 1) * 128] = 1.0
    add("sel8", sel8)
    gam = 1.0 - np.exp2(-5.0 - np.arange(4))
    lg = np.log(gam)
    maskP = np.zeros((128, 512))
    for h in range(4):
        maskP[:, h * 128:(h + 1) * 128] = np.where(l >= m, np.exp(-lg[h] * (m + 1.0)), 0.0)
    add("maskP", maskP)
    add("xiq", np.exp(lg[None, :] * (i[:, None] + 1.0)) * (128.0 ** -0.5))
    add("zeta", np.exp(lg[None, :] * (127.0 - i[:, None])))
    invf = 10000.0 ** (-np.arange(64, dtype=np.float32) / 64.0)
    add("invf", np.broadcast_to(invf[None, :].astype(np.float32), (128, 64)))
    return np.concatenate(cols, axis=1), [float(g ** 128) for g in gam]


CONST_TABLE, RET_G128 = _const_table()
NCONST = CONST_TABLE.shape[1]

PF = {"mixnw": (0, 8), "mlpnw": (8, 8), "ynw": (16, 12), "scw": (28, 32), "scb": (60, 8), "gcw": (68, 48)}
NPF = 116
PT = {"bias16": (0, 16), "alog": (16, 12), "dskip": (28, 8)}
NPT = 36


def _pack_params(inp, depth):
    pf = np.zeros((depth, 128, NPF), np.float32)
    pt = np.zeros((depth, 128, NPT), np.float32)
    for l in range(depth):
        pf[l, :, 0:8] = inp["mix_norm_w"][l].reshape(8, 128).T
        pf[l, :, 8:16] = inp["mlp_norm_w"][l].reshape(8, 128).T
        pf[l, :, 16:20] = inp["ret_norm_w"][l].reshape(4, 128).T
        pf[l, :, 20:24] = inp["ssd_norm_w"][l].reshape(4, 128).T
        pf[l, :, 24:28] = np.repeat(inp["gdn_norm_w"][l].reshape(128, 1), 4, axis=1)
        pf[l, :, 28:60] = inp["ssd_conv_w"][l].reshape(4, 8, 128).transpose(2, 1, 0).reshape(128, 32)
        pf[l, :, 60:68] = inp["ssd_conv_b"][l].reshape(8, 128).T
        pf[l, :, 68:116] = inp["gdn_conv_w"][l].reshape(4, 12, 128).transpose(2, 1, 0).reshape(128, 48)
        pt[l, :, 0:8] = inp["ssd_dt_bias"][l][None, :]
        pt[l, :, 12:16] = inp["gdn_dt_bias"][l][None, :]
        pt[l, :, 16:24] = inp["ssd_a_log"][l][None, :]
        pt[l, :, 24:28] = inp["gdn_a_log"][l][None, :]
        pt[l, :, 28:36] = inp["ssd_d"][l][None, :]
    return pf, pt


GSTAGE = 99
PHASES = {"init", "ret", "ssd", "gdn", "out", "mlp", "final"}


def _phase(name):
    if name in PHASES:
        with ExitStack() as s:
            yield s


def build_program(NT, DEPTH, debug=False):
    S = NT * 128
    nc = bass.Bass("TRN2", target_bir_lowering=False)
    x_in = nc.dram_tensor("x", [S, D], F32, kind="ExternalInput").ap()
    pos_in = nc.dram_tensor("pos", [128, NT], I32, kind="ExternalInput").ap()
    w_in = nc.dram_tensor("w_in", [DEPTH, D, DIN], F32, kind="ExternalInput").ap()
    w_out = nc.dram_tensor("w_out", [DEPTH, MIXW, D], F32, kind="ExternalInput").ap()
    w_up = nc.dram_tensor("w_up", [DEPTH, D, DFF], F32, kind="ExternalInput").ap()
    w_down = nc.dram_tensor("w_down", [DEPTH, DFF, D], F32, kind="ExternalInput").ap()
    pf_in = nc.dram_tensor("pf", [DEPTH, 128, NPF], F32, kind="ExternalInput").ap()
    pt_in = nc.dram_tensor("pt", [DEPTH, 128, NPT], F32, kind="ExternalInput").ap()
    fnw_in = nc.dram_tensor("fnw", [128, D], F32, kind="ExternalInput").ap()
    const_in = nc.dram_tensor("consts", [128, NCONST], F32, kind="ExternalInput").ap()
    y_out = nc.dram_tensor("y", [S, D], F32, kind="ExternalOutput").ap()
    dk = "ExternalOutput" if debug else "Internal"
    xb = nc.dram_tensor("xb", [S, D], F32, kind=dk).ap()
    yc = nc.dram_tensor("yc", [S, MIXW], F32, kind=dk).ap()
    csd = nc.dram_tensor("csd", [128, NT, 128], F32, kind="Internal").ap()

    with ExitStack() as st:
        st.enter_context(nc.allow_low_precision("bf16 matmul operands, fp32 accumulation"))
        K = Kern(nc, st)
        K.make_banks()
        xb_b = [Buf(f"xb{t}") for t in range(NT)]
        yc_b = [[Buf(f"yc{t}_{j}") for j in range(3)] for t in range(NT)]
        y_b = [Buf(f"y{t}") for t in range(NT)]
        csd_b = Buf("csd")
        s_c = K.dma_slot("c")
        s_pf = K.dma_slot("pf")
        s_pt = K.dma_slot("pt")
        s_cs = K.dma_slot("cs")
        s_w = K.dma_slot("w")
        s_w2 = K.dma_slot("w2")
        s_x = [K.dma_slot(f"x{i}") for i in range(2)]
        s_a = [K.dma_slot(f"a{i}") for i in range(2)]
        s_o = [K.dma_slot(f"o{i}") for i in range(2)]

        CT = K.sb(st, "consts", [128, NCONST], F32)
        K.dma("sp", s_c, CT[:], const_in, W=[CT])

        def C(name, rows=128):
            o, n = C_OFF[name]
            return CT[0:rows, o:o + n]

        ident = C("ident")

        def xsrc(l, t):
            if l == 0:
                return x_in[t * 128:(t + 1) * 128, :], []
            return xb[t * 128:(t + 1) * 128, :], [xb_b[t]]

        def norm_hT(ph, xt, nwcols, hf, hT, sm):
            K.act(hf[:], xt[:], AF.Square, [xt], [hf, sm], accum_out=sm[:, 0:1])
            K.act(sm[:, 1:2], sm[:, 0:1], AF.Ln, [sm], [sm], scale=1.0 / D, bias=EPS)
            K.act(sm[:, 2:3], sm[:, 1:2], AF.Exp, [sm], [sm], scale=-0.5)
            K.ts("dve", hf[:], xt[:], sm[:, 2:3], None, ALU.mult, None, [xt, sm], [hf])
            for b in range(2):
                bk = K.bank()
                for j in range(4):
                    k = b * 4 + j
                    K.tr(bk[:, j * 128:(j + 1) * 128], hf[:, k * 128:(k + 1) * 128], ident, [hf, CT], [bk], inc=(j == 3))
                K.tt("dve", hT[:, b * 4:b * 4 + 4, :], bk[:, :].rearrange("p (c e) -> p c e", c=4),
                     nwcols[:, b * 4:b * 4 + 4].unsqueeze(2).to_broadcast([128, 4, 128]), ALU.mult, [bk, ph["pfm"]], [hT])

        def proj_tm(hT, WB, c0, n, out_ap, outT, eng):
            bk = K.bank()
            for k in range(8):
                K.mm(bk[:, 0:n], hT[:, k, :], WB[:, k, c0:c0 + n], k == 0, k == 7, [hT, WB], [bk], inc=(k == 7))
            K.cp(eng, out_ap, bk[:, 0:n], [bk], [outT])

        def proj_fm(hT, WB, c0, nchunk, cx, j0):
            bk = K.bank()
            for j in range(nchunk):
                for k in range(8):
                    K.mm(bk[:, j * 128:(j + 1) * 128], WB[:, k, c0 + j * 128:c0 + (j + 1) * 128], hT[:, k, :],
                         k == 0, k == 7, [hT, WB], [bk], inc=(j == nchunk - 1 and k == 7))
            K.cp("act", cx[:, j0:j0 + nchunk, 3:131],
                 bk[:, 0:nchunk * 128].rearrange("p (c e) -> p c e", c=nchunk), [bk], [cx])

        def conv_silu(cx, cw, cb, nch, acc, tmp):
            for c0 in range(0, nch, 4):
                a = acc[:, c0:c0 + 4, :]
                for j in range(4):
                    wj = cw[:, c0:c0 + 4, j:j + 1].to_broadcast([128, 4, 128])
                    if j == 0:
                        K.tt("dve", a, cx[:, c0:c0 + 4, 0:128], wj, ALU.mult, [cx, ph_cur["pfm"]], [acc])
                    else:
                        K.tt("dve", tmp[:, 0:4, :], cx[:, c0:c0 + 4, j:j + 128], wj, ALU.mult, [cx, ph_cur["pfm"]], [tmp])
                        K.tt("dve", a, a, tmp[:, 0:4, :], ALU.add, [acc, tmp], [acc])
                if cb is not None:
                    K.tt("dve", a, a, cb[:, c0:c0 + 4].unsqueeze(2).to_broadcast([128, 4, 128]), ALU.add,
                         [acc, ph_cur["pfm"]], [acc])
                K.act(tmp[:, 0:4, :], a, AF.Exp, [acc], [tmp], scale=-1.0)
                K.ts("dve", tmp[:, 0:4, :], tmp[:, 0:4, :], 1.0, None, ALU.add, None, [tmp], [tmp])
                K.recip(tmp[:, 0:4, :], tmp[:, 0:4, :], [tmp], [tmp])
                K.tt("dve", a, a, tmp[:, 0:4, :], ALU.mult, [acc, tmp], [acc])
            K.cp("act", cx[:, :, 0:3], cx[:, :, 128:131], [cx], [cx])

        def silu_to(out_ap, outT, in_ap, inT, tmp_ap, tmpT):
            K.act(tmp_ap, in_ap, AF.Exp, [inT], [tmpT], scale=-1.0)
            K.ts("dve", tmp_ap, tmp_ap, 1.0, None, ALU.add, None, [tmpT], [tmpT])
            K.recip(tmp_ap, tmp_ap, [tmpT], [tmpT])
            K.tt("dve", out_ap, in_ap, tmp_ap, ALU.mult, [inT, tmpT], [outT])

        def rstd_from_ss(out_ap, ss_ap, smT, n, eps):
            K.act(out_ap, ss_ap, AF.Ln, [smT], [smT], scale=1.0 / n, bias=eps)
            K.act(out_ap, out_ap, AF.Exp, [smT], [smT], scale=-0.5)

        def softplus16(sp, z, sm2, n):
            K.act(sm2[:, 0:n], z, AF.Abs, [sm2, ph_cur["small"]], [sm2])
            K.act(sm2[:, 0:n], sm2[:, 0:n], AF.Exp, [sm2], [sm2], scale=-1.0)
            K.act(sm2[:, 0:n], sm2[:, 0:n], AF.Ln, [sm2], [sm2], bias=1.0)
            K.ts("dve", sp, z, 0.0, None, ALU.max, None, [ph_cur["small"]], [ph_cur["small"]])
            K.tt("dve", sp, sp, sm2[:, 0:n], ALU.add, [ph_cur["small"], sm2], [ph_cur["small"]])

        def load_w(WB_ap, WBT, src_ap, nk, slot=None):
            for k in range(nk):
                K.dma("pool", slot or s_w, WB_ap[:, k, :], src_ap[k], W=[WBT])

        ph_cur = {}

        def common_alloc(ph, l, wcols, wname):
            ph_cur.clear()
            ph["pfm"] = K.sb(ph["st"], "pfm", [128, NPF], F32)
            ph["ptm"] = K.sb(ph["st"], "ptm", [128, NPT], F32)
            K.dma("sp", s_pf, ph["pfm"][:], pf_in[l], W=[ph["pfm"]])
            K.dma("sp", s_pt, ph["ptm"][:], pt_in[l], W=[ph["ptm"]])
            ph["xt"] = [K.sb(ph["st"], f"xt{i}", [128, D], F32) for i in range(2)]
            ph["hf"] = K.sb(ph["st"], "hf", [128, D], F32)
            ph["hT"] = K.sb(ph["st"], "hT", [128, 8, 128], BF16)
            ph["sm"] = K.sb(ph["st"], "sm", [128, 8], F32)
            ph["small"] = K.sb(ph["st"], "small", [128, 128], F32)
            ph["sm2"] = K.sb(ph["st"], "sm2", [128, 16], F32)
            ph_cur.update(ph)

        def pfm(ph, name):
            o, n = PF[name]
            return ph["pfm"][:, o:o + n]

        def ptm(ph, name):
            o, n = PT[name]
            return ph["ptm"][:, o:o + n]

        def load_x(ph, l, t):
            src, rb = xsrc(l, t)
            xt = ph["xt"][t % 2]
            K.dma("sp", s_x[t % 2], xt[:], src, R=rb, W=[xt])
            return xt

        for pst in _phase("init"):
            posi = K.sb(pst, "posi", [128, NT], I32)
            posf = K.sb(pst, "posf", [128, NT], F32)
            ang = K.sb(pst, "ang", [128, NT, 64], F32)
            kk = K.sb(pst, "kk", [128, NT, 64], F32)
            cs = K.sb(pst, "cs", [128, NT, 128], F32)
            K.dma("sp", s_pf, posi[:], pos_in, W=[posi])
            K.cp("dve", posf[:], posi[:], [posi], [posf])
            K.tt("dve", ang[:], posf[:, :].unsqueeze(2).to_broadcast([128, NT, 64]),
                 C("invf").unsqueeze(1).to_broadcast([128, NT, 64]), ALU.mult, [posf, CT], [ang])
            MAGIC = 12582912.0
            TWO_PI = 2.0 * math.pi
            for which, shift in ((1, 0.0), (0, math.pi / 2.0)):
                if shift != 0.0:
                    K.ts("dve", ang[:], ang[:], shift, None, ALU.add, None, [ang], [ang])
                K.ts("dve", kk[:], ang[:], 1.0 / TWO_PI, MAGIC, ALU.mult, ALU.add, [ang], [kk])
                K.ts("dve", kk[:], kk[:], MAGIC, None, ALU.subtract, None, [kk], [kk])
                K.stt("dve", kk[:], kk[:], -TWO_PI, ang[:], ALU.mult, ALU.add, [kk, ang], [kk])
                K.ts("dve", kk[:], kk[:], math.pi, -math.pi, ALU.min, ALU.max, [kk], [kk])
                K.act(cs[:, :, which * 64:(which + 1) * 64], kk[:], AF.Sin, [kk], [cs])
            K.dma("sp", s_cs, csd, cs[:], R=[cs], W=[csd_b])
            K.barrier()

        for l in range(DEPTH):
            for pst in _phase("ret"):
                ph = {"st": pst}
                common_alloc(ph, l, None, None)
                WB = K.sb(pst, "WBr", [128, 8, 2048], BF16)
                load_w(WB, WB, [w_in[l, k * 128:(k + 1) * 128, 0:2048] for k in range(8)], 8)
                cs = K.sb(pst, "cs", [128, NT, 128], F32)
                K.dma("sp", s_cs, cs[:], csd, R=[csd_b], W=[cs])
                tm = K.sb(pst, "tm", [128, 2048], F32)
                qr = K.sb(pst, "qr", [128, 512], F32)
                kr = K.sb(pst, "kr", [128, 512], F32)
                tA = K.sb(pst, "tA", [128, 512], F32)
                qxT = K.sb(pst, "qxT", [128, 4, 128], BF16)
                krT = K.sb(pst, "krT", [128, 4, 128], BF16)
                krb = K.sb(pst, "krb", [128, 512], BF16)
                vb = K.sb(pst, "vb", [128, 512], BF16)
                vz = K.sb(pst, "vz", [128, 512], BF16)
                smT = K.sb(pst, "smT", [128, 4, 128], BF16)
                Sr = K.sb(pst, "Sr", [128, 512], F32)
                Srb = K.sb(pst, "Srb", [128, 512], BF16)
                yo = K.sb(pst, "yo", [128, 512], F32)
                sm = ph["sm"]
                K.op("dve", lambda h: h.memset(Sr[:], 0.0), [], [Sr])
                K.op("dve", lambda h: h.memset(Srb[:], 0.0), [], [Srb])
                for t in range(NT):
                    xt = load_x(ph, l, t)
                    norm_hT(ph, xt, pfm(ph, "mixnw"), ph["hf"], ph["hT"], sm)
                    for g in range(4):
                        proj_tm(ph["hT"], WB, g * 512, 512, tm[:, g * 512:(g + 1) * 512], tm, "act" if g % 2 else "dve")
                    cosb = cs[:, t, 0:64].unsqueeze(1).to_broadcast([128, 4, 64])
                    sinb = cs[:, t, 64:128].unsqueeze(1).to_broadcast([128, 4, 64])
                    for (src0, dst) in ((0, qr), (512, kr)):
                        v4 = tm[:, src0:src0 + 512].rearrange("p (h t e) -> p h t e", h=4, t=2)
                        d4 = dst[:, :].rearrange("p (h t e) -> p h t e", h=4, t=2)
                        a4 = tA[:, 0:256].rearrange("p (h e) -> p h e", h=4)
                        t1, t2 = v4[:, :, 0, :], v4[:, :, 1, :]
                        K.tt("dve", d4[:, :, 0, :], t1, cosb, ALU.mult, [tm, cs], [dst])
                        K.tt("dve", a4, t2, sinb, ALU.mult, [tm, cs], [tA])
                        K.tt("dve", d4[:, :, 0, :], d4[:, :, 0, :], a4, ALU.subtract, [dst, tA], [dst])
                        K.tt("dve", d4[:, :, 1, :], t2, cosb, ALU.mult, [tm, cs], [dst])
                        K.tt("dve", a4, t1, sinb, ALU.mult, [tm, cs], [tA])
                        K.tt("dve", d4[:, :, 1, :], d4[:, :, 1, :], a4, ALU.add, [dst, tA], [dst])
                    K.tt("dve", qr[:, :].rearrange("p (h e) -> p h e", h=4), qr[:, :].rearrange("p (h e) -> p h e", h=4),
                         C("xiq").unsqueeze(2).to_broadcast([128, 4, 128]), ALU.mult, [qr, CT], [qr])
                    for (src, dstT) in ((qr, qxT), (kr, krT)):
                        bk = K.bank()
                        for h in range(4):
                            K.tr(bk[:, h * 128:(h + 1) * 128], src[:, h * 128:(h + 1) * 128], ident, [src, CT], [bk], inc=(h == 3))
                        K.cp("act", dstT[:, :, :], bk[:, :].rearrange("p (h e) -> p h e", h=4), [bk], [dstT])
                    K.cp("act", krb[:], kr[:], [kr], [krb])
                    K.cp("act", vb[:], tm[:, 1024:1536], [tm], [vb])
                    K.tt("dve", vz[:, :].rearrange("p (h e) -> p h e", h=4), tm[:, 1024:1536].rearrange("p (h e) -> p h e", h=4),
                         C("zeta").unsqueeze(2).to_broadcast([128, 4, 128]), ALU.mult, [tm, CT], [vz])
                    bk = K.bank()
                    for h in range(4):
                        K.mm(bk[:, h * 128:(h + 1) * 128], krT[:, h, :], qxT[:, h, :], True, True, [krT, qxT], [bk], inc=(h == 3))
                    K.tt("dve", smT[:, :, :], bk[:, :].rearrange("p (h e) -> p h e", h=4),
                         C("maskP").rearrange("p (h e) -> p h e", h=4), ALU.mult, [bk, CT], [smT])
                    by = K.bank()
                    for h in range(4):
                        K.mm(by[:, h * 128:(h + 1) * 128], smT[:, h, :], vb[:, h * 128:(h + 1) * 128], True, False, [smT, vb], [by], inc=False)
                        K.mm(by[:, h * 128:(h + 1) * 128], qxT[:, h, :], Srb[:, h * 128:(h + 1) * 128], False, True, [qxT, Srb], [by], inc=(h == 3))
                    bs = K.bank()
                    for h in range(4):
                        K.mm(bs[:, h * 128:(h + 1) * 128], krb[:, h * 128:(h + 1) * 128], vz[:, h * 128:(h + 1) * 128], True, True, [krb, vz], [bs], inc=(h == 3))
                    for h in range(4):
                        K.stt("dve", Sr[:, h * 128:(h + 1) * 128], Sr[:, h * 128:(h + 1) * 128], RET_G128[h],
                              bs[:, h * 128:(h + 1) * 128], ALU.mult, ALU.add, [Sr, bs], [Sr])
                    K.cp("act", Srb[:], Sr[:], [Sr], [Srb])
                    K.act(tA[:], by[:, :], AF.Square, [by], [tA])
                    K.rsum(sm[:, 4:8], tA[:, :].rearrange("p (h e) -> p h e", h=4), [tA], [sm])
                    rstd_from_ss(sm[:, 4:8], sm[:, 4:8], sm, 128.0, EPS)
                    K.tt("dve", yo[:, :].rearrange("p (h e) -> p h e", h=4), by[:, :].rearrange("p (h e) -> p h e", h=4),
                         sm[:, 4:8].unsqueeze(2).to_broadcast([128, 4, 128]), ALU.mult, [by, sm], [yo])
                    silu_to(tA[:], tA, tm[:, 1536:2048], tm, qr[:], qr)
                    K.tt("dve", yo[:], yo[:], tA[:], ALU.mult, [yo, tA], [yo])
                    K.dma("sp", s_o[t % 2], yc[t * 128:(t + 1) * 128, 0:512], yo[:], R=[yo], W=[yc_b[t][0]])
                K.barrier()

            for pst in _phase("ssd"):
                ph = {"st": pst}
                common_alloc(ph, l, None, None)
                NW = 1544
                WB = K.sb(pst, "WBs", [128, 8, NW], BF16)
                load_w(WB, WB, [w_in[l, k * 128:(k + 1) * 128, 2048:3592] for k in range(8)], 8)
                zt = K.sb(pst, "zt", [128, 512], F32)
                cx = K.sb(pst, "cxs", [128, 8, 131], F32)
                xc = K.sb(pst, "xc", [128, 8, 128], F32)
                tmp = K.sb(pst, "ctmp", [128, 4, 128], F32)
                xs_tm = K.sb(pst, "xs_tm", [128, 512], F32)
                bm_tm = K.sb(pst, "bm_tm", [128, 256], BF16)
                bcT = K.sb(pst, "bcT", [128, 4, 128], BF16)
                decT = K.sb(pst, "decT", [128, 8, 128], F32)
                GT = K.sb(pst, "GT", [128, 8, 128], BF16)
                xdt = K.sb(pst, "xdt", [128, 512], BF16)
                xdte = K.sb(pst, "xdte", [128, 512], BF16)
                t1 = K.sb(pst, "t1", [128, 512], F32)
                t2 = K.sb(pst, "t2", [128, 512], F32)
                Ss = K.sb(pst, "Ss", [128, 512], F32)
                Ssb = K.sb(pst, "Ssb", [128, 512], BF16)
                acsT = K.sb(pst, "acsT", [8, 128], F32)
                sml = ph["small"]
                sm = ph["sm"]
                K.op("dve", lambda h: h.memset(Ss[:], 0.0), [], [Ss])
                K.op("dve", lambda h: h.memset(Ssb[:], 0.0), [], [Ssb])
                K.op("dve", lambda h: h.memset(cx[:], 0.0), [], [cx])
                K.act(sml[:, 16:24], ptm(ph, "alog")[:, 0:8], AF.Exp, [ph["ptm"]], [sml])
                K.ts("dve", sml[:, 16:24], sml[:, 16:24], -1.0, None, ALU.mult, None, [sml], [sml])
                scw = pfm(ph, "scw").rearrange("p (c j) -> p c j", c=8)
                for t in range(NT):
                    xt = load_x(ph, l, t)
                    norm_hT(ph, xt, pfm(ph, "mixnw"), ph["hf"], ph["hT"], sm)
                    proj_tm(ph["hT"], WB, 0, 512, zt[:], zt, "act")
                    proj_tm(ph["hT"], WB, 1536, 8, sml[:, 0:8], sml, "dve")
                    proj_fm(ph["hT"], WB, 512, 4, cx, 0)
                    proj_fm(ph["hT"], WB, 1024, 4, cx, 4)
                    conv_silu(cx, scw, pfm(ph, "scb"), 8, xc, tmp)
                    K.tt("dve", sml[:, 0:8], sml[:, 0:8], ptm(ph, "bias16")[:, 0:8], ALU.add, [sml, ph["ptm"]], [sml])
                    softplus16(sml[:, 88:96], sml[:, 0:8], ph["sm2"], 8)
                    K.tt("dve", sml[:, 24:32], sml[:, 88:96], sml[:, 16:24], ALU.mult, [sml], [sml])
                    bk = K.bank()
                    K.mm(bk[:, 0:8], C("tri"), sml[:, 24:32], True, True, [CT, sml], [bk], inc=False)
                    K.mm(bk[:, 8:16], C("ones"), sml[:, 24:32], True, True, [CT, sml], [bk], inc=False)
                    K.mm(bk[0:8, 128:256], sml[:, 24:32], C("tri"), True, True, [CT, sml], [bk], inc=True)
                    K.cp("dve", sml[:, 32:40], bk[:, 0:8], [bk], [sml])
                    K.ts("dve", sml[:, 40:48], bk[:, 0:8], -1.0, None, ALU.mult, None, [bk], [sml])
                    K.cp("dve", sml[:, 56:64], bk[:, 8:16], [bk], [sml])
                    K.cp("act", acsT[:, :], bk[0:8, 128:256], [bk], [acsT])
                    K.act(sml[:, 48:56], sml[:, 32:40], AF.Exp, [sml], [sml])
                    K.tt("dve", sml[:, 64:72], sml[:, 56:64], sml[:, 32:40], ALU.subtract, [sml], [sml])
                    K.act(sml[:, 64:72], sml[:, 64:72], AF.Exp, [sml], [sml])
                    K.act(sml[:, 72:80], sml[:, 56:64], AF.Exp, [sml], [sml])
                    K.tt("dve", sml[:, 80:88], sml[:, 88:96], sml[:, 64:72], ALU.mult, [sml], [sml])
                    for hb in range(2):
                        bk = K.bank()
                        for j in range(4):
                            h = hb * 4 + j
                            K.mm(bk[:, j * 128:(j + 1) * 128], ident, C("negssd"), True, False, [CT], [bk], inc=False)
                            K.mm(bk[:, j * 128:(j + 1) * 128], CT[0:8, C_OFF["sel8"][0] + h * 128:C_OFF["sel8"][0] + (h + 1) * 128],
                                 acsT[:, :], False, True, [CT, acsT], [bk], inc=(j == 3))
                        for j in range(4):
                            h = hb * 4 + j
                            K.act(decT[:, h, :], bk[:, j * 128:(j + 1) * 128], AF.Exp, [bk, sml], [decT], bias=sml[:, 40 + h:41 + h])
                    bk = K.bank()
                    for c in range(4):
                        K.tr(bk[:, c * 128:(c + 1) * 128], xc[:, c, :], ident, [xc, CT], [bk], inc=(c == 3))
                    K.cp("act", xs_tm[:], bk[:, :], [bk], [xs_tm])
                    bk = K.bank()
                    for g in range(2):
                        K.tr(bk[:, g * 128:(g + 1) * 128], xc[:, 4 + g, :], ident, [xc, CT], [bk], inc=(g == 1))
                    K.cp("act", bm_tm[:], bk[:, 0:256], [bk], [bm_tm])
                    K.cp("dve", bcT[:, :, :], xc[:, 4:8, :], [xc], [bcT])
                    bk = K.bank()
                    for g in range(2):
                        K.mm(bk[:, g * 128:(g + 1) * 128], bcT[:, g, :], bcT[:, 2 + g, :], True, True, [bcT], [bk], inc=(g == 1))
                    for g in range(2):
                        K.tt("dve", GT[:, 4 * g:4 * g + 4, :], decT[:, 4 * g:4 * g + 4, :],
                             bk[:, g * 128:(g + 1) * 128].unsqueeze(1).to_broadcast([128, 4, 128]), ALU.mult, [decT, bk], [GT])
                    xs3 = xs_tm[:, :].rearrange("p (h e) -> p h e", h=8)
                    K.tt("dve", xdt[:, :].rearrange("p (h e) -> p h e", h=8), xs3,
                         sml[:, 88:96].unsqueeze(2).to_broadcast([128, 8, 64]), ALU.mult, [xs_tm, sml], [xdt])
                    K.tt("dve", xdte[:, :].rearrange("p (h e) -> p h e", h=8), xs3,
                         sml[:, 80:88].unsqueeze(2).to_broadcast([128, 8, 64]), ALU.mult, [xs_tm, sml], [xdte])
                    by = K.bank()
                    for h in range(8):
                        K.mm(by[:, h * 64:(h + 1) * 64], GT[:, h, :], xdt[:, h * 64:(h + 1) * 64], True, True, [GT, xdt], [by], inc=(h == 7))
                    bc = K.bank()
                    for g in range(2):
                        K.mm(bc[:, g * 256:(g + 1) * 256], bcT[:, 2 + g, :], Ssb[:, g * 256:(g + 1) * 256], True, True, [bcT, Ssb], [bc], inc=(g == 1))
                    bn = K.bank()
                    for g in range(2):
                        K.mm(bn[:, g * 256:(g + 1) * 256], bm_tm[:, g * 128:(g + 1) * 128], xdte[:, g * 256:(g + 1) * 256], True, True, [bm_tm, xdte], [bn], inc=(g == 1))
                    K.tt("dve", t1[:, :].rearrange("p (h e) -> p h e", h=8), bc[:, :].rearrange("p (h e) -> p h e", h=8),
                         sml[:, 48:56].unsqueeze(2).to_broadcast([128, 8, 64]), ALU.mult, [bc, sml], [t1])
                    K.tt("dve", t1[:], t1[:], by[:, :], ALU.add, [t1, by], [t1])
                    K.tt("dve", t2[:, :].rearrange("p (h e) -> p h e", h=8), xs3,
                         ptm(ph, "dskip").unsqueeze(2).to_broadcast([128, 8, 64]), ALU.mult, [xs_tm, ph["ptm"]], [t2])
                    K.tt("dve", t1[:], t1[:], t2[:], ALU.add, [t1, t2], [t1])
                    K.tt("dve", Ss[:, :].rearrange("p (h e) -> p h e", h=8), Ss[:, :].rearrange("p (h e) -> p h e", h=8),
                         sml[:, 72:80].unsqueeze(2).to_broadcast([128, 8, 64]), ALU.mult, [Ss, sml], [Ss])
                    K.tt("dve", Ss[:], Ss[:], bn[:, :], ALU.add, [Ss, bn], [Ss])
                    K.cp("act", Ssb[:], Ss[:], [Ss], [Ssb])
                    silu_to(t2[:], t2, zt[:], zt, xs_tm[:], xs_tm)
                    K.tt("dve", t1[:], t1[:], t2[:], ALU.mult, [t1, t2], [t1])
                    K.act(t2[:], t1[:], AF.Square, [t1], [t2])
                    K.rsum(sm[:, 4:6], t2[:, :].rearrange("p (g e) -> p g e", g=2), [t2], [sm])
                    rstd_from_ss(sm[:, 4:6], sm[:, 4:6], sm, 256.0, EPS)
                    K.tt("dve", t1[:, :].rearrange("p (g e) -> p g e", g=2), t1[:, :].rearrange("p (g e) -> p g e", g=2),
                         sm[:, 4:6].unsqueeze(2).to_broadcast([128, 2, 256]), ALU.mult, [t1, sm], [t1])
                    K.dma("sp", s_o[t % 2], yc[t * 128:(t + 1) * 128, 512:1024], t1[:], R=[t1], W=[yc_b[t][1]])
                K.barrier()

            for pst in _phase("gdn"):
                ph = {"st": pst}
                common_alloc(ph, l, None, None)
                NW = 2056
                WB = K.sb(pst, "WBg", [128, 8, NW], BF16)
                load_w(WB, WB, [w_in[l, k * 128:(k + 1) * 128, 3592:5648] for k in range(8)], 8)
                zt = K.sb(pst, "zt", [128, 512], F32)
                cx = K.sb(pst, "cxg", [128, 12, 131], F32)
                gc = K.sb(pst, "gc", [128, 12, 128], F32)
                tmp = K.sb(pst, "ctmp", [128, 4, 128], F32)
                qkv = K.sb(pst, "qkv", [128, 1536], F32)
                sq = K.sb(pst, "sq", [128, 1024], F32)
                kdm = [K.sb(pst, f"kdm{i}", [128, 512], BF16) for i in range(2)]
                vn = [K.sb(pst, f"vn{i}", [128, 512], BF16) for i in range(2)]
                vf = K.sb(pst, "vf", [128, 512], BF16)
                qnT = K.sb(pst, "qnT", [128, 4, 128], BF16)
                knT = K.sb(pst, "knT", [128, 4, 128], BF16)
                decL = K.sb(pst, "decL", [128, 4, 128], F32)
                decTg = K.sb(pst, "decTg", [128, 4, 128], F32)
                Aa = [K.sb(pst, f"A{i}", [128, 4, 128], F32) for i in range(2)]
                Bb = [K.sb(pst, f"B{i}", [128, 4, 128], F32) for i in range(2)]
                Pm = K.sb(pst, "Pm", [128, 4, 128], F32)
                TTb = K.sb(pst, "TTb", [128, 4, 128], BF16)
                vbt = K.sb(pst, "vbt", [128, 512], BF16)
                kbg = K.sb(pst, "kbg", [128, 512], BF16)
                uu = K.sb(pst, "uu", [128, 512], F32)
                wT = K.sb(pst, "wT", [128, 4, 128], BF16)
                attnT = K.sb(pst, "attnT", [128, 4, 128], BF16)
                otmp = K.sb(pst, "otmp", [128, 512], F32)
                oo = K.sb(pst, "oo", [128, 512], F32)
                Sg = K.sb(pst, "Sg", [128, 512], F32)
                Sgb = K.sb(pst, "Sgb", [128, 512], BF16)
                gcsT = K.sb(pst, "gcsT", [8, 128], F32)
                sml = ph["small"]
                sm = ph["sm"]
                K.op("dve", lambda h: h.memset(Sg[:], 0.0), [], [Sg])
                K.op("dve", lambda h: h.memset(Sgb[:], 0.0), [], [Sgb])
                K.op("dve", lambda h: h.memset(cx[:], 0.0), [], [cx])
                K.op("dve", lambda h: h.memset(sml[:], 0.0), [], [sml])
                K.act(sml[:, 16:20], ptm(ph, "alog")[:, 8:12], AF.Exp, [ph["ptm"]], [sml])
                K.ts("dve", sml[:, 16:20], sml[:, 16:20], -1.0, None, ALU.mult, None, [sml], [sml])
                gcw = pfm(ph, "gcw").rearrange("p (c j) -> p c j", c=12)
                sel_o = C_OFF["sel8"][0]
                for t in range(NT):
                    xt = load_x(ph, l, t)
                    norm_hT(ph, xt, pfm(ph, "mixnw"), ph["hf"], ph["hT"], sm)
                    proj_tm(ph["hT"], WB, 1536, 512, zt[:], zt, "act")
                    proj_tm(ph["hT"], WB, 2048, 8, sml[:, 0:8], sml, "dve")
                    for j0 in range(0, 12, 4):
                        proj_fm(ph["hT"], WB, j0 * 128, 4, cx, j0)
                    conv_silu(cx, gcw, None, 12, gc, tmp)
                    if GSTAGE < 2:
                        continue
                    for j0 in range(0, 12, 4):
                        bk = K.bank()
                        for j in range(4):
                            K.tr(bk[:, j * 128:(j + 1) * 128], gc[:, j0 + j, :], ident, [gc, CT], [bk], inc=(j == 3))
                        K.cp("act" if j0 == 4 else "dve", qkv[:, j0 * 128:(j0 + 4) * 128], bk[:, :], [bk], [qkv])
                    K.act(sq[:], qkv[:, 0:1024], AF.Square, [qkv], [sq])
                    K.rsum(sml[:, 64:72], sq[:, :].rearrange("p (h e) -> p h e", h=8), [sq], [sml])
                    K.act(sml[:, 52:60], sml[:, 64:72], AF.Ln, [sml], [sml], bias=EPS)
                    K.act(sml[:, 52:60], sml[:, 52:60], AF.Exp, [sml], [sml], scale=-0.5)
                    K.ts("dve", sml[:, 52:56], sml[:, 52:56], 128.0 ** -0.5, None, ALU.mult, None, [sml], [sml])
                    K.tt("dve", qkv[:, 0:1024].rearrange("p (h e) -> p h e", h=8), qkv[:, 0:1024].rearrange("p (h e) -> p h e", h=8),
                         sml[:, 52:60].unsqueeze(2).to_broadcast([128, 8, 128]), ALU.mult, [qkv, sml], [qkv])
                    if GSTAGE < 3:
                        continue
                    K.act(sml[:, 8:12], sml[:, 0:4], AF.Exp, [sml], [sml], scale=-1.0)
                    K.ts("dve", sml[:, 8:12], sml[:, 8:12], 1.0, None, ALU.add, None, [sml], [sml])
                    K.recip(sml[:, 8:12], sml[:, 8:12], [sml], [sml])
                    K.ts("dve", sml[:, 12:16], sml[:, 8:12], -1.0, None, ALU.mult, None, [sml], [sml])
                    K.tt("dve", sml[:, 4:8], sml[:, 4:8], ptm(ph, "bias16")[:, 12:16], ALU.add, [sml, ph["ptm"]], [sml])
                    softplus16(sml[:, 76:80], sml[:, 4:8], ph["sm2"], 4)
                    K.tt("dve", sml[:, 20:24], sml[:, 76:80], sml[:, 16:20], ALU.mult, [sml], [sml])
                    bk = K.bank()
                    K.mm(bk[:, 0:4], C("tribd"), sml[:, 20:24], True, True, [CT, sml], [bk], inc=False)
                    K.mm(bk[:, 4:8], C("blockones"), sml[:, 20:24], True, True, [CT, sml], [bk], inc=False)
                    K.mm(bk[:, 8:12], C("bs0"), sml[:, 20:24], True, True, [CT, sml], [bk], inc=False)
                    K.mm(bk[:, 12:16], C("bs1"), sml[:, 20:24], True, True, [CT, sml], [bk], inc=False)
                    K.mm(bk[0:8, 128:256], sml[:, 20:28], C("tribd"), True, True, [CT, sml], [bk], inc=True)
                    K.cp("dve", sml[:, 24:28], bk[:, 0:4], [bk], [sml])
                    K.ts("dve", sml[:, 28:32], bk[:, 0:4], -1.0, None, ALU.mult, None, [bk], [sml])
                    K.cp("dve", sml[:, 32:36], bk[:, 4:8], [bk], [sml])
                    K.act(sml[:, 44:52], bk[:, 8:16], AF.Exp, [bk], [sml])
                    K.cp("act", gcsT[:, :], bk[0:8, 128:256], [bk], [gcsT])
                    K.act(sml[:, 36:40], sml[:, 24:28], AF.Exp, [sml], [sml])
                    K.tt("dve", sml[:, 40:44], sml[:, 32:36], sml[:, 24:28], ALU.subtract, [sml], [sml])
                    K.act(sml[:, 40:44], sml[:, 40:44], AF.Exp, [sml], [sml])
                    K.tt("dve", sml[:, 60:64], sml[:, 8:12], sml[:, 36:40], ALU.mult, [sml], [sml])
                    if GSTAGE < 4:
                        continue
                    qn3 = qkv[:, 0:512].rearrange("p (h e) -> p h e", h=4)
                    kn3 = qkv[:, 512:1024].rearrange("p (h e) -> p h e", h=4)
                    v3 = qkv[:, 1024:1536].rearrange("p (h e) -> p h e", h=4)
                    for i in range(2):
                        K.ts("dve", sml[:, 96 + 4 * i:100 + 4 * i], sml[:, 40:44], C("bs%d" % i)[:, 0:1], None, ALU.mult, None, [sml, CT], [sml])
                        K.tt("dve", kdm[i][:, :].rearrange("p (h e) -> p h e", h=4), kn3,
                             sml[:, 96 + 4 * i:100 + 4 * i].unsqueeze(2).to_broadcast([128, 4, 128]), ALU.mult, [qkv, sml], [kdm[i]])
                    K.tt("dve", kbg[:, :].rearrange("p (h e) -> p h e", h=4), kn3,
                         sml[:, 60:64].unsqueeze(2).to_broadcast([128, 4, 128]), ALU.mult, [qkv, sml], [kbg])
                    K.tt("dve", vbt[:, :].rearrange("p (h e) -> p h e", h=4), v3,
                         sml[:, 8:12].unsqueeze(2).to_broadcast([128, 4, 128]), ALU.mult, [qkv, sml], [vbt])
                    for (c0, dstT) in ((0, qnT), (512, knT)):
                        bk = K.bank()
                        for h in range(4):
                            K.tr(bk[:, h * 128:(h + 1) * 128], qkv[:, c0 + h * 128:c0 + (h + 1) * 128], ident, [qkv, CT], [bk], inc=(h == 3))
                        K.cp("act", dstT[:, :, :], bk[:, :].rearrange("p (h e) -> p h e", h=4), [bk], [dstT])
                    if GSTAGE < 5:
                        continue
                    for (msk, dst, scale, bcol) in (("posS", decL, -1.0, 24), ("negI", decTg, 1.0, 28)):
                        bk = K.bank()
                        for h in range(4):
                            K.mm(bk[:, h * 128:(h + 1) * 128], ident, C(msk), True, False, [CT], [bk], inc=False)
                            K.mm(bk[:, h * 128:(h + 1) * 128], CT[0:8, sel_o + h * 128:sel_o + (h + 1) * 128], gcsT[:, :],
                                 False, True, [CT, gcsT], [bk], inc=(h == 3))
                        for h in range(4):
                            K.act(dst[:, h, :], bk[:, h * 128:(h + 1) * 128], AF.Exp, [bk, sml], [dst],
                                  scale=scale, bias=sml[:, bcol + h:bcol + h + 1])
                    if GSTAGE < 6:
                        continue
                    bk = K.bank()
                    for h in range(4):
                        K.mm(bk[:, h * 128:(h + 1) * 128], knT[:, h, :], knT[:, h, :], True, True, [knT], [bk], inc=(h == 3))
                    if GSTAGE == 60:
                        continue
                    for h in range(4):
                        K.stt("dve", Bb[0][:, h, :], bk[:, h * 128:(h + 1) * 128], sml[:, 12 + h:13 + h], decL[:, h, :],
                              ALU.mult, ALU.mult, [bk, sml, decL], [Bb[0]])
                    if GSTAGE == 61:
                        continue
                    bk = K.bank()
                    for h in range(4):
                        K.tr(bk[:, h * 128:(h + 1) * 128], Bb[0][:, h, :], ident, [Bb[0], CT], [bk], inc=(h == 3))
                    if GSTAGE == 62:
                        continue
                    K.cp("act", Aa[0][:, :, :], bk[:, :].rearrange("p (h e) -> p h e", h=4), [bk], [Aa[0]])
                    if GSTAGE == 63:
                        continue
                    for h in range(4):
                        K.tt("dve", Pm[:, h, :], Aa[0][:, h, :], ident, ALU.add, [Aa[0], CT], [Pm])
                    if GSTAGE < 7:
                        continue
                    cur = 0
                    for lev in range(1, 6):
                        nxt = 1 - cur
                        if lev <= 4:
                            bk = K.bank()
                            for h in range(4):
                                K.mm(bk[:, h * 128:(h + 1) * 128], Bb[cur][:, h, :], Aa[cur][:, h, :], True, True, [Bb[cur], Aa[cur]], [bk], inc=(h == 3))
                            K.cp("act", Aa[nxt][:, :, :], bk[:, :].rearrange("p (h e) -> p h e", h=4), [bk], [Aa[nxt]])
                        bk = K.bank()
                        for h in range(4):
                            K.mm(bk[:, h * 128:(h + 1) * 128], Aa[cur][:, h, :], Bb[cur][:, h, :], True, True, [Bb[cur], Aa[cur]], [bk], inc=(h == 3))
                        K.cp("dve", Bb[nxt][:, :, :], bk[:, :].rearrange("p (h e) -> p h e", h=4), [bk], [Bb[nxt]])
                        bk = K.bank()
                        for h in range(4):
                            K.mm(bk[:, h * 128:(h + 1) * 128], Bb[nxt][:, h, :], Pm[:, h, :], True, True, [Bb[nxt], Pm], [bk], inc=(h == 3))
                        K.tt("dve", Pm[:, :, :], Pm[:, :, :], bk[:, :].rearrange("p (h e) -> p h e", h=4), ALU.add, [Pm, bk], [Pm])
                        cur = nxt
                    if GSTAGE < 8:
                        continue
                    K.cp("act", TTb[:, :, :], Pm[:, :, :], [Pm], [TTb])
                    bk = K.bank()
                    for h in range(4):
                        K.mm(bk[:, h * 128:(h + 1) * 128], TTb[:, h, :], vbt[:, h * 128:(h + 1) * 128], True, True, [TTb, vbt], [bk], inc=(h == 3))
                    K.cp("act", uu[:], bk[:, :], [bk], [uu])
                    bk = K.bank()
                    for h in range(4):
                        K.mm(bk[:, h * 128:(h + 1) * 128], kbg[:, h * 128:(h + 1) * 128], TTb[:, h, :], True, True, [TTb, kbg], [bk], inc=(h == 3))
                    K.cp("act", wT[:, :, :], bk[:, :].rearrange("p (h e) -> p h e", h=4), [bk], [wT])
                    bk = K.bank()
                    for h in range(4):
                        K.mm(bk[:, h * 128:(h + 1) * 128], knT[:, h, :], qnT[:, h, :], True, True, [knT, qnT], [bk], inc=(h == 3))
                    K.tt("dve", attnT[:, :, :], bk[:, :].rearrange("p (h e) -> p h e", h=4), decTg[:, :, :], ALU.mult, [bk, decTg], [attnT])
                    if GSTAGE < 9:
                        continue
                    for hf in range(2):
                        bk = K.bank()
                        for h in range(4):
                            K.mm(bk[:, h * 128:(h + 1) * 128], wT[:, h, :], Sgb[:, h * 128:(h + 1) * 128], True, True, [wT, Sgb], [bk], inc=(h == 3))
                        K.tt("dve", vn[hf][:], uu[:], bk[:, :], ALU.subtract, [uu, bk], [vn[hf]])
                        bk = K.bank()
                        for h in range(4):
                            K.mm(bk[:, h * 128:(h + 1) * 128], qnT[:, h, :], Sgb[:, h * 128:(h + 1) * 128], True, True,
                                 [qnT, Sgb], [bk], inc=(h == 3))
                        K.ts("dve", sml[:, 104 + 4 * hf:108 + 4 * hf], sml[:, 36:40], C("bs%d" % hf)[:, 0:1], None, ALU.mult, None, [sml, CT], [sml])
                        dst = otmp if hf == 0 else uu
                        K.tt("dve", dst[:, :].rearrange("p (h e) -> p h e", h=4), bk[:, :].rearrange("p (h e) -> p h e", h=4),
                             sml[:, 104 + 4 * hf:108 + 4 * hf].unsqueeze(2).to_broadcast([128, 4, 128]), ALU.mult, [bk, sml], [dst])
                        if hf == 1:
                            K.tt("dve", otmp[:], otmp[:], uu[:], ALU.add, [otmp, uu], [otmp])
                        bk = K.bank()
                        for h in range(4):
                            K.mm(bk[:, h * 128:(h + 1) * 128], kdm[hf][:, h * 128:(h + 1) * 128], vn[hf][:, h * 128:(h + 1) * 128],
                                 True, True, [kdm[hf], vn[hf]], [bk], inc=(h == 3))
                        K.tt("dve", Sg[:, :].rearrange("p (h e) -> p h e", h=4), Sg[:, :].rearrange("p (h e) -> p h e", h=4),
                             sml[:, 44 + 4 * hf:48 + 4 * hf].unsqueeze(2).to_broadcast([128, 4, 128]), ALU.mult, [Sg, sml], [Sg])
                        K.tt("dve", Sg[:], Sg[:], bk[:, :], ALU.add, [Sg, bk], [Sg])
                        K.cp("act", Sgb[:], Sg[:], [Sg], [Sgb])
                    K.ts("dve", vf[:], vn[0][:], C("bs0")[:, 0:1], None, ALU.mult, None, [vn[0], CT], [vf])
                    K.stt("dve", vf[:], vn[1][:], C("bs1")[:, 0:1], vf[:], ALU.mult, ALU.add, [vn[1], CT, vf], [vf])
                    bk = K.bank()
                    for h in range(4):
                        K.mm(bk[:, h * 128:(h + 1) * 128], attnT[:, h, :], vf[:, h * 128:(h + 1) * 128], True, True, [attnT, vf], [bk], inc=(h == 3))
                    K.tt("dve", oo[:], otmp[:], bk[:, :], ALU.add, [otmp, bk], [oo])
                    K.act(otmp[:], oo[:], AF.Square, [oo], [otmp])
                    K.rsum(sm[:, 4:8], otmp[:, :].rearrange("p (h e) -> p h e", h=4), [otmp], [sm])
                    rstd_from_ss(sm[:, 4:8], sm[:, 4:8], sm, 128.0, EPS)
                    K.tt("dve", oo[:, :].rearrange("p (h e) -> p h e", h=4), oo[:, :].rearrange("p (h e) -> p h e", h=4),
                         sm[:, 4:8].unsqueeze(2).to_broadcast([128, 4, 128]), ALU.mult, [oo, sm], [oo])
                    silu_to(otmp[:], otmp, zt[:], zt, uu[:], uu)
                    K.tt("dve", oo[:], oo[:], otmp[:], ALU.mult, [oo, otmp], [oo])
                    K.dma("sp", s_o[t % 2], yc[t * 128:(t + 1) * 128, 1024:1536], oo[:], R=[oo], W=[yc_b[t][2]])
                K.barrier()

            for pst in _phase("out"):
                ph = {"st": pst}
                common_alloc(ph, l, None, None)
                WB = K.sb(pst, "WBo", [128, 12, 1024], BF16)
                load_w(WB, WB, [w_out[l, k * 128:(k + 1) * 128, :] for k in range(12)], 12)
                yt = [K.sb(pst, f"yt{i}", [128, MIXW], F32) for i in range(2)]
                yT = K.sb(pst, "yT", [128, 12, 128], BF16)
                xo = [K.sb(pst, f"xo{i}", [128, D], F32) for i in range(2)]
                ynw = pfm(ph, "ynw")
                for t in range(NT):
                    xt = load_x(ph, l, t)
                    ytt = yt[t % 2]
                    K.dma("sp", s_a[t % 2], ytt[:], yc[t * 128:(t + 1) * 128, :], R=yc_b[t], W=[ytt])
                    for j0 in range(0, 12, 4):
                        bk = K.bank()
                        for j in range(4):
                            K.tr(bk[:, j * 128:(j + 1) * 128], ytt[:, (j0 + j) * 128:(j0 + j + 1) * 128], ident, [ytt, CT], [bk], inc=(j == 3))
                        K.tt("dve", yT[:, j0:j0 + 4, :], bk[:, :].rearrange("p (c e) -> p c e", c=4),
                             ynw[:, j0:j0 + 4].unsqueeze(2).to_broadcast([128, 4, 128]), ALU.mult, [bk, ph["pfm"]], [yT])
                    xot = xo[t % 2]
                    for n in range(2):
                        bk = K.bank()
                        for k in range(12):
                            K.mm(bk[:, :], yT[:, k, :], WB[:, k, n * 512:(n + 1) * 512], k == 0, k == 11, [yT, WB], [bk], inc=(k == 11))
                        K.tt("dve", xot[:, n * 512:(n + 1) * 512], xt[:, n * 512:(n + 1) * 512], bk[:, :], ALU.add, [xt, bk], [xot])
                    K.dma("sp", s_o[t % 2], xb[t * 128:(t + 1) * 128, :], xot[:], R=[xot], W=[xb_b[t]])
                K.barrier()

            for pst in _phase("mlp"):
                ph = {"st": pst}
                common_alloc(ph, l, None, None)
                WU = K.sb(pst, "WU", [128, 8, DFF], BF16)
                WD = K.sb(pst, "WD", [128, 32, D], BF16)
                load_w(WU, WU, [w_up[l, k * 128:(k + 1) * 128, :] for k in range(8)], 8)
                load_w(WD, WD, [w_down[l, k * 128:(k + 1) * 128, :] for k in range(32)], 32, s_w2)
                aT = K.sb(pst, "aT", [128, 32, 128], BF16)
                rl = K.sb(pst, "rl", [128, 512], F32)
                xo = [K.sb(pst, f"xo{i}", [128, D], F32) for i in range(2)]
                for t in range(NT):
                    xt = ph["xt"][t % 2]
                    K.dma("sp", s_x[t % 2], xt[:], xb[t * 128:(t + 1) * 128, :], R=[xb_b[t]], W=[xt])
                    norm_hT(ph, xt, pfm(ph, "mlpnw"), ph["hf"], ph["hT"], ph["sm"])
                    for f0 in range(0, 32, 4):
                        bk = K.bank()
                        for j in range(4):
                            f = f0 + j
                            for k in range(8):
                                K.mm(bk[:, j * 128:(j + 1) * 128], WU[:, k, f * 128:(f + 1) * 128], ph["hT"][:, k, :],
                                     k == 0, k == 7, [ph["hT"], WU], [bk], inc=(j == 3 and k == 7))
                        K.act(rl[:], bk[:, :], AF.Relu, [bk], [rl])
                        K.tt("dve", aT[:, f0:f0 + 4, :], rl[:, :].rearrange("p (c e) -> p c e", c=4),
                             rl[:, :].rearrange("p (c e) -> p c e", c=4), ALU.mult, [rl], [aT])
                    xot = xo[t % 2]
                    for n in range(2):
                        bk = K.bank()
                        for k in range(32):
                            K.mm(bk[:, :], aT[:, k, :], WD[:, k, n * 512:(n + 1) * 512], k == 0, k == 31, [aT, WD], [bk], inc=(k == 31))
                        K.tt("dve", xot[:, n * 512:(n + 1) * 512], xt[:, n * 512:(n + 1) * 512], bk[:, :], ALU.add, [xt, bk], [xot])
                    K.dma("sp", s_o[t % 2], xb[t * 128:(t + 1) * 128, :], xot[:], R=[xot], W=[xb_b[t]])
                K.barrier()

        for pst in _phase("final"):
            fnw = K.sb(pst, "fnw", [128, D], F32)
            K.dma("sp", s_pf, fnw[:], fnw_in, W=[fnw])
            xts = [K.sb(pst, f"fx{i}", [128, D], F32) for i in range(2)]
            hf2 = [K.sb(pst, f"fh{i}", [128, D], F32) for i in range(2)]
            sm = K.sb(pst, "fsm", [128, 8], F32)
            for t in range(NT):
                xt = xts[t % 2]
                hf = hf2[t % 2]
                K.dma("sp", s_x[t % 2], xt[:], xb[t * 128:(t + 1) * 128, :], R=[xb_b[t]], W=[xt])
                K.act(hf[:], xt[:], AF.Square, [xt], [hf, sm], accum_out=sm[:, 0:1])
                K.act(sm[:, 1:2], sm[:, 0:1], AF.Ln, [sm], [sm], scale=1.0 / D, bias=EPS)
                K.act(sm[:, 2:3], sm[:, 1:2], AF.Exp, [sm], [sm], scale=-0.5)
                K.stt("dve", hf[:], xt[:], sm[:, 2:3], fnw[:], ALU.mult, ALU.mult, [xt, sm, fnw], [hf])
                K.dma("sp", s_o[t % 2], y_out[t * 128:(t + 1) * 128, :], hf[:], R=[hf], W=[y_b[t]])
            K.barrier()
        build_program.stats = dict(K.ninst, nsem=K.nsem)
    return nc


def _run(inp, NT, DEPTH, debug=False, ncores=8):
    B = inp["x"].shape[0]
    pf, pt = _pack_params(inp, DEPTH)
    fnw = np.ascontiguousarray(np.broadcast_to(np.asarray(inp["final_norm_w"], np.float32)[None, :], (128, D)))
    nc = build_program(NT, DEPTH, debug)
    in_maps = []
    for c in range(ncores):
        b = c % B
        in_maps.append({
            "x": np.ascontiguousarray(inp["x"][b], dtype=np.float32),
            "pos": np.ascontiguousarray(np.asarray(inp["positions"][b], np.int32).reshape(NT, 128).T),
            "w_in": np.ascontiguousarray(inp["w_in"][:DEPTH], dtype=np.float32),
            "w_out": np.ascontiguousarray(inp["w_out"][:DEPTH], dtype=np.float32),
            "w_up": np.ascontiguousarray(inp["w_up"][:DEPTH], dtype=np.float32),
            "w_down": np.ascontiguousarray(inp["w_down"][:DEPTH], dtype=np.float32),
            "pf": pf, "pt": pt, "fnw": fnw, "consts": CONST_TABLE,
        })
    res = run_bass_kernel_spmd(nc, in_maps, core_ids=list(range(ncores)))
    return res


def kernel(**inputs):
    inp = {k: np.asarray(v) for k, v in inputs.items()}
    B, S, _ = inp["x"].shape
    res = _run(inp, S // 128, inp["w_in"].shape[0])
    out = np.stack([np.asarray(res.results[b]["y"], dtype=np.float32) for b in range(B)], axis=0)
    return out
```

```python
import math
from contextlib import ExitStack

import numpy as np
import concourse.bass as bass
import concourse.mybir as mybir
from concourse.bass_utils import run_bass_kernel_spmd

F32 = mybir.dt.float32
BF16 = mybir.dt.bfloat16
I32 = mybir.dt.int32
ALU = mybir.AluOpType
AF = mybir.ActivationFunctionType
AX = mybir.AxisListType

D = 1024
DIN = 5648
DFF = 4096
MIXW = 1536
EPS = 1e-6
EPOCH_LEN = 12000
NEG = -30000.0


class Buf:
    def __init__(self, name):
        self.name = name
        self.w = None
        self.r = []


class Prod:
    def __init__(self, K, name, handle):
        self.K = K
        self.name = name
        self.h = handle
        self.sems = []
        self.epoch = -1
        self.cnt = 0
        self.pending = False
        self.new_epoch()

    def new_epoch(self):
        self.sems.append(self.K.new_sem(f"{self.name}_e{len(self.sems)}"))
        self.epoch += 1
        self.cnt = 0


class T:
    def __init__(self, t, name):
        self.t = t
        self.b = Buf(name)

    def __getitem__(self, idx):
        return self.t[idx]


def _bufs(lst):
    out = []
    for x in lst:
        if isinstance(x, (list, tuple)):
            out.extend(_bufs(x))
        elif isinstance(x, T):
            out.append(x.b)
        else:
            out.append(x)
    return out


class Kern:
    def __init__(self, nc, stack):
        self.nc = nc
        self.stack = stack
        self.nsem = 0
        self.prods = {}
        for nm, h in (("pe", nc.tensor), ("act", nc.scalar), ("dve", nc.vector),
                      ("pool", nc.gpsimd), ("sp", nc.sync)):
            self.prods[nm] = Prod(self, nm, h)
        self.engines = ["pe", "act", "dve", "pool", "sp"]
        self.waited = {}
        self.ninst = {k: 0 for k in self.engines}
        self.banks = []
        self.bank_i = 0

    def new_sem(self, name):
        self.nsem += 1
        return self.stack.enter_context(self.nc.semaphore(name))

    def dma_slot(self, name):
        p = Prod(self, "dma_" + name, None)
        self.prods[p.name] = p
        return p

    def sb(self, stack, name, shape, dtype):
        self.nsb = getattr(self, "nsb", 0) + 1
        name = f"s{self.nsb}_{name}"
        return T(stack.enter_context(self.nc.sbuf_tensor(name, list(shape), dtype)), name)

    def make_banks(self):
        for i in range(8):
            t = self.stack.enter_context(self.nc.psum_tensor(f"bank{i}", [128, 512], F32))
            self.banks.append(T(t, f"bank{i}"))

    def bank(self):
        b = self.banks[self.bank_i % 8]
        self.bank_i += 1
        return b

    def _need(self, eng, reads, writes):
        need = {}

        def add(tok):
            if tok is None:
                return
            p, ep, c = tok
            if p is eng and eng.name == "pe":
                return
            key = (p.name, ep)
            if need.get(key, (None, 0))[1] < c:
                need[key] = (p, c)

        for b in reads:
            add(b.w)
        for b in writes:
            add(b.w)
            for t in b.r:
                add(t)
        for (pname, ep), (p, c) in need.items():
            wk = (eng.name, pname, ep)
            if self.waited.get(wk, 0) >= c:
                continue
            if ep == p.epoch:
                assert c <= p.cnt, f"wait on un-issued inc: {eng.name} waits {pname} {c}>{p.cnt}"
            eng.h.wait_ge(p.sems[ep], c)
            self.waited[wk] = c

    def _mark(self, tok, reads, writes):
        for b in reads:
            b.r.append(tok)
            if len(b.r) > 48:
                best = {}
                for (p, ep, c) in b.r:
                    k = (p.name, ep)
                    if k not in best or best[k][2] < c:
                        best[k] = (p, ep, c)
                b.r = list(best.values())
        for b in writes:
            b.w = tok
            b.r = []

    def op(self, engname, fn, R=(), W=(), inc=True):
        reads, writes = _bufs(R), _bufs(W)
        eng = self.prods[engname]
        self._need(eng, reads, writes)
        inst = fn(eng.h)
        self.ninst[engname] += 1
        if inc:
            if eng.cnt + 1 > EPOCH_LEN:
                assert not eng.pending
                eng.new_epoch()
            eng.cnt += 1
            inst.then_inc(eng.sems[eng.epoch], 1)
            eng.pending = False
            tok = (eng, eng.epoch, eng.cnt)
        else:
            if eng.cnt + 1 > EPOCH_LEN and not eng.pending:
                eng.new_epoch()
            eng.pending = True
            tok = (eng, eng.epoch, eng.cnt + 1)
        self._mark(tok, reads, writes)
        return inst

    def dma(self, qname, slot, out, in_, R=(), W=(), **kw):
        reads, writes = _bufs(R), _bufs(W)
        q = self.prods[qname]
        self._need(q, reads, writes)
        inst = q.h.dma_start(out=out, in_=in_, **kw)
        self.ninst[qname] += 1
        if slot.cnt + 16 > EPOCH_LEN:
            slot.new_epoch()
        slot.cnt += 16
        inst.then_inc(slot.sems[slot.epoch], 16)
        tok = (slot, slot.epoch, slot.cnt)
        self._mark(tok, reads, writes)
        return inst

    def barrier(self):
        for en in self.engines:
            e = self.prods[en]
            for p in self.prods.values():
                assert not p.pending
                if p.cnt == 0:
                    continue
                wk = (e.name, p.name, p.epoch)
                if self.waited.get(wk, 0) >= p.cnt:
                    continue
                e.h.wait_ge(p.sems[p.epoch], p.cnt)
                self.waited[wk] = p.cnt

    def tt(self, eng, out, in0, in1, op, R, W):
        return self.op(eng, lambda h: h.tensor_tensor(out=out, in0=in0, in1=in1, op=op), R, W)

    def ts(self, eng, out, in0, s1, s2, op0, op1, R, W):
        if s2 is None:
            return self.op(eng, lambda h: h.tensor_scalar(out=out, in0=in0, scalar1=s1, scalar2=None, op0=op0), R, W)
        return self.op(eng, lambda h: h.tensor_scalar(out=out, in0=in0, scalar1=s1, scalar2=s2, op0=op0, op1=op1), R, W)

    def stt(self, eng, out, in0, scalar, in1, op0, op1, R, W):
        return self.op(eng, lambda h: h.scalar_tensor_tensor(out=out, in0=in0, scalar=scalar, in1=in1,
                                                             op0=op0, op1=op1), R, W)

    def act(self, out, in_, func, R, W, **kw):
        return self.op("act", lambda h: h.activation(out=out, in_=in_, func=func, **kw), R, W)

    def cp(self, eng, out, in_, R, W):
        if eng == "act":
            return self.op("act", lambda h: h.copy(out=out, in_=in_), R, W)
        return self.op(eng, lambda h: h.tensor_copy(out=out, in_=in_), R, W)

    def mm(self, out, lhsT, rhs, start, stop, R, W, inc):
        return self.op("pe", lambda h: h.matmul(out, lhsT=lhsT, rhs=rhs, start=start, stop=stop), R, W, inc=inc)

    def tr(self, out, in_, ident, R, W, inc):
        return self.op("pe", lambda h: h.transpose(out=out, in_=in_, identity=ident), R, W, inc=inc)

    def rsum(self, out, in_, R, W):
        return self.op("dve", lambda h: h.tensor_reduce(out=out, in_=in_, axis=AX.X, op=ALU.add), R, W)

    def recip(self, out, in_, R, W):
        return self.op("dve", lambda h: h.reciprocal(out=out, in_=in_), R, W)


C_OFF = {}


def _const_table():
    cols = []

    def add(name, arr):
        arr = np.asarray(arr, np.float32)
        full = np.zeros((128, arr.shape[1]), np.float32)
        full[:arr.shape[0]] = arr
        C_OFF[name] = (sum(c.shape[1] for c in cols), arr.shape[1])
        cols.append(full)

    i = np.arange(128)
    m, l = i[:, None], i[None, :]
    same = (m // 64) == (l // 64)
    add("ident", np.eye(128))
    add("tri", (m <= l))
    add("tribd", (m <= l) & same)
    add("ones", np.ones((128, 128)))
    add("blockones", same)
    add("bs0", np.broadcast_to(m < 64, (128, 128)))
    add("bs1", np.broadcast_to(m >= 64, (128, 128)))
    add("negssd", np.where(l >= m, 0.0, NEG))
    add("negI", np.where((l >= m) & same, 0.0, NEG))
    add("posS", np.where((m > l) & same, 0.0, -NEG))
    sel8 = np.zeros((8, 8 * 128))
    for h in range(8):
        sel8[h, h * 128:(h + 1) * 128] = 1.0
    add("sel8", sel8)
    gam = 1.0 - np.exp2(-5.0 - np.arange(4))
    lg = np.log(gam)
    maskP = np.zeros((128, 512))
    for h in range(4):
        maskP[:, h * 128:(h + 1) * 128] = np.where(l >= m, np.exp(-lg[h] * (m + 1.0)), 0.0)
    add("maskP", maskP)
    add("xiq", np.exp(lg[None, :] * (i[:, None] + 1.0)) * (128.0 ** -0.5))
    add("zeta", np.exp(lg[None, :] * (127.0 - i[:, None])))
    invf = 10000.0 ** (-np.arange(64, dtype=np.float32) / 64.0)
    add("invf", np.broadcast_to(invf[None, :].astype(np.float32), (128, 64)))
    return np.concatenate(cols, axis=1), [float(g ** 128) for g in gam]


CONST_TABLE, RET_G128 = _const_table()
NCONST = CONST_TABLE.shape[1]

PF = {"mixnw": (0, 8), "mlpnw": (8, 8), "ynw": (16, 12), "scw": (28, 32), "scb": (60, 8), "gcw": (68, 48)}
NPF = 116
PT = {"bias16": (0, 16), "alog": (16, 12), "dskip": (28, 8)}
NPT = 36


def _pack_params(inp, depth):
    pf = np.zeros((depth, 128, NPF), np.float32)
    pt = np.zeros((depth, 128, NPT), np.float32)
    for l in range(depth):
        pf[l, :, 0:8] = inp["mix_norm_w"][l].reshape(8, 128).T
        pf[l, :, 8:16] = inp["mlp_norm_w"][l].reshape(8, 128).T
        pf[l, :, 16:20] = inp["ret_norm_w"][l].reshape(4, 128).T
        pf[l, :, 20:24] = inp["ssd_norm_w"][l].reshape(4, 128).T
        pf[l, :, 24:28] = np.repeat(inp["gdn_norm_w"][l].reshape(128, 1), 4, axis=1)
        pf[l, :, 28:60] = inp["ssd_conv_w"][l].reshape(4, 8, 128).transpose(2, 1, 0).reshape(128, 32)
        pf[l, :, 60:68] = inp["ssd_conv_b"][l].reshape(8, 128).T
        pf[l, :, 68:116] = inp["gdn_conv_w"][l].reshape(4, 12, 128).transpose(2, 1, 0).reshape(128, 48)
        pt[l, :, 0:8] = inp["ssd_dt_bias"][l][None, :]
        pt[l, :, 12:16] = inp["gdn_dt_bias"][l][None, :]
        pt[l, :, 16:24] = inp["ssd_a_log"][l][None, :]
        pt[l, :, 24:28] = inp["gdn_a_log"][l][None, :]
        pt[l, :, 28:36] = inp["ssd_d"][l][None, :]
    return pf, pt


GSTAGE = 99
CONV_ENG = "pool"
PHASES = {"init", "ret", "ssd", "gdn", "out", "mlp", "final"}


def _phase(name):
    if name in PHASES:
        with ExitStack() as s:
            yield s


def build_program(NT, DEPTH, debug=False):
    S = NT * 128
    nc = bass.Bass("TRN2", target_bir_lowering=False)
    x_in = nc.dram_tensor("x", [S, D], F32, kind="ExternalInput").ap()
    pos_in = nc.dram_tensor("pos", [128, NT], I32, kind="ExternalInput").ap()
    w_in = nc.dram_tensor("w_in", [DEPTH, D, DIN], F32, kind="ExternalInput").ap()
    w_out = nc.dram_tensor("w_out", [DEPTH, MIXW, D], F32, kind="ExternalInput").ap()
    w_up = nc.dram_tensor("w_up", [DEPTH, D, DFF], F32, kind="ExternalInput").ap()
    w_down = nc.dram_tensor("w_down", [DEPTH, DFF, D], F32, kind="ExternalInput").ap()
    pf_in = nc.dram_tensor("pf", [DEPTH, 128, NPF], F32, kind="ExternalInput").ap()
    pt_in = nc.dram_tensor("pt", [DEPTH, 128, NPT], F32, kind="ExternalInput").ap()
    fnw_in = nc.dram_tensor("fnw", [128, D], F32, kind="ExternalInput").ap()
    const_in = nc.dram_tensor("consts", [128, NCONST], F32, kind="ExternalInput").ap()
    y_out = nc.dram_tensor("y", [S, D], F32, kind="ExternalOutput").ap()
    dk = "ExternalOutput" if debug else "Internal"
    xb = nc.dram_tensor("xb", [S, D], F32, kind=dk).ap()
    yc = nc.dram_tensor("yc", [S, MIXW], F32, kind=dk).ap()
    csd = nc.dram_tensor("csd", [128, NT, 128], F32, kind="Internal").ap()

    with ExitStack() as st:
        st.enter_context(nc.allow_low_precision("bf16 matmul operands, fp32 accumulation"))
        K = Kern(nc, st)
        K.make_banks()
        xb_b = [Buf(f"xb{t}") for t in range(NT)]
        yc_b = [[Buf(f"yc{t}_{j}") for j in range(3)] for t in range(NT)]
        y_b = [Buf(f"y{t}") for t in range(NT)]
        csd_b = Buf("csd")
        s_c = K.dma_slot("c")
        s_pf = K.dma_slot("pf")
        s_pt = K.dma_slot("pt")
        s_cs = K.dma_slot("cs")
        s_w = K.dma_slot("w")
        s_w2 = K.dma_slot("w2")
        s_x = [K.dma_slot(f"x{i}") for i in range(2)]
        s_a = [K.dma_slot(f"a{i}") for i in range(2)]
        s_o = [K.dma_slot(f"o{i}") for i in range(2)]

        CT = K.sb(st, "consts", [128, NCONST], F32)
        K.dma("sp", s_c, CT[:], const_in, W=[CT])

        def C(name, rows=128):
            o, n = C_OFF[name]
            return CT[0:rows, o:o + n]

        ident = C("ident")

        def xsrc(l, t):
            if l == 0:
                return x_in[t * 128:(t + 1) * 128, :], []
            return xb[t * 128:(t + 1) * 128, :], [xb_b[t]]

        def norm_hT(ph, xt, nwcols, hf, hT, sm):
            K.act(hf[:], xt[:], AF.Square, [xt], [hf, sm], accum_out=sm[:, 0:1])
            K.act(sm[:, 1:2], sm[:, 0:1], AF.Ln, [sm], [sm], scale=1.0 / D, bias=EPS)
            K.act(sm[:, 2:3], sm[:, 1:2], AF.Exp, [sm], [sm], scale=-0.5)
            K.act(hf[:], xt[:], AF.Copy, [xt, sm], [hf], scale=sm[:, 2:3])
            for b in range(2):
                bk = K.bank()
                for j in range(4):
                    k = b * 4 + j
                    K.tr(bk[:, j * 128:(j + 1) * 128], hf[:, k * 128:(k + 1) * 128], ident, [hf, CT], [bk], inc=(j == 3))
                K.tt("dve", hT[:, b * 4:b * 4 + 4, :], bk[:, :].rearrange("p (c e) -> p c e", c=4),
                     nwcols[:, b * 4:b * 4 + 4].unsqueeze(2).to_broadcast([128, 4, 128]), ALU.mult, [bk, ph["pfm"]], [hT])

        def proj_tm(hT, WB, c0, n, out_ap, outT, eng):
            bk = K.bank()
            for k in range(8):
                K.mm(bk[:, 0:n], hT[:, k, :], WB[:, k, c0:c0 + n], k == 0, k == 7, [hT, WB], [bk], inc=(k == 7))
            K.cp(eng, out_ap, bk[:, 0:n], [bk], [outT])

        def proj_fm(hT, WB, c0, nchunk, cx, j0):
            bk = K.bank()
            for j in range(nchunk):
                for k in range(8):
                    K.mm(bk[:, j * 128:(j + 1) * 128], WB[:, k, c0 + j * 128:c0 + (j + 1) * 128], hT[:, k, :],
                         k == 0, k == 7, [hT, WB], [bk], inc=(j == nchunk - 1 and k == 7))
            K.cp("act", cx[:, j0:j0 + nchunk, 3:131],
                 bk[:, 0:nchunk * 128].rearrange("p (c e) -> p c e", c=nchunk), [bk], [cx])

        def conv_silu(cx, cxn, cw, cb, nch, acc, tmps):
            for c0 in range(0, nch, 4):
                a = acc[:, c0:c0 + 4, :]
                for j in range(1, 4):
                    wj = cw[:, c0:c0 + 4, j:j + 1].to_broadcast([128, 4, 128])
                    K.tt(CONV_ENG, tmps[j - 1][:, 0:4, :], cx[:, c0:c0 + 4, j:j + 128], wj, ALU.mult, [cx, ph_cur["pfm"]], [tmps[j - 1]])
                w0 = cw[:, c0:c0 + 4, 0:1].to_broadcast([128, 4, 128])
                K.tt("dve", a, cx[:, c0:c0 + 4, 0:128], w0, ALU.mult, [cx, ph_cur["pfm"]], [acc])
                for j in range(1, 4):
                    K.tt("dve", a, a, tmps[j - 1][:, 0:4, :], ALU.add, [acc, tmps[j - 1]], [acc])
                if cb is not None:
                    K.tt("dve", a, a, cb[:, c0:c0 + 4].unsqueeze(2).to_broadcast([128, 4, 128]), ALU.add,
                         [acc, ph_cur["pfm"]], [acc])
                t0 = tmps[0]
                K.act(t0[:, 0:4, :], a, AF.Exp, [acc], [t0], scale=-1.0)
                K.act(t0[:, 0:4, :], t0[:, 0:4, :], AF.Ln, [t0], [t0], bias=1.0)
                K.act(t0[:, 0:4, :], t0[:, 0:4, :], AF.Exp, [t0], [t0], scale=-1.0)
                K.tt("dve", a, a, t0[:, 0:4, :], ALU.mult, [acc, t0], [acc])
                yield
            K.cp("act", cxn[:, :, 0:3], cx[:, :, 128:131], [cx], [cxn])

        def silu_to(out_ap, outT, in_ap, inT, tmp_ap, tmpT):
            K.act(tmp_ap, in_ap, AF.Exp, [inT], [tmpT], scale=-1.0)
            K.act(tmp_ap, tmp_ap, AF.Ln, [tmpT], [tmpT], bias=1.0)
            K.act(tmp_ap, tmp_ap, AF.Exp, [tmpT], [tmpT], scale=-1.0)
            K.tt("dve", out_ap, in_ap, tmp_ap, ALU.mult, [inT, tmpT], [outT])

        def rstd_from_ss(out_ap, ss_ap, smT, n, eps):
            K.act(out_ap, ss_ap, AF.Ln, [smT], [smT], scale=1.0 / n, bias=eps)
            K.act(out_ap, out_ap, AF.Exp, [smT], [smT], scale=-0.5)

        def softplus16(sp, z, sm2, n):
            K.act(sm2[:, 0:n], z, AF.Abs, [sm2, ph_cur["small"]], [sm2])
            K.act(sm2[:, 0:n], sm2[:, 0:n], AF.Exp, [sm2], [sm2], scale=-1.0)
            K.act(sm2[:, 0:n], sm2[:, 0:n], AF.Ln, [sm2], [sm2], bias=1.0)
            K.ts("dve", sp, z, 0.0, None, ALU.max, None, [ph_cur["small"]], [ph_cur["small"]])
            K.tt("dve", sp, sp, sm2[:, 0:n], ALU.add, [ph_cur["small"], sm2], [ph_cur["small"]])

        def load_w(WB_ap, WBT, src_ap, nk, slot=None):
            for k in range(nk):
                K.dma("pool", slot or s_w, WB_ap[:, k, :], src_ap[k], W=[WBT])

        ph_cur = {}

        def pipeline(A, B, ratio):
            for _ in A(0):
                pass
            for t in range(NT):
                gb = B(t)
                ga = A(t + 1) if t + 1 < NT else iter(())
                doneA = False
                i = 0
                for _ in gb:
                    i += 1
                    if not doneA and i % ratio == 0:
                        try:
                            next(ga)
                        except StopIteration:
                            doneA = True
                if not doneA:
                    for _ in ga:
                        pass

        def common_alloc(ph, l, wcols, wname):
            ph_cur.clear()
            ph["pfm"] = K.sb(ph["st"], "pfm", [128, NPF], F32)
            ph["ptm"] = K.sb(ph["st"], "ptm", [128, NPT], F32)
            K.dma("sp", s_pf, ph["pfm"][:], pf_in[l], W=[ph["pfm"]])
            K.dma("sp", s_pt, ph["ptm"][:], pt_in[l], W=[ph["ptm"]])
            ph["xt"] = [K.sb(ph["st"], f"xt{i}", [128, D], F32) for i in range(2)]
            ph["hf"] = [K.sb(ph["st"], f"hf{i}", [128, D], F32) for i in range(2)]
            ph["hT"] = [K.sb(ph["st"], f"hT{i}", [128, 8, 128], BF16) for i in range(2)]
            ph["smA"] = [K.sb(ph["st"], f"smA{i}", [128, 8], F32) for i in range(2)]
            ph["raw"] = [K.sb(ph["st"], f"raw{i}", [128, 16], F32) for i in range(2)]
            ph["sm"] = K.sb(ph["st"], "sm", [128, 8], F32)
            ph["small"] = K.sb(ph["st"], "small", [128, 128], F32)
            ph["sm2"] = K.sb(ph["st"], "sm2", [128, 16], F32)
            ph_cur.update(ph)

        def pfm(ph, name):
            o, n = PF[name]
            return ph["pfm"][:, o:o + n]

        def ptm(ph, name):
            o, n = PT[name]
            return ph["ptm"][:, o:o + n]

        def load_x(ph, l, t):
            src, rb = xsrc(l, t)
            xt = ph["xt"][t % 2]
            K.dma("sp", s_x[t % 2], xt[:], src, R=rb, W=[xt])
            return xt

        for pst in _phase("init"):
            posi = K.sb(pst, "posi", [128, NT], I32)
            posf = K.sb(pst, "posf", [128, NT], F32)
            ang = K.sb(pst, "ang", [128, NT, 64], F32)
            kk = K.sb(pst, "kk", [128, NT, 64], F32)
            cs = K.sb(pst, "cs", [128, NT, 128], F32)
            K.dma("sp", s_pf, posi[:], pos_in, W=[posi])
            K.cp("dve", posf[:], posi[:], [posi], [posf])
            K.tt("dve", ang[:], posf[:, :].unsqueeze(2).to_broadcast([128, NT, 64]),
                 C("invf").unsqueeze(1).to_broadcast([128, NT, 64]), ALU.mult, [posf, CT], [ang])
            MAGIC = 12582912.0
            TWO_PI = 2.0 * math.pi
            for which, shift in ((1, 0.0), (0, math.pi / 2.0)):
                if shift != 0.0:
                    K.ts("dve", ang[:], ang[:], shift, None, ALU.add, None, [ang], [ang])
                K.ts("dve", kk[:], ang[:], 1.0 / TWO_PI, MAGIC, ALU.mult, ALU.add, [ang], [kk])
                K.ts("dve", kk[:], kk[:], MAGIC, None, ALU.subtract, None, [kk], [kk])
                K.stt("dve", kk[:], kk[:], -TWO_PI, ang[:], ALU.mult, ALU.add, [kk, ang], [kk])
                K.ts("dve", kk[:], kk[:], math.pi, -math.pi, ALU.min, ALU.max, [kk], [kk])
                K.act(cs[:, :, which * 64:(which + 1) * 64], kk[:], AF.Sin, [kk], [cs])
            K.dma("sp", s_cs, csd, cs[:], R=[cs], W=[csd_b])
            K.barrier()

        for l in range(DEPTH):
            for pst in _phase("ret"):
                ph = {"st": pst}
                common_alloc(ph, l, None, None)
                WB = K.sb(pst, "WBr", [128, 8, 2048], BF16)
                load_w(WB, WB, [w_in[l, k * 128:(k + 1) * 128, 0:2048] for k in range(8)], 8)
                cs = K.sb(pst, "cs", [128, NT, 128], F32)
                K.dma("sp", s_cs, cs[:], csd, R=[csd_b], W=[cs])
                tms = [K.sb(pst, f"tm{i}", [128, 2048], F32) for i in range(2)]
                qr = K.sb(pst, "qr", [128, 512], F32)
                kr = K.sb(pst, "kr", [128, 512], F32)
                tA = K.sb(pst, "tA", [128, 512], F32)
                qxT = K.sb(pst, "qxT", [128, 4, 128], BF16)
                krT = K.sb(pst, "krT", [128, 4, 128], BF16)
                krb = K.sb(pst, "krb", [128, 512], BF16)
                vb = K.sb(pst, "vb", [128, 512], BF16)
                vz = K.sb(pst, "vz", [128, 512], BF16)
                smT = K.sb(pst, "smT", [128, 4, 128], BF16)
                Sr = K.sb(pst, "Sr", [128, 512], F32)
                Srb = K.sb(pst, "Srb", [128, 512], BF16)
                yo = K.sb(pst, "yo", [128, 512], F32)
                sm = ph["sm"]
                K.op("dve", lambda h: h.memset(Sr[:], 0.0), [], [Sr])
                K.op("dve", lambda h: h.memset(Srb[:], 0.0), [], [Srb])
                def A(t):
                    par = t % 2
                    tm = tms[par]
                    xt = load_x(ph, l, t)
                    norm_hT(ph, xt, pfm(ph, "mixnw"), ph["hf"][par], ph["hT"][par], ph["smA"][par])
                    yield
                    for g in range(4):
                        proj_tm(ph["hT"][par], WB, g * 512, 512, tm[:, g * 512:(g + 1) * 512], tm, "act" if g % 2 else "dve")
                        yield
                    yield

                def B(t):
                    par = t % 2
                    tm = tms[par]
                    cosb = cs[:, t, 0:64].unsqueeze(1).to_broadcast([128, 4, 64])
                    sinb = cs[:, t, 64:128].unsqueeze(1).to_broadcast([128, 4, 64])
                    for (src0, dst) in ((0, qr), (512, kr)):
                        v4 = tm[:, src0:src0 + 512].rearrange("p (h t e) -> p h t e", h=4, t=2)
                        d4 = dst[:, :].rearrange("p (h t e) -> p h t e", h=4, t=2)
                        a4 = tA[:, 0:256].rearrange("p (h e) -> p h e", h=4)
                        t1, t2 = v4[:, :, 0, :], v4[:, :, 1, :]
                        K.tt("dve", d4[:, :, 0, :], t1, cosb, ALU.mult, [tm, cs], [dst])
                        K.tt("dve", a4, t2, sinb, ALU.mult, [tm, cs], [tA])
                        K.tt("dve", d4[:, :, 0, :], d4[:, :, 0, :], a4, ALU.subtract, [dst, tA], [dst])
                        K.tt("dve", d4[:, :, 1, :], t2, cosb, ALU.mult, [tm, cs], [dst])
                        K.tt("dve", a4, t1, sinb, ALU.mult, [tm, cs], [tA])
                        K.tt("dve", d4[:, :, 1, :], d4[:, :, 1, :], a4, ALU.add, [dst, tA], [dst])
                    K.tt("dve", qr[:, :].rearrange("p (h e) -> p h e", h=4), qr[:, :].rearrange("p (h e) -> p h e", h=4),
                         C("xiq").unsqueeze(2).to_broadcast([128, 4, 128]), ALU.mult, [qr, CT], [qr])
                    for (src, dstT) in ((qr, qxT), (kr, krT)):
                        yield
                        bk = K.bank()
                        for h in range(4):
                            K.tr(bk[:, h * 128:(h + 1) * 128], src[:, h * 128:(h + 1) * 128], ident, [src, CT], [bk], inc=(h == 3))
                        K.cp("act", dstT[:, :, :], bk[:, :].rearrange("p (h e) -> p h e", h=4), [bk], [dstT])
                    K.cp("act", krb[:], kr[:], [kr], [krb])
                    K.cp("act", vb[:], tm[:, 1024:1536], [tm], [vb])
                    K.tt("dve", vz[:, :].rearrange("p (h e) -> p h e", h=4), tm[:, 1024:1536].rearrange("p (h e) -> p h e", h=4),
                         C("zeta").unsqueeze(2).to_broadcast([128, 4, 128]), ALU.mult, [tm, CT], [vz])
                    yield
                    bk = K.bank()
                    for h in range(4):
                        K.mm(bk[:, h * 128:(h + 1) * 128], krT[:, h, :], qxT[:, h, :], True, True, [krT, qxT], [bk], inc=(h == 3))
                    K.tt("dve", smT[:, :, :], bk[:, :].rearrange("p (h e) -> p h e", h=4),
                         C("maskP").rearrange("p (h e) -> p h e", h=4), ALU.mult, [bk, CT], [smT])
                    yield
                    by = K.bank()
                    for h in range(4):
                        K.mm(by[:, h * 128:(h + 1) * 128], smT[:, h, :], vb[:, h * 128:(h + 1) * 128], True, False, [smT, vb], [by], inc=False)
                        K.mm(by[:, h * 128:(h + 1) * 128], qxT[:, h, :], Srb[:, h * 128:(h + 1) * 128], False, True, [qxT, Srb], [by], inc=(h == 3))
                    yield
                    bs = K.bank()
                    for h in range(4):
                        K.mm(bs[:, h * 128:(h + 1) * 128], krb[:, h * 128:(h + 1) * 128], vz[:, h * 128:(h + 1) * 128], True, True, [krb, vz], [bs], inc=(h == 3))
                    for h in range(4):
                        K.stt("dve", Sr[:, h * 128:(h + 1) * 128], Sr[:, h * 128:(h + 1) * 128], RET_G128[h],
                              bs[:, h * 128:(h + 1) * 128], ALU.mult, ALU.add, [Sr, bs], [Sr])
                    K.cp("act", Srb[:], Sr[:], [Sr], [Srb])
                    K.act(tA[:], by[:, :], AF.Square, [by], [tA])
                    K.rsum(sm[:, 4:8], tA[:, :].rearrange("p (h e) -> p h e", h=4), [tA], [sm])
                    rstd_from_ss(sm[:, 4:8], sm[:, 4:8], sm, 128.0, EPS)
                    K.tt("dve", yo[:, :].rearrange("p (h e) -> p h e", h=4), by[:, :].rearrange("p (h e) -> p h e", h=4),
                         sm[:, 4:8].unsqueeze(2).to_broadcast([128, 4, 128]), ALU.mult, [by, sm], [yo])
                    silu_to(tA[:], tA, tm[:, 1536:2048], tm, qr[:], qr)
                    K.tt("dve", yo[:], yo[:], tA[:], ALU.mult, [yo, tA], [yo])
                    K.dma("sp", s_o[t % 2], yc[t * 128:(t + 1) * 128, 0:512], yo[:], R=[yo], W=[yc_b[t][0]])
                    yield

                pipeline(A, B, 4)
                K.barrier()

            for pst in _phase("ssd"):
                ph = {"st": pst}
                common_alloc(ph, l, None, None)
                NW = 1544
                WB = K.sb(pst, "WBs", [128, 8, NW], BF16)
                load_w(WB, WB, [w_in[l, k * 128:(k + 1) * 128, 2048:3592] for k in range(8)], 8)
                zts = [K.sb(pst, f"zt{i}", [128, 512], F32) for i in range(2)]
                cxs = [K.sb(pst, f"cxs{i}", [128, 8, 131], F32) for i in range(2)]
                xc = K.sb(pst, "xc", [128, 8, 128], F32)
                tmps = [K.sb(pst, f"ctmp{i}", [128, 4, 128], F32) for i in range(3)]
                xs_tm = K.sb(pst, "xs_tm", [128, 512], F32)
                bm_tm = K.sb(pst, "bm_tm", [128, 256], BF16)
                bcT = K.sb(pst, "bcT", [128, 4, 128], BF16)
                decT = K.sb(pst, "decT", [128, 8, 128], F32)
                GT = K.sb(pst, "GT", [128, 8, 128], BF16)
                xdt = K.sb(pst, "xdt", [128, 512], BF16)
                xdte = K.sb(pst, "xdte", [128, 512], BF16)
                t1 = K.sb(pst, "t1", [128, 512], F32)
                t2 = K.sb(pst, "t2", [128, 512], F32)
                Ss = K.sb(pst, "Ss", [128, 512], F32)
                Ssb = K.sb(pst, "Ssb", [128, 512], BF16)
                acsT = K.sb(pst, "acsT", [8, 128], F32)
                sml = ph["small"]
                sm = ph["sm"]
                K.op("dve", lambda h: h.memset(Ss[:], 0.0), [], [Ss])
                K.op("dve", lambda h: h.memset(Ssb[:], 0.0), [], [Ssb])
                for cx_ in cxs:
                    K.op("dve", lambda h: h.memset(cx_[:], 0.0), [], [cx_])
                K.act(sml[:, 16:24], ptm(ph, "alog")[:, 0:8], AF.Exp, [ph["ptm"]], [sml])
                K.ts("dve", sml[:, 16:24], sml[:, 16:24], -1.0, None, ALU.mult, None, [sml], [sml])
                scw = pfm(ph, "scw").rearrange("p (c j) -> p c j", c=8)
                def A(t):
                    par = t % 2
                    zt = zts[par]
                    cx = cxs[par]
                    raw = ph["raw"][par]
                    xt = load_x(ph, l, t)
                    norm_hT(ph, xt, pfm(ph, "mixnw"), ph["hf"][par], ph["hT"][par], ph["smA"][par])
                    yield
                    proj_tm(ph["hT"][par], WB, 0, 512, zt[:], zt, "act")
                    yield
                    proj_tm(ph["hT"][par], WB, 1536, 8, raw[:, 0:8], raw, "dve")
                    yield
                    proj_fm(ph["hT"][par], WB, 512, 4, cx, 0)
                    yield
                    proj_fm(ph["hT"][par], WB, 1024, 4, cx, 4)
                    yield
                    yield

                def B(t):
                    par = t % 2
                    zt = zts[par]
                    cx = cxs[par]
                    cxn = cxs[1 - par]
                    raw = ph["raw"][par]
                    yield from conv_silu(cx, cxn, scw, pfm(ph, "scb"), 8, xc, tmps)
                    K.tt("dve", sml[:, 0:8], raw[:, 0:8], ptm(ph, "bias16")[:, 0:8], ALU.add, [raw, ph["ptm"]], [sml])
                    softplus16(sml[:, 88:96], sml[:, 0:8], ph["sm2"], 8)
                    K.tt("dve", sml[:, 24:32], sml[:, 88:96], sml[:, 16:24], ALU.mult, [sml], [sml])
                    yield
                    bk = K.bank()
                    K.mm(bk[:, 0:8], C("tri"), sml[:, 24:32], True, True, [CT, sml], [bk], inc=False)
                    K.mm(bk[:, 8:16], C("ones"), sml[:, 24:32], True, True, [CT, sml], [bk], inc=False)
                    K.mm(bk[0:8, 128:256], sml[:, 24:32], C("tri"), True, True, [CT, sml], [bk], inc=True)
                    K.cp("dve", sml[:, 32:40], bk[:, 0:8], [bk], [sml])
                    K.ts("dve", sml[:, 40:48], bk[:, 0:8], -1.0, None, ALU.mult, None, [bk], [sml])
                    K.cp("dve", sml[:, 56:64], bk[:, 8:16], [bk], [sml])
                    K.cp("act", acsT[:, :], bk[0:8, 128:256], [bk], [acsT])
                    K.act(sml[:, 48:56], sml[:, 32:40], AF.Exp, [sml], [sml])
                    K.tt("dve", sml[:, 64:72], sml[:, 56:64], sml[:, 32:40], ALU.subtract, [sml], [sml])
                    K.act(sml[:, 64:72], sml[:, 64:72], AF.Exp, [sml], [sml])
                    K.act(sml[:, 72:80], sml[:, 56:64], AF.Exp, [sml], [sml])
                    K.tt("dve", sml[:, 80:88], sml[:, 88:96], sml[:, 64:72], ALU.mult, [sml], [sml])
                    for hb in range(2):
                        yield
                        bk = K.bank()
                        for j in range(4):
                            h = hb * 4 + j
                            K.mm(bk[:, j * 128:(j + 1) * 128], ident, C("negssd"), True, False, [CT], [bk], inc=False)
                            K.mm(bk[:, j * 128:(j + 1) * 128], CT[0:8, C_OFF["sel8"][0] + h * 128:C_OFF["sel8"][0] + (h + 1) * 128],
                                 acsT[:, :], False, True, [CT, acsT], [bk], inc=(j == 3))
                        for j in range(4):
                            h = hb * 4 + j
                            K.act(decT[:, h, :], bk[:, j * 128:(j + 1) * 128], AF.Exp, [bk, sml], [decT], bias=sml[:, 40 + h:41 + h])
                    yield
                    bk = K.bank()
                    for c in range(4):
                        K.tr(bk[:, c * 128:(c + 1) * 128], xc[:, c, :], ident, [xc, CT], [bk], inc=(c == 3))
                    K.cp("act", xs_tm[:], bk[:, :], [bk], [xs_tm])
                    yield
                    bk = K.bank()
                    for g in range(2):
                        K.tr(bk[:, g * 128:(g + 1) * 128], xc[:, 4 + g, :], ident, [xc, CT], [bk], inc=(g == 1))
                    K.cp("act", bm_tm[:], bk[:, 0:256], [bk], [bm_tm])
                    K.cp("dve", bcT[:, :, :], xc[:, 4:8, :], [xc], [bcT])
                    yield
                    bk = K.bank()
                    for g in range(2):
                        K.mm(bk[:, g * 128:(g + 1) * 128], bcT[:, g, :], bcT[:, 2 + g, :], True, True, [bcT], [bk], inc=(g == 1))
                    for g in range(2):
                        K.tt("dve", GT[:, 4 * g:4 * g + 4, :], decT[:, 4 * g:4 * g + 4, :],
                             bk[:, g * 128:(g + 1) * 128].unsqueeze(1).to_broadcast([128, 4, 128]), ALU.mult, [decT, bk], [GT])
                    xs3 = xs_tm[:, :].rearrange("p (h e) -> p h e", h=8)
                    K.tt("dve", xdt[:, :].rearrange("p (h e) -> p h e", h=8), xs3,
                         sml[:, 88:96].unsqueeze(2).to_broadcast([128, 8, 64]), ALU.mult, [xs_tm, sml], [xdt])
                    K.tt("dve", xdte[:, :].rearrange("p (h e) -> p h e", h=8), xs3,
                         sml[:, 80:88].unsqueeze(2).to_broadcast([128, 8, 64]), ALU.mult, [xs_tm, sml], [xdte])
                    yield
                    by = K.bank()
                    for h in range(8):
                        K.mm(by[:, h * 64:(h + 1) * 64], GT[:, h, :], xdt[:, h * 64:(h + 1) * 64], True, True, [GT, xdt], [by], inc=(h == 7))
                    yield
                    bc = K.bank()
                    for g in range(2):
                        K.mm(bc[:, g * 256:(g + 1) * 256], bcT[:, 2 + g, :], Ssb[:, g * 256:(g + 1) * 256], True, True, [bcT, Ssb], [bc], inc=(g == 1))
                    yield
                    bn = K.bank()
                    for g in range(2):
                        K.mm(bn[:, g * 256:(g + 1) * 256], bm_tm[:, g * 128:(g + 1) * 128], xdte[:, g * 256:(g + 1) * 256], True, True, [bm_tm, xdte], [bn], inc=(g == 1))
                    K.tt("dve", t1[:, :].rearrange("p (h e) -> p h e", h=8), bc[:, :].rearrange("p (h e) -> p h e", h=8),
                         sml[:, 48:56].unsqueeze(2).to_broadcast([128, 8, 64]), ALU.mult, [bc, sml], [t1])
                    K.tt("dve", t1[:], t1[:], by[:, :], ALU.add, [t1, by], [t1])
                    K.tt("dve", t2[:, :].rearrange("p (h e) -> p h e", h=8), xs3,
                         ptm(ph, "dskip").unsqueeze(2).to_broadcast([128, 8, 64]), ALU.mult, [xs_tm, ph["ptm"]], [t2])
                    K.tt("dve", t1[:], t1[:], t2[:], ALU.add, [t1, t2], [t1])
                    K.tt("dve", Ss[:, :].rearrange("p (h e) -> p h e", h=8), Ss[:, :].rearrange("p (h e) -> p h e", h=8),
                         sml[:, 72:80].unsqueeze(2).to_broadcast([128, 8, 64]), ALU.mult, [Ss, sml], [Ss])
                    K.tt("dve", Ss[:], Ss[:], bn[:, :], ALU.add, [Ss, bn], [Ss])
                    K.cp("act", Ssb[:], Ss[:], [Ss], [Ssb])
                    silu_to(t2[:], t2, zt[:], zt, xs_tm[:], xs_tm)
                    K.tt("dve", t1[:], t1[:], t2[:], ALU.mult, [t1, t2], [t1])
                    K.act(t2[:], t1[:], AF.Square, [t1], [t2])
                    K.rsum(sm[:, 4:6], t2[:, :].rearrange("p (g e) -> p g e", g=2), [t2], [sm])
                    rstd_from_ss(sm[:, 4:6], sm[:, 4:6], sm, 256.0, EPS)
                    K.tt("dve", t1[:, :].rearrange("p (g e) -> p g e", g=2), t1[:, :].rearrange("p (g e) -> p g e", g=2),
                         sm[:, 4:6].unsqueeze(2).to_broadcast([128, 2, 256]), ALU.mult, [t1, sm], [t1])
                    K.dma("sp", s_o[t % 2], yc[t * 128:(t + 1) * 128, 512:1024], t1[:], R=[t1], W=[yc_b[t][1]])
                    yield

                pipeline(A, B, 4)
                K.barrier()

            for pst in _phase("gdn"):
                ph = {"st": pst}
                common_alloc(ph, l, None, None)
                NW = 2056
                WB = K.sb(pst, "WBg", [128, 8, NW], BF16)
                load_w(WB, WB, [w_in[l, k * 128:(k + 1) * 128, 3592:5648] for k in range(8)], 8)
                zts = [K.sb(pst, f"zt{i}", [128, 512], F32) for i in range(2)]
                cxs = [K.sb(pst, f"cxg{i}", [128, 12, 131], F32) for i in range(2)]
                gc = K.sb(pst, "gc", [128, 12, 128], F32)
                tmps = [K.sb(pst, f"ctmp{i}", [128, 4, 128], F32) for i in range(3)]
                qkv = K.sb(pst, "qkv", [128, 1536], F32)
                sq = K.sb(pst, "sq", [128, 1024], F32)
                kdm = [K.sb(pst, f"kdm{i}", [128, 512], BF16) for i in range(2)]
                vn = [K.sb(pst, f"vn{i}", [128, 512], BF16) for i in range(2)]
                vf = K.sb(pst, "vf", [128, 512], BF16)
                qnT = K.sb(pst, "qnT", [128, 4, 128], BF16)
                knT = K.sb(pst, "knT", [128, 4, 128], BF16)
                decL = K.sb(pst, "decL", [128, 4, 128], F32)
                decTg = K.sb(pst, "decTg", [128, 4, 128], F32)
                Aa = [K.sb(pst, f"A{i}", [128, 4, 128], F32) for i in range(2)]
                Bb = [K.sb(pst, f"B{i}", [128, 4, 128], F32) for i in range(2)]
                Pm = K.sb(pst, "Pm", [128, 4, 128], F32)
                TTb = K.sb(pst, "TTb", [128, 4, 128], BF16)
                vbt = K.sb(pst, "vbt", [128, 512], BF16)
                kbg = K.sb(pst, "kbg", [128, 512], BF16)
                uu = K.sb(pst, "uu", [128, 512], F32)
                wT = K.sb(pst, "wT", [128, 4, 128], BF16)
                attnT = K.sb(pst, "attnT", [128, 4, 128], BF16)
                otmp = K.sb(pst, "otmp", [128, 512], F32)
                oo = K.sb(pst, "oo", [128, 512], F32)
                Sg = K.sb(pst, "Sg", [128, 512], F32)
                Sgb = K.sb(pst, "Sgb", [128, 512], BF16)
                gcsT = K.sb(pst, "gcsT", [8, 128], F32)
                sml = ph["small"]
                sm = ph["sm"]
                K.op("dve", lambda h: h.memset(Sg[:], 0.0), [], [Sg])
                K.op("dve", lambda h: h.memset(Sgb[:], 0.0), [], [Sgb])
                for cx_ in cxs:
                    K.op("dve", lambda h: h.memset(cx_[:], 0.0), [], [cx_])
                K.op("dve", lambda h: h.memset(sml[:], 0.0), [], [sml])
                K.act(sml[:, 16:20], ptm(ph, "alog")[:, 8:12], AF.Exp, [ph["ptm"]], [sml])
                K.ts("dve", sml[:, 16:20], sml[:, 16:20], -1.0, None, ALU.mult, None, [sml], [sml])
                gcw = pfm(ph, "gcw").rearrange("p (c j) -> p c j", c=12)
                sel_o = C_OFF["sel8"][0]
                def A(t):
                    par = t % 2
                    zt = zts[par]
                    cx = cxs[par]
                    raw = ph["raw"][par]
                    xt = load_x(ph, l, t)
                    norm_hT(ph, xt, pfm(ph, "mixnw"), ph["hf"][par], ph["hT"][par], ph["smA"][par])
                    yield
                    proj_tm(ph["hT"][par], WB, 1536, 512, zt[:], zt, "act")
                    yield
                    proj_tm(ph["hT"][par], WB, 2048, 8, raw[:, 0:8], raw, "dve")
                    yield
                    for j0 in range(0, 12, 4):
                        proj_fm(ph["hT"][par], WB, j0 * 128, 4, cx, j0)
                        yield
                    yield

                def B(t):
                    par = t % 2
                    zt = zts[par]
                    cx = cxs[par]
                    cxn = cxs[1 - par]
                    raw = ph["raw"][par]
                    yield from conv_silu(cx, cxn, gcw, None, 12, gc, tmps)
                    if GSTAGE < 2:
                        return
                    for j0 in range(0, 12, 4):
                        yield
                        bk = K.bank()
                        for j in range(4):
                            K.tr(bk[:, j * 128:(j + 1) * 128], gc[:, j0 + j, :], ident, [gc, CT], [bk], inc=(j == 3))
                        K.cp("act" if j0 == 4 else "dve", qkv[:, j0 * 128:(j0 + 4) * 128], bk[:, :], [bk], [qkv])
                    K.act(sq[:], qkv[:, 0:1024], AF.Square, [qkv], [sq])
                    K.rsum(sml[:, 64:72], sq[:, :].rearrange("p (h e) -> p h e", h=8), [sq], [sml])
                    K.act(sml[:, 52:60], sml[:, 64:72], AF.Ln, [sml], [sml], bias=EPS)
                    K.act(sml[:, 52:60], sml[:, 52:60], AF.Exp, [sml], [sml], scale=-0.5)
                    K.ts("dve", sml[:, 52:56], sml[:, 52:56], 128.0 ** -0.5, None, ALU.mult, None, [sml], [sml])
                    K.tt("dve", qkv[:, 0:1024].rearrange("p (h e) -> p h e", h=8), qkv[:, 0:1024].rearrange("p (h e) -> p h e", h=8),
                         sml[:, 52:60].unsqueeze(2).to_broadcast([128, 8, 128]), ALU.mult, [qkv, sml], [qkv])
                    if GSTAGE < 3:
                        return
                    K.act(sml[:, 8:12], raw[:, 0:4], AF.Exp, [raw], [sml], scale=-1.0)
                    K.ts("dve", sml[:, 8:12], sml[:, 8:12], 1.0, None, ALU.add, None, [sml], [sml])
                    K.recip(sml[:, 8:12], sml[:, 8:12], [sml], [sml])
                    K.ts("dve", sml[:, 12:16], sml[:, 8:12], -1.0, None, ALU.mult, None, [sml], [sml])
                    K.tt("dve", sml[:, 4:8], raw[:, 4:8], ptm(ph, "bias16")[:, 12:16], ALU.add, [raw, ph["ptm"]], [sml])
                    softplus16(sml[:, 76:80], sml[:, 4:8], ph["sm2"], 4)
                    K.tt("dve", sml[:, 20:24], sml[:, 76:80], sml[:, 16:20], ALU.mult, [sml], [sml])
                    yield
                    bk = K.bank()
                    K.mm(bk[:, 0:4], C("tribd"), sml[:, 20:24], True, True, [CT, sml], [bk], inc=False)
                    K.mm(bk[:, 4:8], C("blockones"), sml[:, 20:24], True, True, [CT, sml], [bk], inc=False)
                    K.mm(bk[:, 8:12], C("bs0"), sml[:, 20:24], True, True, [CT, sml], [bk], inc=False)
                    K.mm(bk[:, 12:16], C("bs1"), sml[:, 20:24], True, True, [CT, sml], [bk], inc=False)
                    K.mm(bk[0:8, 128:256], sml[:, 20:28], C("tribd"), True, True, [CT, sml], [bk], inc=True)
                    K.cp("dve", sml[:, 24:28], bk[:, 0:4], [bk], [sml])
                    K.ts("dve", sml[:, 28:32], bk[:, 0:4], -1.0, None, ALU.mult, None, [bk], [sml])
                    K.cp("dve", sml[:, 32:36], bk[:, 4:8], [bk], [sml])
                    K.act(sml[:, 44:52], bk[:, 8:16], AF.Exp, [bk], [sml])
                    K.cp("act", gcsT[:, :], bk[0:8, 128:256], [bk], [gcsT])
                    K.act(sml[:, 36:40], sml[:, 24:28], AF.Exp, [sml], [sml])
                    K.tt("dve", sml[:, 40:44], sml[:, 32:36], sml[:, 24:28], ALU.subtract, [sml], [sml])
                    K.act(sml[:, 40:44], sml[:, 40:44], AF.Exp, [sml], [sml])
                    K.tt("dve", sml[:, 60:64], sml[:, 8:12], sml[:, 36:40], ALU.mult, [sml], [sml])
                    if GSTAGE < 4:
                        return
                    qn3 = qkv[:, 0:512].rearrange("p (h e) -> p h e", h=4)
                    kn3 = qkv[:, 512:1024].rearrange("p (h e) -> p h e", h=4)
                    v3 = qkv[:, 1024:1536].rearrange("p (h e) -> p h e", h=4)
                    for i in range(2):
                        K.ts("dve", sml[:, 96 + 4 * i:100 + 4 * i], sml[:, 40:44], C("bs%d" % i)[:, 0:1], None, ALU.mult, None, [sml, CT], [sml])
                        K.tt("dve", kdm[i][:, :].rearrange("p (h e) -> p h e", h=4), kn3,
                             sml[:, 96 + 4 * i:100 + 4 * i].unsqueeze(2).to_broadcast([128, 4, 128]), ALU.mult, [qkv, sml], [kdm[i]])
                    K.tt("dve", kbg[:, :].rearrange("p (h e) -> p h e", h=4), kn3,
                         sml[:, 60:64].unsqueeze(2).to_broadcast([128, 4, 128]), ALU.mult, [qkv, sml], [kbg])
                    K.tt("dve", vbt[:, :].rearrange("p (h e) -> p h e", h=4), v3,
                         sml[:, 8:12].unsqueeze(2).to_broadcast([128, 4, 128]), ALU.mult, [qkv, sml], [vbt])
                    for (c0, dstT) in ((0, qnT), (512, knT)):
                        yield
                        bk = K.bank()
                        for h in range(4):
                            K.tr(bk[:, h * 128:(h + 1) * 128], qkv[:, c0 + h * 128:c0 + (h + 1) * 128], ident, [qkv, CT], [bk], inc=(h == 3))
                        K.cp("act", dstT[:, :, :], bk[:, :].rearrange("p (h e) -> p h e", h=4), [bk], [dstT])
                    if GSTAGE < 5:
                        return
                    for (msk, dst, scale, bcol) in (("posS", decL, -1.0, 24), ("negI", decTg, 1.0, 28)):
                        yield
                        bk = K.bank()
                        for h in range(4):
                            K.mm(bk[:, h * 128:(h + 1) * 128], ident, C(msk), True, False, [CT], [bk], inc=False)
                            K.mm(bk[:, h * 128:(h + 1) * 128], CT[0:8, sel_o + h * 128:sel_o + (h + 1) * 128], gcsT[:, :],
                                 False, True, [CT, gcsT], [bk], inc=(h == 3))
                        for h in range(4):
                            K.act(dst[:, h, :], bk[:, h * 128:(h + 1) * 128], AF.Exp, [bk, sml], [dst],
                                  scale=scale, bias=sml[:, bcol + h:bcol + h + 1])
                    if GSTAGE < 6:
                        return
                    yield
                    bk = K.bank()
                    for h in range(4):
                        K.mm(bk[:, h * 128:(h + 1) * 128], knT[:, h, :], knT[:, h, :], True, True, [knT], [bk], inc=(h == 3))
                    if GSTAGE == 60:
                        return
                    for h in range(4):
                        K.stt("dve", Bb[0][:, h, :], bk[:, h * 128:(h + 1) * 128], sml[:, 12 + h:13 + h], decL[:, h, :],
                              ALU.mult, ALU.mult, [bk, sml, decL], [Bb[0]])
                    if GSTAGE == 61:
                        return
                    yield
                    bk = K.bank()
                    for h in range(4):
                        K.tr(bk[:, h * 128:(h + 1) * 128], Bb[0][:, h, :], ident, [Bb[0], CT], [bk], inc=(h == 3))
                    if GSTAGE == 62:
                        return
                    K.cp("act", Aa[0][:, :, :], bk[:, :].rearrange("p (h e) -> p h e", h=4), [bk], [Aa[0]])
                    if GSTAGE == 63:
                        return
                    for h in range(4):
                        K.tt("dve", Pm[:, h, :], Aa[0][:, h, :], ident, ALU.add, [Aa[0], CT], [Pm])
                    if GSTAGE < 7:
                        return
                    cur = 0
                    for lev in range(1, 6):
                        nxt = 1 - cur
                        if lev <= 4:
                            yield
                            bk = K.bank()
                            for h in range(4):
                                K.mm(bk[:, h * 128:(h + 1) * 128], Bb[cur][:, h, :], Aa[cur][:, h, :], True, True, [Bb[cur], Aa[cur]], [bk], inc=(h == 3))
                            K.cp("act", Aa[nxt][:, :, :], bk[:, :].rearrange("p (h e) -> p h e", h=4), [bk], [Aa[nxt]])
                        yield
                        bk = K.bank()
                        for h in range(4):
                            K.mm(bk[:, h * 128:(h + 1) * 128], Aa[cur][:, h, :], Bb[cur][:, h, :], True, True, [Bb[cur], Aa[cur]], [bk], inc=(h == 3))
                        K.cp("dve", Bb[nxt][:, :, :], bk[:, :].rearrange("p (h e) -> p h e", h=4), [bk], [Bb[nxt]])
                        yield
                        bk = K.bank()
                        for h in range(4):
                            K.mm(bk[:, h * 128:(h + 1) * 128], Bb[nxt][:, h, :], Pm[:, h, :], True, True, [Bb[nxt], Pm], [bk], inc=(h == 3))
                        K.tt("dve", Pm[:, :, :], Pm[:, :, :], bk[:, :].rearrange("p (h e) -> p h e", h=4), ALU.add, [Pm, bk], [Pm])
                        cur = nxt
                    if GSTAGE < 8:
                        return
                    K.cp("act", TTb[:, :, :], Pm[:, :, :], [Pm], [TTb])
                    yield
                    bk = K.bank()
                    for h in range(4):
                        K.mm(bk[:, h * 128:(h + 1) * 128], TTb[:, h, :], vbt[:, h * 128:(h + 1) * 128], True, True, [TTb, vbt], [bk], inc=(h == 3))
                    K.cp("act", uu[:], bk[:, :], [bk], [uu])
                    yield
                    bk = K.bank()
                    for h in range(4):
                        K.mm(bk[:, h * 128:(h + 1) * 128], kbg[:, h * 128:(h + 1) * 128], TTb[:, h, :], True, True, [TTb, kbg], [bk], inc=(h == 3))
                    K.cp("act", wT[:, :, :], bk[:, :].rearrange("p (h e) -> p h e", h=4), [bk], [wT])
                    yield
                    bk = K.bank()
                    for h in range(4):
                        K.mm(bk[:, h * 128:(h + 1) * 128], knT[:, h, :], qnT[:, h, :], True, True, [knT, qnT], [bk], inc=(h == 3))
                    K.tt("dve", attnT[:, :, :], bk[:, :].rearrange("p (h e) -> p h e", h=4), decTg[:, :, :], ALU.mult, [bk, decTg], [attnT])
                    if GSTAGE < 9:
                        return
                    for hf in range(2):
                        yield
                        bk = K.bank()
                        for h in range(4):
                            K.mm(bk[:, h * 128:(h + 1) * 128], wT[:, h, :], Sgb[:, h * 128:(h + 1) * 128], True, True, [wT, Sgb], [bk], inc=(h == 3))
                        K.tt("dve", vn[hf][:], uu[:], bk[:, :], ALU.subtract, [uu, bk], [vn[hf]])
                        yield
                        bk = K.bank()
                        for h in range(4):
                            K.mm(bk[:, h * 128:(h + 1) * 128], qnT[:, h, :], Sgb[:, h * 128:(h + 1) * 128], True, True,
                                 [qnT, Sgb], [bk], inc=(h == 3))
                        K.ts("dve", sml[:, 104 + 4 * hf:108 + 4 * hf], sml[:, 36:40], C("bs%d" % hf)[:, 0:1], None, ALU.mult, None, [sml, CT], [sml])
                        dst = otmp if hf == 0 else uu
                        K.tt("dve", dst[:, :].rearrange("p (h e) -> p h e", h=4), bk[:, :].rearrange("p (h e) -> p h e", h=4),
                             sml[:, 104 + 4 * hf:108 + 4 * hf].unsqueeze(2).to_broadcast([128, 4, 128]), ALU.mult, [bk, sml], [dst])
                        if hf == 1:
                            K.tt("dve", otmp[:], otmp[:], uu[:], ALU.add, [otmp, uu], [otmp])
                        yield
                        bk = K.bank()
                        for h in range(4):
                            K.mm(bk[:, h * 128:(h + 1) * 128], kdm[hf][:, h * 128:(h + 1) * 128], vn[hf][:, h * 128:(h + 1) * 128],
                                 True, True, [kdm[hf], vn[hf]], [bk], inc=(h == 3))
                        K.tt("dve", Sg[:, :].rearrange("p (h e) -> p h e", h=4), Sg[:, :].rearrange("p (h e) -> p h e", h=4),
                             sml[:, 44 + 4 * hf:48 + 4 * hf].unsqueeze(2).to_broadcast([128, 4, 128]), ALU.mult, [Sg, sml], [Sg])
                        K.tt("dve", Sg[:], Sg[:], bk[:, :], ALU.add, [Sg, bk], [Sg])
                        K.cp("act", Sgb[:], Sg[:], [Sg], [Sgb])
                    K.ts("dve", vf[:], vn[0][:], C("bs0")[:, 0:1], None, ALU.mult, None, [vn[0], CT], [vf])
                    K.stt("dve", vf[:], vn[1][:], C("bs1")[:, 0:1], vf[:], ALU.mult, ALU.add, [vn[1], CT, vf], [vf])
                    yield
                    bk = K.bank()
                    for h in range(4):
                        K.mm(bk[:, h * 128:(h + 1) * 128], attnT[:, h, :], vf[:, h * 128:(h + 1) * 128], True, True, [attnT, vf], [bk], inc=(h == 3))
                    K.tt("dve", oo[:], otmp[:], bk[:, :], ALU.add, [otmp, bk], [oo])
                    K.act(otmp[:], oo[:], AF.Square, [oo], [otmp])
                    K.rsum(sm[:, 4:8], otmp[:, :].rearrange("p (h e) -> p h e", h=4), [otmp], [sm])
                    rstd_from_ss(sm[:, 4:8], sm[:, 4:8], sm, 128.0, EPS)
                    K.tt("dve", oo[:, :].rearrange("p (h e) -> p h e", h=4), oo[:, :].rearrange("p (h e) -> p h e", h=4),
                         sm[:, 4:8].unsqueeze(2).to_broadcast([128, 4, 128]), ALU.mult, [oo, sm], [oo])
                    silu_to(otmp[:], otmp, zt[:], zt, uu[:], uu)
                    K.tt("dve", oo[:], oo[:], otmp[:], ALU.mult, [oo, otmp], [oo])
                    K.dma("sp", s_o[t % 2], yc[t * 128:(t + 1) * 128, 1024:1536], oo[:], R=[oo], W=[yc_b[t][2]])
                    yield

                pipeline(A, B, 5)
                K.barrier()

            for pst in _phase("out"):
                ph = {"st": pst}
                common_alloc(ph, l, None, None)
                WB = K.sb(pst, "WBo", [128, 12, 1024], BF16)
                load_w(WB, WB, [w_out[l, k * 128:(k + 1) * 128, :] for k in range(12)], 12)
                yt = [K.sb(pst, f"yt{i}", [128, MIXW], F32) for i in range(2)]
                yTs = [K.sb(pst, f"yT{i}", [128, 12, 128], BF16) for i in range(2)]
                xo = [K.sb(pst, f"xo{i}", [128, D], F32) for i in range(2)]
                ynw = pfm(ph, "ynw")
                def A(t):
                    par = t % 2
                    yT = yTs[par]
                    xt = load_x(ph, l, t)
                    ytt = yt[t % 2]
                    K.dma("sp", s_a[t % 2], ytt[:], yc[t * 128:(t + 1) * 128, :], R=yc_b[t], W=[ytt])
                    for j0 in range(0, 12, 4):
                        bk = K.bank()
                        for j in range(4):
                            K.tr(bk[:, j * 128:(j + 1) * 128], ytt[:, (j0 + j) * 128:(j0 + j + 1) * 128], ident, [ytt, CT], [bk], inc=(j == 3))
                        K.tt("dve", yT[:, j0:j0 + 4, :], bk[:, :].rearrange("p (c e) -> p c e", c=4),
                             ynw[:, j0:j0 + 4].unsqueeze(2).to_broadcast([128, 4, 128]), ALU.mult, [bk, ph["pfm"]], [yT])
                    yield

                def B(t):
                    par = t % 2
                    yT = yTs[par]
                    xt = ph["xt"][par]
                    xot = xo[t % 2]
                    for n in range(2):
                        yield
                        bk = K.bank()
                        for k in range(12):
                            K.mm(bk[:, :], yT[:, k, :], WB[:, k, n * 512:(n + 1) * 512], k == 0, k == 11, [yT, WB], [bk], inc=(k == 11))
                        K.tt("dve", xot[:, n * 512:(n + 1) * 512], xt[:, n * 512:(n + 1) * 512], bk[:, :], ALU.add, [xt, bk], [xot])
                    K.dma("sp", s_o[t % 2], xb[t * 128:(t + 1) * 128, :], xot[:], R=[xot], W=[xb_b[t]])
                    yield

                pipeline(A, B, 1)
                K.barrier()

            for pst in _phase("mlp"):
                ph = {"st": pst}
                common_alloc(ph, l, None, None)
                WU = K.sb(pst, "WU", [128, 8, DFF], BF16)
                WD = K.sb(pst, "WD", [128, 32, D], BF16)
                load_w(WU, WU, [w_up[l, k * 128:(k + 1) * 128, :] for k in range(8)], 8)
                load_w(WD, WD, [w_down[l, k * 128:(k + 1) * 128, :] for k in range(32)], 32, s_w2)
                aT = K.sb(pst, "aT", [128, 32, 128], BF16)
                rl = K.sb(pst, "rl", [128, 512], F32)
                xo = [K.sb(pst, f"xo{i}", [128, D], F32) for i in range(2)]
                def A(t):
                    par = t % 2
                    xt = ph["xt"][t % 2]
                    K.dma("sp", s_x[t % 2], xt[:], xb[t * 128:(t + 1) * 128, :], R=[xb_b[t]], W=[xt])
                    norm_hT(ph, xt, pfm(ph, "mlpnw"), ph["hf"][par], ph["hT"][par], ph["smA"][par])
                    yield
                    yield

                def B(t):
                    par = t % 2
                    xt = ph["xt"][par]
                    for f0 in range(0, 32, 4):
                        yield
                        bk = K.bank()
                        for j in range(4):
                            f = f0 + j
                            for k in range(8):
                                K.mm(bk[:, j * 128:(j + 1) * 128], WU[:, k, f * 128:(f + 1) * 128], ph["hT"][par][:, k, :],
                                     k == 0, k == 7, [ph["hT"][par], WU], [bk], inc=(j == 3 and k == 7))
                        K.act(rl[:], bk[:, :], AF.Relu, [bk], [rl])
                        K.tt("dve", aT[:, f0:f0 + 4, :], rl[:, :].rearrange("p (c e) -> p c e", c=4),
                             rl[:, :].rearrange("p (c e) -> p c e", c=4), ALU.mult, [rl], [aT])
                    xot = xo[t % 2]
                    for n in range(2):
                        yield
                        bk = K.bank()
                        for k in range(32):
                            K.mm(bk[:, :], aT[:, k, :], WD[:, k, n * 512:(n + 1) * 512], k == 0, k == 31, [aT, WD], [bk], inc=(k == 31))
                        K.tt("dve", xot[:, n * 512:(n + 1) * 512], xt[:, n * 512:(n + 1) * 512], bk[:, :], ALU.add, [xt, bk], [xot])
                    K.dma("sp", s_o[t % 2], xb[t * 128:(t + 1) * 128, :], xot[:], R=[xot], W=[xb_b[t]])
                    yield

                pipeline(A, B, 3)
                K.barrier()

        for pst in _phase("final"):
            fnw = K.sb(pst, "fnw", [128, D], F32)
            K.dma("sp", s_pf, fnw[:], fnw_in, W=[fnw])
            xts = [K.sb(pst, f"fx{i}", [128, D], F32) for i in range(2)]
            hf2 = [K.sb(pst, f"fh{i}", [128, D], F32) for i in range(2)]
            sm = K.sb(pst, "fsm", [128, 8], F32)
            for t in range(NT):
                xt = xts[t % 2]
                hf = hf2[t % 2]
                K.dma("sp", s_x[t % 2], xt[:], xb[t * 128:(t + 1) * 128, :], R=[xb_b[t]], W=[xt])
                K.act(hf[:], xt[:], AF.Square, [xt], [hf, sm], accum_out=sm[:, 0:1])
                K.act(sm[:, 1:2], sm[:, 0:1], AF.Ln, [sm], [sm], scale=1.0 / D, bias=EPS)
                K.act(sm[:, 2:3], sm[:, 1:2], AF.Exp, [sm], [sm], scale=-0.5)
                K.stt("dve", hf[:], xt[:], sm[:, 2:3], fnw[:], ALU.mult, ALU.mult, [xt, sm, fnw], [hf])
                K.dma("sp", s_o[t % 2], y_out[t * 128:(t + 1) * 128, :], hf[:], R=[hf], W=[y_b[t]])
            K.barrier()
        build_program.stats = dict(K.ninst, nsem=K.nsem)
    return nc


def _run(inp, NT, DEPTH, debug=False, ncores=8):
    B = inp["x"].shape[0]
    pf, pt = _pack_params(inp, DEPTH)
    fnw = np.ascontiguousarray(np.broadcast_to(np.asarray(inp["final_norm_w"], np.float32)[None, :], (128, D)))
    nc = build_program(NT, DEPTH, debug)
    in_maps = []
    for c in range(ncores):
        b = c % B
        in_maps.append({
            "x": np.ascontiguousarray(inp["x"][b], dtype=np.float32),
            "pos": np.ascontiguousarray(np.asarray(inp["positions"][b], np.int32).reshape(NT, 128).T),
            "w_in": np.ascontiguousarray(inp["w_in"][:DEPTH], dtype=np.float32),
            "w_out": np.ascontiguousarray(inp["w_out"][:DEPTH], dtype=np.float32),
            "w_up": np.ascontiguousarray(inp["w_up"][:DEPTH], dtype=np.float32),
            "w_down": np.ascontiguousarray(inp["w_down"][:DEPTH], dtype=np.float32),
            "pf": pf, "pt": pt, "fnw": fnw, "consts": CONST_TABLE,
        })
    res = run_bass_kernel_spmd(nc, in_maps, core_ids=list(range(ncores)))
    return res


def kernel(**inputs):
    inp = {k: np.asarray(v) for k, v in inputs.items()}
    B, S, _ = inp["x"].shape
    res = _run(inp, S // 128, inp["w_in"].shape[0])
    out = np.stack([np.asarray(res.results[b]["y"], dtype=np.float32) for b in range(B)], axis=0)
    return out
```

```python
import math
from contextlib import ExitStack

import numpy as np
import concourse.bass as bass
import concourse.mybir as mybir
from concourse.bass_utils import run_bass_kernel_spmd

F32 = mybir.dt.float32
BF16 = mybir.dt.bfloat16
I32 = mybir.dt.int32
ALU = mybir.AluOpType
AF = mybir.ActivationFunctionType
AX = mybir.AxisListType

D = 1024
DIN = 5648
DFF = 4096
MIXW = 1536
EPS = 1e-6
EPOCH_LEN = 12000
NEG = -30000.0


class Buf:
    def __init__(self, name):
        self.name = name
        self.w = None
        self.r = []


class Prod:
    def __init__(self, K, name, handle):
        self.K = K
        self.name = name
        self.h = handle
        self.sems = []
        self.epoch = -1
        self.cnt = 0
        self.pending = False
        self.new_epoch()

    def new_epoch(self):
        self.sems.append(self.K.new_sem(f"{self.name}_e{len(self.sems)}"))
        self.epoch += 1
        self.cnt = 0


class T:
    def __init__(self, t, name):
        self.t = t
        self.b = Buf(name)

    def __getitem__(self, idx):
        return self.t[idx]


def _bufs(lst):
    out = []
    for x in lst:
        if isinstance(x, (list, tuple)):
            out.extend(_bufs(x))
        elif isinstance(x, T):
            out.append(x.b)
        else:
            out.append(x)
    return out


class Kern:
    def __init__(self, nc, stack):
        self.nc = nc
        self.stack = stack
        self.nsem = 0
        self.prods = {}
        for nm, h in (("pe", nc.tensor), ("act", nc.scalar), ("dve", nc.vector),
                      ("pool", nc.gpsimd), ("sp", nc.sync)):
            self.prods[nm] = Prod(self, nm, h)
        self.engines = ["pe", "act", "dve", "pool", "sp"]
        self.waited = {}
        self.ninst = {k: 0 for k in self.engines}
        self.banks = []
        self.cur_pool = 0
        self.pool_i = [0, 0]

    def new_sem(self, name):
        self.nsem += 1
        return self.stack.enter_context(self.nc.semaphore(name))

    def dma_slot(self, name):
        p = Prod(self, "dma_" + name, None)
        self.prods[p.name] = p
        return p

    def sb(self, stack, name, shape, dtype):
        self.nsb = getattr(self, "nsb", 0) + 1
        name = f"s{self.nsb}_{name}"
        return T(stack.enter_context(self.nc.sbuf_tensor(name, list(shape), dtype)), name)

    def make_banks(self):
        for i in range(8):
            t = self.stack.enter_context(self.nc.psum_tensor(f"bank{i}", [128, 512], F32))
            self.banks.append(T(t, f"bank{i}"))

    def bank(self):
        p = self.cur_pool
        b = self.banks[4 * p + self.pool_i[p] % 4]
        self.pool_i[p] += 1
        return b

    def _need(self, eng, reads, writes):
        need = {}

        def add(tok):
            if tok is None:
                return
            p, ep, c = tok
            if p is eng and eng.name == "pe":
                return
            key = (p.name, ep)
            if need.get(key, (None, 0))[1] < c:
                need[key] = (p, c)

        for b in reads:
            add(b.w)
        for b in writes:
            add(b.w)
            for t in b.r:
                add(t)
        for (pname, ep), (p, c) in need.items():
            wk = (eng.name, pname, ep)
            if self.waited.get(wk, 0) >= c:
                continue
            if ep == p.epoch:
                assert c <= p.cnt, f"wait on un-issued inc: {eng.name} waits {pname} {c}>{p.cnt}"
            eng.h.wait_ge(p.sems[ep], c)
            self.waited[wk] = c

    def _mark(self, tok, reads, writes):
        for b in reads:
            b.r.append(tok)
            if len(b.r) > 48:
                best = {}
                for (p, ep, c) in b.r:
                    k = (p.name, ep)
                    if k not in best or best[k][2] < c:
                        best[k] = (p, ep, c)
                b.r = list(best.values())
        for b in writes:
            b.w = tok
            b.r = []

    def op(self, engname, fn, R=(), W=(), inc=True):
        reads, writes = _bufs(R), _bufs(W)
        eng = self.prods[engname]
        self._need(eng, reads, writes)
        inst = fn(eng.h)
        self.ninst[engname] += 1
        if inc:
            if eng.cnt + 1 > EPOCH_LEN:
                assert not eng.pending
                eng.new_epoch()
            eng.cnt += 1
            inst.then_inc(eng.sems[eng.epoch], 1)
            eng.pending = False
            tok = (eng, eng.epoch, eng.cnt)
        else:
            if eng.cnt + 1 > EPOCH_LEN and not eng.pending:
                eng.new_epoch()
            eng.pending = True
            tok = (eng, eng.epoch, eng.cnt + 1)
        self._mark(tok, reads, writes)
        return inst

    def dma(self, qname, slot, out, in_, R=(), W=(), **kw):
        reads, writes = _bufs(R), _bufs(W)
        q = self.prods[qname]
        self._need(q, reads, writes)
        inst = q.h.dma_start(out=out, in_=in_, **kw)
        self.ninst[qname] += 1
        if slot.cnt + 16 > EPOCH_LEN:
            slot.new_epoch()
        slot.cnt += 16
        inst.then_inc(slot.sems[slot.epoch], 16)
        tok = (slot, slot.epoch, slot.cnt)
        self._mark(tok, reads, writes)
        return inst

    def barrier(self):
        for en in self.engines:
            e = self.prods[en]
            for p in self.prods.values():
                assert not p.pending
                if p.cnt == 0:
                    continue
                wk = (e.name, p.name, p.epoch)
                if self.waited.get(wk, 0) >= p.cnt:
                    continue
                e.h.wait_ge(p.sems[p.epoch], p.cnt)
                self.waited[wk] = p.cnt

    def tt(self, eng, out, in0, in1, op, R, W):
        return self.op(eng, lambda h: h.tensor_tensor(out=out, in0=in0, in1=in1, op=op), R, W)

    def ts(self, eng, out, in0, s1, s2, op0, op1, R, W):
        if s2 is None:
            return self.op(eng, lambda h: h.tensor_scalar(out=out, in0=in0, scalar1=s1, scalar2=None, op0=op0), R, W)
        return self.op(eng, lambda h: h.tensor_scalar(out=out, in0=in0, scalar1=s1, scalar2=s2, op0=op0, op1=op1), R, W)

    def stt(self, eng, out, in0, scalar, in1, op0, op1, R, W):
        return self.op(eng, lambda h: h.scalar_tensor_tensor(out=out, in0=in0, scalar=scalar, in1=in1,
                                                             op0=op0, op1=op1), R, W)

    def act(self, out, in_, func, R, W, **kw):
        return self.op("act", lambda h: h.activation(out=out, in_=in_, func=func, **kw), R, W)

    def cp(self, eng, out, in_, R, W):
        if eng == "act":
            return self.op("act", lambda h: h.copy(out=out, in_=in_), R, W)
        return self.op(eng, lambda h: h.tensor_copy(out=out, in_=in_), R, W)

    def mm(self, out, lhsT, rhs, start, stop, R, W, inc):
        return self.op("pe", lambda h: h.matmul(out, lhsT=lhsT, rhs=rhs, start=start, stop=stop), R, W, inc=inc)

    def tr(self, out, in_, ident, R, W, inc):
        return self.op("pe", lambda h: h.transpose(out=out, in_=in_, identity=ident), R, W, inc=inc)

    def rsum(self, out, in_, R, W):
        return self.op("dve", lambda h: h.tensor_reduce(out=out, in_=in_, axis=AX.X, op=ALU.add), R, W)

    def recip(self, out, in_, R, W):
        return self.op("dve", lambda h: h.reciprocal(out=out, in_=in_), R, W)


C_OFF = {}


def _const_table():
    cols = []

    def add(name, arr):
        arr = np.asarray(arr, np.float32)
        full = np.zeros((128, arr.shape[1]), np.float32)
        full[:arr.shape[0]] = arr
        C_OFF[name] = (sum(c.shape[1] for c in cols), arr.shape[1])
        cols.append(full)

    i = np.arange(128)
    m, l = i[:, None], i[None, :]
    same = (m // 64) == (l // 64)
    add("ident", np.eye(128))
    add("tri", (m <= l))
    add("tribd", (m <= l) & same)
    add("ones", np.ones((128, 128)))
    add("blockones", same)
    add("bs0", np.broadcast_to(m < 64, (128, 128)))
    add("bs1", np.broadcast_to(m >= 64, (128, 128)))
    add("negssd", np.where(l >= m, 0.0, NEG))
    add("negI", np.where((l >= m) & same, 0.0, NEG))
    add("posS", np.where((m > l) & same, 0.0, -NEG))
    sel8 = np.zeros((8, 8 * 128))
    for h in range(8):
        sel8[h, h * 128:(h + 1) * 128] = 1.0
    add("sel8", sel8)
    gam = 1.0 - np.exp2(-5.0 - np.arange(4))
    lg = np.log(gam)
    maskP = np.zeros((128, 512))
    for h in range(4):
        maskP[:, h * 128:(h + 1) * 128] = np.where(l >= m, np.exp(-lg[h] * (m + 1.0)), 0.0)
    add("maskP", maskP)
    add("xiq", np.exp(lg[None, :] * (i[:, None] + 1.0)) * (128.0 ** -0.5))
    add("zeta", np.exp(lg[None, :] * (127.0 - i[:, None])))
    invf = 10000.0 ** (-np.arange(64, dtype=np.float32) / 64.0)
    add("invf", np.broadcast_to(invf[None, :].astype(np.float32), (128, 64)))
    return np.concatenate(cols, axis=1), [float(g ** 128) for g in gam]


CONST_TABLE, RET_G128 = _const_table()
NCONST = CONST_TABLE.shape[1]

PF = {"mixnw": (0, 8), "mlpnw": (8, 8), "ynw": (16, 12), "scw": (28, 32), "scb": (60, 8), "gcw": (68, 48)}
NPF = 116
PT = {"bias16": (0, 16), "alog": (16, 12), "dskip": (28, 8)}
NPT = 36


def _pack_params(inp, depth):
    pf = np.zeros((depth, 128, NPF), np.float32)
    pt = np.zeros((depth, 128, NPT), np.float32)
    for l in range(depth):
        pf[l, :, 0:8] = inp["mix_norm_w"][l].reshape(8, 128).T
        pf[l, :, 8:16] = inp["mlp_norm_w"][l].reshape(8, 128).T
        pf[l, :, 16:20] = inp["ret_norm_w"][l].reshape(4, 128).T
        pf[l, :, 20:24] = inp["ssd_norm_w"][l].reshape(4, 128).T
        pf[l, :, 24:28] = np.repeat(inp["gdn_norm_w"][l].reshape(128, 1), 4, axis=1)
        pf[l, :, 28:60] = inp["ssd_conv_w"][l].reshape(4, 8, 128).transpose(2, 1, 0).reshape(128, 32)
        pf[l, :, 60:68] = inp["ssd_conv_b"][l].reshape(8, 128).T
        pf[l, :, 68:116] = inp["gdn_conv_w"][l].reshape(4, 12, 128).transpose(2, 1, 0).reshape(128, 48)
        pt[l, :, 0:8] = inp["ssd_dt_bias"][l][None, :]
        pt[l, :, 12:16] = inp["gdn_dt_bias"][l][None, :]
        pt[l, :, 16:24] = inp["ssd_a_log"][l][None, :]
        pt[l, :, 24:28] = inp["gdn_a_log"][l][None, :]
        pt[l, :, 28:36] = inp["ssd_d"][l][None, :]
    return pf, pt


GSTAGE = 99
MAXSTEP = 10 ** 9
CONV_ENG = "pool"
PHASES = {"init", "ret", "ssd", "gdn", "out", "mlp", "final"}


def _phase(name):
    if name in PHASES:
        with ExitStack() as s:
            yield s


def build_program(NT, DEPTH, debug=False):
    S = NT * 128
    nc = bass.Bass("TRN2", target_bir_lowering=False)
    x_in = nc.dram_tensor("x", [S, D], F32, kind="ExternalInput").ap()
    pos_in = nc.dram_tensor("pos", [128, NT], I32, kind="ExternalInput").ap()
    w_in = nc.dram_tensor("w_in", [DEPTH, D, DIN], F32, kind="ExternalInput").ap()
    w_out = nc.dram_tensor("w_out", [DEPTH, MIXW, D], F32, kind="ExternalInput").ap()
    w_up = nc.dram_tensor("w_up", [DEPTH, D, DFF], F32, kind="ExternalInput").ap()
    w_down = nc.dram_tensor("w_down", [DEPTH, DFF, D], F32, kind="ExternalInput").ap()
    pf_in = nc.dram_tensor("pf", [DEPTH, 128, NPF], F32, kind="ExternalInput").ap()
    pt_in = nc.dram_tensor("pt", [DEPTH, 128, NPT], F32, kind="ExternalInput").ap()
    fnw_in = nc.dram_tensor("fnw", [128, D], F32, kind="ExternalInput").ap()
    const_in = nc.dram_tensor("consts", [128, NCONST], F32, kind="ExternalInput").ap()
    y_out = nc.dram_tensor("y", [S, D], F32, kind="ExternalOutput").ap()
    dk = "ExternalOutput" if debug else "Internal"
    xb = nc.dram_tensor("xb", [S, D], F32, kind=dk).ap()
    yc = nc.dram_tensor("yc", [S, MIXW], F32, kind=dk).ap()
    csd = nc.dram_tensor("csd", [128, NT, 128], F32, kind="Internal").ap()

    with ExitStack() as st:
        st.enter_context(nc.allow_low_precision("bf16 matmul operands, fp32 accumulation"))
        K = Kern(nc, st)
        K.make_banks()
        xb_b = [Buf(f"xb{t}") for t in range(NT)]
        yc_b = [[Buf(f"yc{t}_{j}") for j in range(3)] for t in range(NT)]
        y_b = [Buf(f"y{t}") for t in range(NT)]
        csd_b = Buf("csd")
        s_c = K.dma_slot("c")
        s_pf = K.dma_slot("pf")
        s_pt = K.dma_slot("pt")
        s_cs = K.dma_slot("cs")
        s_w = K.dma_slot("w")
        s_w2 = K.dma_slot("w2")
        s_x = [K.dma_slot(f"x{i}") for i in range(2)]
        s_a = [K.dma_slot(f"a{i}") for i in range(2)]
        s_o = [K.dma_slot(f"o{i}") for i in range(2)]

        CT = K.sb(st, "consts", [128, NCONST], F32)
        K.dma("sp", s_c, CT[:], const_in, W=[CT])

        def C(name, rows=128):
            o, n = C_OFF[name]
            return CT[0:rows, o:o + n]

        ident = C("ident")

        def xsrc(l, t):
            if l == 0:
                return x_in[t * 128:(t + 1) * 128, :], []
            return xb[t * 128:(t + 1) * 128, :], [xb_b[t]]

        def norm_hT(ph, xt, nwcols, hf, hT, sm):
            K.act(hf[:], xt[:], AF.Square, [xt], [hf, sm], accum_out=sm[:, 0:1])
            K.act(sm[:, 1:2], sm[:, 0:1], AF.Ln, [sm], [sm], scale=1.0 / D, bias=EPS)
            K.act(sm[:, 2:3], sm[:, 1:2], AF.Exp, [sm], [sm], scale=-0.5)
            K.act(hf[:], xt[:], AF.Copy, [xt, sm], [hf], scale=sm[:, 2:3])
            for b in range(2):
                bk = K.bank()
                for j in range(4):
                    k = b * 4 + j
                    K.tr(bk[:, j * 128:(j + 1) * 128], hf[:, k * 128:(k + 1) * 128], ident, [hf, CT], [bk], inc=(j == 3))
                K.tt("dve", hT[:, b * 4:b * 4 + 4, :], bk[:, :].rearrange("p (c e) -> p c e", c=4),
                     nwcols[:, b * 4:b * 4 + 4].unsqueeze(2).to_broadcast([128, 4, 128]), ALU.mult, [bk, ph["pfm"]], [hT])

        def proj_tm(hT, WB, c0, n, out_ap, outT, eng):
            bk = K.bank()
            for k in range(8):
                K.mm(bk[:, 0:n], hT[:, k, :], WB[:, k, c0:c0 + n], k == 0, k == 7, [hT, WB], [bk], inc=(k == 7))
            K.cp(eng, out_ap, bk[:, 0:n], [bk], [outT])

        def proj_fm(hT, WB, c0, nchunk, cx, j0):
            bk = K.bank()
            for j in range(nchunk):
                for k in range(8):
                    K.mm(bk[:, j * 128:(j + 1) * 128], WB[:, k, c0 + j * 128:c0 + (j + 1) * 128], hT[:, k, :],
                         k == 0, k == 7, [hT, WB], [bk], inc=(j == nchunk - 1 and k == 7))
            K.cp("act", cx[:, j0:j0 + nchunk, 3:131],
                 bk[:, 0:nchunk * 128].rearrange("p (c e) -> p c e", c=nchunk), [bk], [cx])

        def conv_silu(cx, cxn, cw, cb, nch, acc, tmps):
            for c0 in range(0, nch, 4):
                a = acc[:, c0:c0 + 4, :]
                for j in range(1, 4):
                    wj = cw[:, c0:c0 + 4, j:j + 1].to_broadcast([128, 4, 128])
                    K.tt(CONV_ENG, tmps[j - 1][:, 0:4, :], cx[:, c0:c0 + 4, j:j + 128], wj, ALU.mult, [cx, ph_cur["pfm"]], [tmps[j - 1]])
                w0 = cw[:, c0:c0 + 4, 0:1].to_broadcast([128, 4, 128])
                K.tt("dve", a, cx[:, c0:c0 + 4, 0:128], w0, ALU.mult, [cx, ph_cur["pfm"]], [acc])
                for j in range(1, 4):
                    K.tt("dve", a, a, tmps[j - 1][:, 0:4, :], ALU.add, [acc, tmps[j - 1]], [acc])
                if cb is not None:
                    K.tt("dve", a, a, cb[:, c0:c0 + 4].unsqueeze(2).to_broadcast([128, 4, 128]), ALU.add,
                         [acc, ph_cur["pfm"]], [acc])
                t0 = tmps[0]
                K.act(t0[:, 0:4, :], a, AF.Exp, [acc], [t0], scale=-1.0)
                K.act(t0[:, 0:4, :], t0[:, 0:4, :], AF.Ln, [t0], [t0], bias=1.0)
                K.act(t0[:, 0:4, :], t0[:, 0:4, :], AF.Exp, [t0], [t0], scale=-1.0)
                K.tt("dve", a, a, t0[:, 0:4, :], ALU.mult, [acc, t0], [acc])
                yield
            K.cp("act", cxn[:, :, 0:3], cx[:, :, 128:131], [cx], [cxn])

        def silu_to(out_ap, outT, in_ap, inT, tmp_ap, tmpT):
            K.act(tmp_ap, in_ap, AF.Exp, [inT], [tmpT], scale=-1.0)
            K.act(tmp_ap, tmp_ap, AF.Ln, [tmpT], [tmpT], bias=1.0)
            K.act(tmp_ap, tmp_ap, AF.Exp, [tmpT], [tmpT], scale=-1.0)
            K.tt("dve", out_ap, in_ap, tmp_ap, ALU.mult, [inT, tmpT], [outT])

        def rstd_from_ss(out_ap, ss_ap, smT, n, eps):
            K.act(out_ap, ss_ap, AF.Ln, [smT], [smT], scale=1.0 / n, bias=eps)
            K.act(out_ap, out_ap, AF.Exp, [smT], [smT], scale=-0.5)

        def softplus16(sp, z, sm2, n, smlT):
            K.act(sm2[:, 0:n], z, AF.Abs, [sm2, smlT], [sm2])
            K.act(sm2[:, 0:n], sm2[:, 0:n], AF.Exp, [sm2], [sm2], scale=-1.0)
            K.act(sm2[:, 0:n], sm2[:, 0:n], AF.Ln, [sm2], [sm2], bias=1.0)
            K.ts("dve", sp, z, 0.0, None, ALU.max, None, [smlT], [smlT])
            K.tt("dve", sp, sp, sm2[:, 0:n], ALU.add, [smlT, sm2], [smlT])

        def load_w(WB_ap, WBT, src_ap, nk, slot=None):
            for k in range(nk):
                K.dma("pool", slot or s_w, WB_ap[:, k, :], src_ap[k], W=[WBT])

        ph_cur = {}

        def step(g, pool):
            K.cur_pool = pool
            try:
                next(g)
                return True
            except StopIteration:
                return False
            finally:
                K.cur_pool = 0

        def pipeline(A, B, ratio):
            ga = A(0)
            while step(ga, 0):
                pass
            for t in range(NT):
                gb = B(t)
                ga = A(t + 1) if t + 1 < NT else iter(())
                doneA = False
                i = 0
                while step(gb, 1):
                    i += 1
                    if not doneA and i % ratio == 0:
                        doneA = not step(ga, 0)
                while not doneA:
                    doneA = not step(ga, 0)

        def pipeline2(A, B):
            def G(t):
                i = 0
                for gen in (A(t), B(t)):
                    for _ in gen:
                        i += 1
                        if i >= MAXSTEP:
                            return
                        yield
            n0 = 0
            g = G(0)
            while step(g, 0):
                n0 += 1
            half = max(1, n0 // 2)
            t_next = 1
            older = None
            younger = None
            so = 0
            sy = 0
            po = py = 0
            if t_next < NT:
                older, po, so = G(t_next), t_next % 2, 0
                t_next += 1
            while older is not None:
                if step(older, po):
                    so += 1
                else:
                    older, po, so = younger, py, sy
                    younger = None
                    if older is None:
                        if t_next < NT:
                            older, po, so = G(t_next), t_next % 2, 0
                            t_next += 1
                        continue
                if younger is None and so >= half and t_next < NT:
                    younger, py, sy = G(t_next), t_next % 2, 0
                    t_next += 1
                if younger is not None:
                    if step(younger, py):
                        sy += 1
                    else:
                        younger = None

        def common_alloc(ph, l, wcols, wname):
            ph_cur.clear()
            ph["pfm"] = K.sb(ph["st"], "pfm", [128, NPF], F32)
            ph["ptm"] = K.sb(ph["st"], "ptm", [128, NPT], F32)
            K.dma("sp", s_pf, ph["pfm"][:], pf_in[l], W=[ph["pfm"]])
            K.dma("sp", s_pt, ph["ptm"][:], pt_in[l], W=[ph["ptm"]])
            ph["xt"] = [K.sb(ph["st"], f"xt{i}", [128, D], F32) for i in range(2)]
            ph["hf"] = [K.sb(ph["st"], f"hf{i}", [128, D], F32) for i in range(2)]
            ph["hT"] = [K.sb(ph["st"], f"hT{i}", [128, 8, 128], BF16) for i in range(2)]
            ph["smA"] = [K.sb(ph["st"], f"smA{i}", [128, 8], F32) for i in range(2)]
            ph["raw"] = [K.sb(ph["st"], f"raw{i}", [128, 16], F32) for i in range(2)]
            ph["sm"] = [K.sb(ph["st"], f"sm{i}", [128, 8], F32) for i in range(2)]
            ph["small"] = [K.sb(ph["st"], f"small{i}", [128, 128], F32) for i in range(2)]
            ph["sm2"] = [K.sb(ph["st"], f"sm2{i}", [128, 16], F32) for i in range(2)]
            for sm_ in ph["small"]:
                K.op("dve", lambda h: h.memset(sm_[:], 0.0), [], [sm_])
            ph_cur.update(ph)

        def pfm(ph, name):
            o, n = PF[name]
            return ph["pfm"][:, o:o + n]

        def ptm(ph, name):
            o, n = PT[name]
            return ph["ptm"][:, o:o + n]

        def load_x(ph, l, t):
            src, rb = xsrc(l, t)
            xt = ph["xt"][t % 2]
            K.dma("sp", s_x[t % 2], xt[:], src, R=rb, W=[xt])
            return xt

        for pst in _phase("init"):
            posi = K.sb(pst, "posi", [128, NT], I32)
            posf = K.sb(pst, "posf", [128, NT], F32)
            ang = K.sb(pst, "ang", [128, NT, 64], F32)
            kk = K.sb(pst, "kk", [128, NT, 64], F32)
            cs = K.sb(pst, "cs", [128, NT, 128], F32)
            K.dma("sp", s_pf, posi[:], pos_in, W=[posi])
            K.cp("dve", posf[:], posi[:], [posi], [posf])
            K.tt("dve", ang[:], posf[:, :].unsqueeze(2).to_broadcast([128, NT, 64]),
                 C("invf").unsqueeze(1).to_broadcast([128, NT, 64]), ALU.mult, [posf, CT], [ang])
            MAGIC = 12582912.0
            TWO_PI = 2.0 * math.pi
            for which, shift in ((1, 0.0), (0, math.pi / 2.0)):
                if shift != 0.0:
                    K.ts("dve", ang[:], ang[:], shift, None, ALU.add, None, [ang], [ang])
                K.ts("dve", kk[:], ang[:], 1.0 / TWO_PI, MAGIC, ALU.mult, ALU.add, [ang], [kk])
                K.ts("dve", kk[:], kk[:], MAGIC, None, ALU.subtract, None, [kk], [kk])
                K.stt("dve", kk[:], kk[:], -TWO_PI, ang[:], ALU.mult, ALU.add, [kk, ang], [kk])
                K.ts("dve", kk[:], kk[:], math.pi, -math.pi, ALU.min, ALU.max, [kk], [kk])
                K.act(cs[:, :, which * 64:(which + 1) * 64], kk[:], AF.Sin, [kk], [cs])
            K.dma("sp", s_cs, csd, cs[:], R=[cs], W=[csd_b])
            K.barrier()

        for l in range(DEPTH):
            for pst in _phase("ret"):
                ph = {"st": pst}
                common_alloc(ph, l, None, None)
                WB = K.sb(pst, "WBr", [128, 8, 2048], BF16)
                load_w(WB, WB, [w_in[l, k * 128:(k + 1) * 128, 0:2048] for k in range(8)], 8)
                cs = K.sb(pst, "cs", [128, NT, 128], F32)
                K.dma("sp", s_cs, cs[:], csd, R=[csd_b], W=[cs])
                tms = [K.sb(pst, f"tm{i}", [128, 2048], F32) for i in range(2)]
                qr_s = [K.sb(pst, "qr", [128, 512], F32) for _p in range(2)]
                kr_s = [K.sb(pst, "kr", [128, 512], F32) for _p in range(2)]
                tA_s = [K.sb(pst, "tA", [128, 512], F32) for _p in range(2)]
                qxT_s = [K.sb(pst, "qxT", [128, 4, 128], BF16) for _p in range(2)]
                krT_s = [K.sb(pst, "krT", [128, 4, 128], BF16) for _p in range(2)]
                krb_s = [K.sb(pst, "krb", [128, 512], BF16) for _p in range(2)]
                vb_s = [K.sb(pst, "vb", [128, 512], BF16) for _p in range(2)]
                vz_s = [K.sb(pst, "vz", [128, 512], BF16) for _p in range(2)]
                smT_s = [K.sb(pst, "smT", [128, 4, 128], BF16) for _p in range(2)]
                Sr = K.sb(pst, "Sr", [128, 512], F32)
                Srb = K.sb(pst, "Srb", [128, 512], BF16)
                yo_s = [K.sb(pst, "yo", [128, 512], F32) for _p in range(2)]
                sm = ph["sm"][0]
                K.op("dve", lambda h: h.memset(Sr[:], 0.0), [], [Sr])
                K.op("dve", lambda h: h.memset(Srb[:], 0.0), [], [Srb])
                def A(t):
                    par = t % 2
                    qr = qr_s[par]
                    kr = kr_s[par]
                    tA = tA_s[par]
                    qxT = qxT_s[par]
                    krT = krT_s[par]
                    krb = krb_s[par]
                    vb = vb_s[par]
                    vz = vz_s[par]
                    smT = smT_s[par]
                    yo = yo_s[par]
                    sm = ph["sm"][par]
                    sml = ph["small"][par]
                    tm = tms[par]
                    xt = load_x(ph, l, t)
                    norm_hT(ph, xt, pfm(ph, "mixnw"), ph["hf"][par], ph["hT"][par], ph["smA"][par])
                    yield
                    for g in range(4):
                        proj_tm(ph["hT"][par], WB, g * 512, 512, tm[:, g * 512:(g + 1) * 512], tm, "act" if g % 2 else "dve")
                        yield
                    yield

                def B(t):
                    par = t % 2
                    qr = qr_s[par]
                    kr = kr_s[par]
                    tA = tA_s[par]
                    qxT = qxT_s[par]
                    krT = krT_s[par]
                    krb = krb_s[par]
                    vb = vb_s[par]
                    vz = vz_s[par]
                    smT = smT_s[par]
                    yo = yo_s[par]
                    sm = ph["sm"][par]
                    sml = ph["small"][par]
                    tm = tms[par]
                    cosb = cs[:, t, 0:64].unsqueeze(1).to_broadcast([128, 4, 64])
                    sinb = cs[:, t, 64:128].unsqueeze(1).to_broadcast([128, 4, 64])
                    for (src0, dst) in ((0, qr), (512, kr)):
                        v4 = tm[:, src0:src0 + 512].rearrange("p (h t e) -> p h t e", h=4, t=2)
                        d4 = dst[:, :].rearrange("p (h t e) -> p h t e", h=4, t=2)
                        a4 = tA[:, 0:256].rearrange("p (h e) -> p h e", h=4)
                        t1, t2 = v4[:, :, 0, :], v4[:, :, 1, :]
                        K.tt("dve", d4[:, :, 0, :], t1, cosb, ALU.mult, [tm, cs], [dst])
                        K.tt("dve", a4, t2, sinb, ALU.mult, [tm, cs], [tA])
                        K.tt("dve", d4[:, :, 0, :], d4[:, :, 0, :], a4, ALU.subtract, [dst, tA], [dst])
                        K.tt("dve", d4[:, :, 1, :], t2, cosb, ALU.mult, [tm, cs], [dst])
                        K.tt("dve", a4, t1, sinb, ALU.mult, [tm, cs], [tA])
                        K.tt("dve", d4[:, :, 1, :], d4[:, :, 1, :], a4, ALU.add, [dst, tA], [dst])
                    K.tt("dve", qr[:, :].rearrange("p (h e) -> p h e", h=4), qr[:, :].rearrange("p (h e) -> p h e", h=4),
                         C("xiq").unsqueeze(2).to_broadcast([128, 4, 128]), ALU.mult, [qr, CT], [qr])
                    for (src, dstT) in ((qr, qxT), (kr, krT)):
                        yield
                        bk = K.bank()
                        for h in range(4):
                            K.tr(bk[:, h * 128:(h + 1) * 128], src[:, h * 128:(h + 1) * 128], ident, [src, CT], [bk], inc=(h == 3))
                        K.cp("act", dstT[:, :, :], bk[:, :].rearrange("p (h e) -> p h e", h=4), [bk], [dstT])
                    K.cp("act", krb[:], kr[:], [kr], [krb])
                    K.cp("act", vb[:], tm[:, 1024:1536], [tm], [vb])
                    K.tt("dve", vz[:, :].rearrange("p (h e) -> p h e", h=4), tm[:, 1024:1536].rearrange("p (h e) -> p h e", h=4),
                         C("zeta").unsqueeze(2).to_broadcast([128, 4, 128]), ALU.mult, [tm, CT], [vz])
                    yield
                    bk = K.bank()
                    for h in range(4):
                        K.mm(bk[:, h * 128:(h + 1) * 128], krT[:, h, :], qxT[:, h, :], True, True, [krT, qxT], [bk], inc=(h == 3))
                    K.tt("dve", smT[:, :, :], bk[:, :].rearrange("p (h e) -> p h e", h=4),
                         C("maskP").rearrange("p (h e) -> p h e", h=4), ALU.mult, [bk, CT], [smT])
                    yield
                    by = K.bank()
                    for h in range(4):
                        K.mm(by[:, h * 128:(h + 1) * 128], smT[:, h, :], vb[:, h * 128:(h + 1) * 128], True, False, [smT, vb], [by], inc=False)
                        K.mm(by[:, h * 128:(h + 1) * 128], qxT[:, h, :], Srb[:, h * 128:(h + 1) * 128], False, True, [qxT, Srb], [by], inc=(h == 3))
                    yield
                    bs = K.bank()
                    for h in range(4):
                        K.mm(bs[:, h * 128:(h + 1) * 128], krb[:, h * 128:(h + 1) * 128], vz[:, h * 128:(h + 1) * 128], True, True, [krb, vz], [bs], inc=(h == 3))
                    for h in range(4):
                        K.stt("dve", Sr[:, h * 128:(h + 1) * 128], Sr[:, h * 128:(h + 1) * 128], RET_G128[h],
                              bs[:, h * 128:(h + 1) * 128], ALU.mult, ALU.add, [Sr, bs], [Sr])
                    K.cp("act", Srb[:], Sr[:], [Sr], [Srb])
                    K.act(tA[:], by[:, :], AF.Square, [by], [tA])
                    K.rsum(sm[:, 4:8], tA[:, :].rearrange("p (h e) -> p h e", h=4), [tA], [sm])
                    rstd_from_ss(sm[:, 4:8], sm[:, 4:8], sm, 128.0, EPS)
                    K.tt("dve", yo[:, :].rearrange("p (h e) -> p h e", h=4), by[:, :].rearrange("p (h e) -> p h e", h=4),
                         sm[:, 4:8].unsqueeze(2).to_broadcast([128, 4, 128]), ALU.mult, [by, sm], [yo])
                    silu_to(tA[:], tA, tm[:, 1536:2048], tm, qr[:], qr)
                    K.tt("dve", yo[:], yo[:], tA[:], ALU.mult, [yo, tA], [yo])
                    K.dma("sp", s_o[t % 2], yc[t * 128:(t + 1) * 128, 0:512], yo[:], R=[yo], W=[yc_b[t][0]])
                    yield

                pipeline2(A, B)
                K.barrier()

            for pst in _phase("ssd"):
                ph = {"st": pst}
                common_alloc(ph, l, None, None)
                NW = 1544
                WB = K.sb(pst, "WBs", [128, 8, NW], BF16)
                load_w(WB, WB, [w_in[l, k * 128:(k + 1) * 128, 2048:3592] for k in range(8)], 8)
                zts = [K.sb(pst, f"zt{i}", [128, 512], F32) for i in range(2)]
                cxs = [K.sb(pst, f"cxs{i}", [128, 8, 131], F32) for i in range(2)]
                xc_s = [K.sb(pst, "xc", [128, 8, 128], F32) for _p in range(2)]
                tmps_s = [[K.sb(pst, f"ctmp{i}", [128, 4, 128], F32) for i in range(3)] for _p in range(2)]
                xs_tm_s = [K.sb(pst, "xs_tm", [128, 512], F32) for _p in range(2)]
                bm_tm_s = [K.sb(pst, "bm_tm", [128, 256], BF16) for _p in range(2)]
                bcT_s = [K.sb(pst, "bcT", [128, 4, 128], BF16) for _p in range(2)]
                decT_s = [K.sb(pst, "decT", [128, 8, 128], F32) for _p in range(2)]
                GT_s = [K.sb(pst, "GT", [128, 8, 128], BF16) for _p in range(2)]
                xdt_s = [K.sb(pst, "xdt", [128, 512], BF16) for _p in range(2)]
                xdte_s = [K.sb(pst, "xdte", [128, 512], BF16) for _p in range(2)]
                t1_s = [K.sb(pst, "t1", [128, 512], F32) for _p in range(2)]
                t2_s = [K.sb(pst, "t2", [128, 512], F32) for _p in range(2)]
                Ss = K.sb(pst, "Ss", [128, 512], F32)
                Ssb = K.sb(pst, "Ssb", [128, 512], BF16)
                acsT_s = [K.sb(pst, "acsT", [8, 128], F32) for _p in range(2)]
                sml = ph["small"][0]
                sm = ph["sm"][0]
                K.op("dve", lambda h: h.memset(Ss[:], 0.0), [], [Ss])
                K.op("dve", lambda h: h.memset(Ssb[:], 0.0), [], [Ssb])
                for cx_ in cxs:
                    K.op("dve", lambda h: h.memset(cx_[:], 0.0), [], [cx_])
                K.act(sml[:, 16:24], ptm(ph, "alog")[:, 0:8], AF.Exp, [ph["ptm"]], [sml])
                K.ts("dve", sml[:, 16:24], sml[:, 16:24], -1.0, None, ALU.mult, None, [sml], [sml])
                scw = pfm(ph, "scw").rearrange("p (c j) -> p c j", c=8)
                K.cp("dve", ph["small"][1][:], ph["small"][0][:], [ph["small"][0]], [ph["small"][1]])
                def A(t):
                    par = t % 2
                    xc = xc_s[par]
                    tmps = tmps_s[par]
                    xs_tm = xs_tm_s[par]
                    bm_tm = bm_tm_s[par]
                    bcT = bcT_s[par]
                    decT = decT_s[par]
                    GT = GT_s[par]
                    xdt = xdt_s[par]
                    xdte = xdte_s[par]
                    t1 = t1_s[par]
                    t2 = t2_s[par]
                    acsT = acsT_s[par]
                    sm = ph["sm"][par]
                    sml = ph["small"][par]
                    zt = zts[par]
                    cx = cxs[par]
                    raw = ph["raw"][par]
                    xt = load_x(ph, l, t)
                    norm_hT(ph, xt, pfm(ph, "mixnw"), ph["hf"][par], ph["hT"][par], ph["smA"][par])
                    yield
                    proj_tm(ph["hT"][par], WB, 0, 512, zt[:], zt, "act")
                    yield
                    proj_tm(ph["hT"][par], WB, 1536, 8, raw[:, 0:8], raw, "dve")
                    yield
                    proj_fm(ph["hT"][par], WB, 512, 4, cx, 0)
                    yield
                    proj_fm(ph["hT"][par], WB, 1024, 4, cx, 4)
                    yield
                    yield

                def B(t):
                    par = t % 2
                    xc = xc_s[par]
                    tmps = tmps_s[par]
                    xs_tm = xs_tm_s[par]
                    bm_tm = bm_tm_s[par]
                    bcT = bcT_s[par]
                    decT = decT_s[par]
                    GT = GT_s[par]
                    xdt = xdt_s[par]
                    xdte = xdte_s[par]
                    t1 = t1_s[par]
                    t2 = t2_s[par]
                    acsT = acsT_s[par]
                    sm = ph["sm"][par]
                    sml = ph["small"][par]
                    zt = zts[par]
                    cx = cxs[par]
                    cxn = cxs[1 - par]
                    raw = ph["raw"][par]
                    yield from conv_silu(cx, cxn, scw, pfm(ph, "scb"), 8, xc, tmps)
                    K.tt("dve", sml[:, 0:8], raw[:, 0:8], ptm(ph, "bias16")[:, 0:8], ALU.add, [raw, ph["ptm"]], [sml])
                    softplus16(sml[:, 88:96], sml[:, 0:8], ph["sm2"][par], 8, sml)
                    K.tt("dve", sml[:, 24:32], sml[:, 88:96], sml[:, 16:24], ALU.mult, [sml], [sml])
                    yield
                    bk = K.bank()
                    K.mm(bk[:, 0:8], C("tri"), sml[:, 24:32], True, True, [CT, sml], [bk], inc=False)
                    K.mm(bk[:, 8:16], C("ones"), sml[:, 24:32], True, True, [CT, sml], [bk], inc=False)
                    K.mm(bk[0:8, 128:256], sml[:, 24:32], C("tri"), True, True, [CT, sml], [bk], inc=True)
                    K.cp("dve", sml[:, 32:40], bk[:, 0:8], [bk], [sml])
                    K.ts("dve", sml[:, 40:48], bk[:, 0:8], -1.0, None, ALU.mult, None, [bk], [sml])
                    K.cp("dve", sml[:, 56:64], bk[:, 8:16], [bk], [sml])
                    K.cp("act", acsT[:, :], bk[0:8, 128:256], [bk], [acsT])
                    K.act(sml[:, 48:56], sml[:, 32:40], AF.Exp, [sml], [sml])
                    K.tt("dve", sml[:, 64:72], sml[:, 56:64], sml[:, 32:40], ALU.subtract, [sml], [sml])
                    K.act(sml[:, 64:72], sml[:, 64:72], AF.Exp, [sml], [sml])
                    K.act(sml[:, 72:80], sml[:, 56:64], AF.Exp, [sml], [sml])
                    K.tt("dve", sml[:, 80:88], sml[:, 88:96], sml[:, 64:72], ALU.mult, [sml], [sml])
                    for hb in range(2):
                        yield
                        bk = K.bank()
                        for j in range(4):
                            h = hb * 4 + j
                            K.mm(bk[:, j * 128:(j + 1) * 128], ident, C("negssd"), True, False, [CT], [bk], inc=False)
                            K.mm(bk[:, j * 128:(j + 1) * 128], CT[0:8, C_OFF["sel8"][0] + h * 128:C_OFF["sel8"][0] + (h + 1) * 128],
                                 acsT[:, :], False, True, [CT, acsT], [bk], inc=(j == 3))
                        for j in range(4):
                            h = hb * 4 + j
                            K.act(decT[:, h, :], bk[:, j * 128:(j + 1) * 128], AF.Exp, [bk, sml], [decT], bias=sml[:, 40 + h:41 + h])
                    yield
                    bk = K.bank()
                    for c in range(4):
                        K.tr(bk[:, c * 128:(c + 1) * 128], xc[:, c, :], ident, [xc, CT], [bk], inc=(c == 3))
                    K.cp("act", xs_tm[:], bk[:, :], [bk], [xs_tm])
                    yield
                    bk = K.bank()
                    for g in range(2):
                        K.tr(bk[:, g * 128:(g + 1) * 128], xc[:, 4 + g, :], ident, [xc, CT], [bk], inc=(g == 1))
                    K.cp("act", bm_tm[:], bk[:, 0:256], [bk], [bm_tm])
                    K.cp("dve", bcT[:, :, :], xc[:, 4:8, :], [xc], [bcT])
                    yield
                    bk = K.bank()
                    for g in range(2):
                        K.mm(bk[:, g * 128:(g + 1) * 128], bcT[:, g, :], bcT[:, 2 + g, :], True, True, [bcT], [bk], inc=(g == 1))
                    for g in range(2):
                        K.tt("dve", GT[:, 4 * g:4 * g + 4, :], decT[:, 4 * g:4 * g + 4, :],
                             bk[:, g * 128:(g + 1) * 128].unsqueeze(1).to_broadcast([128, 4, 128]), ALU.mult, [decT, bk], [GT])
                    xs3 = xs_tm[:, :].rearrange("p (h e) -> p h e", h=8)
                    K.tt("dve", xdt[:, :].rearrange("p (h e) -> p h e", h=8), xs3,
                         sml[:, 88:96].unsqueeze(2).to_broadcast([128, 8, 64]), ALU.mult, [xs_tm, sml], [xdt])
                    K.tt("dve", xdte[:, :].rearrange("p (h e) -> p h e", h=8), xs3,
                         sml[:, 80:88].unsqueeze(2).to_broadcast([128, 8, 64]), ALU.mult, [xs_tm, sml], [xdte])
                    yield
                    by = K.bank()
                    for h in range(8):
                        K.mm(by[:, h * 64:(h + 1) * 64], GT[:, h, :], xdt[:, h * 64:(h + 1) * 64], True, True, [GT, xdt], [by], inc=(h == 7))
                    yield
                    bc = K.bank()
                    for g in range(2):
                        K.mm(bc[:, g * 256:(g + 1) * 256], bcT[:, 2 + g, :], Ssb[:, g * 256:(g + 1) * 256], True, True, [bcT, Ssb], [bc], inc=(g == 1))
                    yield
                    bn = K.bank()
                    for g in range(2):
                        K.mm(bn[:, g * 256:(g + 1) * 256], bm_tm[:, g * 128:(g + 1) * 128], xdte[:, g * 256:(g + 1) * 256], True, True, [bm_tm, xdte], [bn], inc=(g == 1))
                    K.tt("dve", t1[:, :].rearrange("p (h e) -> p h e", h=8), bc[:, :].rearrange("p (h e) -> p h e", h=8),
                         sml[:, 48:56].unsqueeze(2).to_broadcast([128, 8, 64]), ALU.mult, [bc, sml], [t1])
                    K.tt("dve", t1[:], t1[:], by[:, :], ALU.add, [t1, by], [t1])
                    K.tt("dve", t2[:, :].rearrange("p (h e) -> p h e", h=8), xs3,
                         ptm(ph, "dskip").unsqueeze(2).to_broadcast([128, 8, 64]), ALU.mult, [xs_tm, ph["ptm"]], [t2])
                    K.tt("dve", t1[:], t1[:], t2[:], ALU.add, [t1, t2], [t1])
                    K.tt("dve", Ss[:, :].rearrange("p (h e) -> p h e", h=8), Ss[:, :].rearrange("p (h e) -> p h e", h=8),
                         sml[:, 72:80].unsqueeze(2).to_broadcast([128, 8, 64]), ALU.mult, [Ss, sml], [Ss])
                    K.tt("dve", Ss[:], Ss[:], bn[:, :], ALU.add, [Ss, bn], [Ss])
                    K.cp("act", Ssb[:], Ss[:], [Ss], [Ssb])
                    silu_to(t2[:], t2, zt[:], zt, xs_tm[:], xs_tm)
                    K.tt("dve", t1[:], t1[:], t2[:], ALU.mult, [t1, t2], [t1])
                    K.act(t2[:], t1[:], AF.Square, [t1], [t2])
                    K.rsum(sm[:, 4:6], t2[:, :].rearrange("p (g e) -> p g e", g=2), [t2], [sm])
                    rstd_from_ss(sm[:, 4:6], sm[:, 4:6], sm, 256.0, EPS)
                    K.tt("dve", t1[:, :].rearrange("p (g e) -> p g e", g=2), t1[:, :].rearrange("p (g e) -> p g e", g=2),
                         sm[:, 4:6].unsqueeze(2).to_broadcast([128, 2, 256]), ALU.mult, [t1, sm], [t1])
                    K.dma("sp", s_o[t % 2], yc[t * 128:(t + 1) * 128, 512:1024], t1[:], R=[t1], W=[yc_b[t][1]])
                    yield

                pipeline(A, B, 4)
                K.barrier()

            for pst in _phase("gdn"):
                ph = {"st": pst}
                common_alloc(ph, l, None, None)
                NW = 2056
                WB = K.sb(pst, "WBg", [128, 8, NW], BF16)
                load_w(WB, WB, [w_in[l, k * 128:(k + 1) * 128, 3592:5648] for k in range(8)], 8)
                zts = [K.sb(pst, f"zt{i}", [128, 512], F32) for i in range(2)]
                cxs = [K.sb(pst, f"cxg{i}", [128, 12, 131], F32) for i in range(2)]
                gc_s = [K.sb(pst, "gc", [128, 12, 128], F32) for _p in range(2)]
                tmps_s = [[K.sb(pst, f"ctmp{i}", [128, 4, 128], F32) for i in range(3)] for _p in range(2)]
                qkv_s = [K.sb(pst, "qkv", [128, 1536], F32) for _p in range(2)]
                sq_s = [K.sb(pst, "sq", [128, 1024], F32) for _p in range(2)]
                kdm_s = [[K.sb(pst, f"kdm{i}", [128, 512], BF16) for i in range(2)] for _p in range(2)]
                vn_s = [[K.sb(pst, f"vn{i}", [128, 512], BF16) for i in range(2)] for _p in range(2)]
                vf_s = [K.sb(pst, "vf", [128, 512], BF16) for _p in range(2)]
                qnT_s = [K.sb(pst, "qnT", [128, 4, 128], BF16) for _p in range(2)]
                knT_s = [K.sb(pst, "knT", [128, 4, 128], BF16) for _p in range(2)]
                decL_s = [K.sb(pst, "decL", [128, 4, 128], F32) for _p in range(2)]
                decTg_s = [K.sb(pst, "decTg", [128, 4, 128], F32) for _p in range(2)]
                Aa_s = [[K.sb(pst, f"A{i}", [128, 4, 128], F32) for i in range(2)] for _p in range(2)]
                Bb_s = [[K.sb(pst, f"B{i}", [128, 4, 128], F32) for i in range(2)] for _p in range(2)]
                Pm_s = [K.sb(pst, "Pm", [128, 4, 128], F32) for _p in range(2)]
                TTb_s = [K.sb(pst, "TTb", [128, 4, 128], BF16) for _p in range(2)]
                vbt_s = [K.sb(pst, "vbt", [128, 512], BF16) for _p in range(2)]
                kbg_s = [K.sb(pst, "kbg", [128, 512], BF16) for _p in range(2)]
                uu_s = [K.sb(pst, "uu", [128, 512], F32) for _p in range(2)]
                wT_s = [K.sb(pst, "wT", [128, 4, 128], BF16) for _p in range(2)]
                attnT_s = [K.sb(pst, "attnT", [128, 4, 128], BF16) for _p in range(2)]
                otmp_s = [K.sb(pst, "otmp", [128, 512], F32) for _p in range(2)]
                oo_s = [K.sb(pst, "oo", [128, 512], F32) for _p in range(2)]
                Sg = K.sb(pst, "Sg", [128, 512], F32)
                Sgb = K.sb(pst, "Sgb", [128, 512], BF16)
                gcsT_s = [K.sb(pst, "gcsT", [8, 128], F32) for _p in range(2)]
                sml = ph["small"][0]
                sm = ph["sm"][0]
                K.op("dve", lambda h: h.memset(Sg[:], 0.0), [], [Sg])
                K.op("dve", lambda h: h.memset(Sgb[:], 0.0), [], [Sgb])
                for cx_ in cxs:
                    K.op("dve", lambda h: h.memset(cx_[:], 0.0), [], [cx_])
                K.op("dve", lambda h: h.memset(sml[:], 0.0), [], [sml])
                K.act(sml[:, 16:20], ptm(ph, "alog")[:, 8:12], AF.Exp, [ph["ptm"]], [sml])
                K.ts("dve", sml[:, 16:20], sml[:, 16:20], -1.0, None, ALU.mult, None, [sml], [sml])
                gcw = pfm(ph, "gcw").rearrange("p (c j) -> p c j", c=12)
                sel_o = C_OFF["sel8"][0]
                K.cp("dve", ph["small"][1][:], ph["small"][0][:], [ph["small"][0]], [ph["small"][1]])
                def A(t):
                    par = t % 2
                    gc = gc_s[par]
                    tmps = tmps_s[par]
                    qkv = qkv_s[par]
                    sq = sq_s[par]
                    kdm = kdm_s[par]
                    vn = vn_s[par]
                    vf = vf_s[par]
                    qnT = qnT_s[par]
                    knT = knT_s[par]
                    decL = decL_s[par]
                    decTg = decTg_s[par]
                    Aa = Aa_s[par]
                    Bb = Bb_s[par]
                    Pm = Pm_s[par]
                    TTb = TTb_s[par]
                    vbt = vbt_s[par]
                    kbg = kbg_s[par]
                    uu = uu_s[par]
                    wT = wT_s[par]
                    attnT = attnT_s[par]
                    otmp = otmp_s[par]
                    oo = oo_s[par]
                    gcsT = gcsT_s[par]
                    sm = ph["sm"][par]
                    sml = ph["small"][par]
                    zt = zts[par]
                    cx = cxs[par]
                    raw = ph["raw"][par]
                    xt = load_x(ph, l, t)
                    norm_hT(ph, xt, pfm(ph, "mixnw"), ph["hf"][par], ph["hT"][par], ph["smA"][par])
                    yield
                    proj_tm(ph["hT"][par], WB, 1536, 512, zt[:], zt, "act")
                    yield
                    proj_tm(ph["hT"][par], WB, 2048, 8, raw[:, 0:8], raw, "dve")
                    yield
                    for j0 in range(0, 12, 4):
                        proj_fm(ph["hT"][par], WB, j0 * 128, 4, cx, j0)
                        yield
                    yield

                def B(t):
                    par = t % 2
                    gc = gc_s[par]
                    tmps = tmps_s[par]
                    qkv = qkv_s[par]
                    sq = sq_s[par]
                    kdm = kdm_s[par]
                    vn = vn_s[par]
                    vf = vf_s[par]
                    qnT = qnT_s[par]
                    knT = knT_s[par]
                    decL = decL_s[par]
                    decTg = decTg_s[par]
                    Aa = Aa_s[par]
                    Bb = Bb_s[par]
                    Pm = Pm_s[par]
                    TTb = TTb_s[par]
                    vbt = vbt_s[par]
                    kbg = kbg_s[par]
                    uu = uu_s[par]
                    wT = wT_s[par]
                    attnT = attnT_s[par]
                    otmp = otmp_s[par]
                    oo = oo_s[par]
                    gcsT = gcsT_s[par]
                    sm = ph["sm"][par]
                    sml = ph["small"][par]
                    zt = zts[par]
                    cx = cxs[par]
                    cxn = cxs[1 - par]
                    raw = ph["raw"][par]
                    yield from conv_silu(cx, cxn, gcw, None, 12, gc, tmps)
                    if GSTAGE < 2:
                        return
                    for j0 in range(0, 12, 4):
                        yield
                        bk = K.bank()
                        for j in range(4):
                            K.tr(bk[:, j * 128:(j + 1) * 128], gc[:, j0 + j, :], ident, [gc, CT], [bk], inc=(j == 3))
                        K.cp("act" if j0 == 4 else "dve", qkv[:, j0 * 128:(j0 + 4) * 128], bk[:, :], [bk], [qkv])
                    K.act(sq[:], qkv[:, 0:1024], AF.Square, [qkv], [sq])
                    K.rsum(sml[:, 64:72], sq[:, :].rearrange("p (h e) -> p h e", h=8), [sq], [sml])
                    K.act(sml[:, 52:60], sml[:, 64:72], AF.Ln, [sml], [sml], bias=EPS)
                    K.act(sml[:, 52:60], sml[:, 52:60], AF.Exp, [sml], [sml], scale=-0.5)
                    K.ts("dve", sml[:, 52:56], sml[:, 52:56], 128.0 ** -0.5, None, ALU.mult, None, [sml], [sml])
                    K.tt("dve", qkv[:, 0:1024].rearrange("p (h e) -> p h e", h=8), qkv[:, 0:1024].rearrange("p (h e) -> p h e", h=8),
                         sml[:, 52:60].unsqueeze(2).to_broadcast([128, 8, 128]), ALU.mult, [qkv, sml], [qkv])
                    if GSTAGE < 3:
                        return
                    K.act(sml[:, 8:12], raw[:, 0:4], AF.Exp, [raw], [sml], scale=-1.0)
                    K.ts("dve", sml[:, 8:12], sml[:, 8:12], 1.0, None, ALU.add, None, [sml], [sml])
                    K.recip(sml[:, 8:12], sml[:, 8:12], [sml], [sml])
                    K.ts("dve", sml[:, 12:16], sml[:, 8:12], -1.0, None, ALU.mult, None, [sml], [sml])
                    K.tt("dve", sml[:, 4:8], raw[:, 4:8], ptm(ph, "bias16")[:, 12:16], ALU.add, [raw, ph["ptm"]], [sml])
                    softplus16(sml[:, 76:80], sml[:, 4:8], ph["sm2"][par], 4, sml)
                    K.tt("dve", sml[:, 20:24], sml[:, 76:80], sml[:, 16:20], ALU.mult, [sml], [sml])
                    yield
                    bk = K.bank()
                    K.mm(bk[:, 0:4], C("tribd"), sml[:, 20:24], True, True, [CT, sml], [bk], inc=False)
                    K.mm(bk[:, 4:8], C("blockones"), sml[:, 20:24], True, True, [CT, sml], [bk], inc=False)
                    K.mm(bk[:, 8:12], C("bs0"), sml[:, 20:24], True, True, [CT, sml], [bk], inc=False)
                    K.mm(bk[:, 12:16], C("bs1"), sml[:, 20:24], True, True, [CT, sml], [bk], inc=False)
                    K.mm(bk[0:8, 128:256], sml[:, 20:28], C("tribd"), True, True, [CT, sml], [bk], inc=True)
                    K.cp("dve", sml[:, 24:28], bk[:, 0:4], [bk], [sml])
                    K.ts("dve", sml[:, 28:32], bk[:, 0:4], -1.0, None, ALU.mult, None, [bk], [sml])
                    K.cp("dve", sml[:, 32:36], bk[:, 4:8], [bk], [sml])
                    K.act(sml[:, 44:52], bk[:, 8:16], AF.Exp, [bk], [sml])
                    K.cp("act", gcsT[:, :], bk[0:8, 128:256], [bk], [gcsT])
                    K.act(sml[:, 36:40], sml[:, 24:28], AF.Exp, [sml], [sml])
                    K.tt("dve", sml[:, 40:44], sml[:, 32:36], sml[:, 24:28], ALU.subtract, [sml], [sml])
                    K.act(sml[:, 40:44], sml[:, 40:44], AF.Exp, [sml], [sml])
                    K.tt("dve", sml[:, 60:64], sml[:, 8:12], sml[:, 36:40], ALU.mult, [sml], [sml])
                    if GSTAGE < 4:
                        return
                    qn3 = qkv[:, 0:512].rearrange("p (h e) -> p h e", h=4)
                    kn3 = qkv[:, 512:1024].rearrange("p (h e) -> p h e", h=4)
                    v3 = qkv[:, 1024:1536].rearrange("p (h e) -> p h e", h=4)
                    for i in range(2):
                        K.ts("dve", sml[:, 96 + 4 * i:100 + 4 * i], sml[:, 40:44], C("bs%d" % i)[:, 0:1], None, ALU.mult, None, [sml, CT], [sml])
                        K.tt("dve", kdm[i][:, :].rearrange("p (h e) -> p h e", h=4), kn3,
                             sml[:, 96 + 4 * i:100 + 4 * i].unsqueeze(2).to_broadcast([128, 4, 128]), ALU.mult, [qkv, sml], [kdm[i]])
                    K.tt("dve", kbg[:, :].rearrange("p (h e) -> p h e", h=4), kn3,
                         sml[:, 60:64].unsqueeze(2).to_broadcast([128, 4, 128]), ALU.mult, [qkv, sml], [kbg])
                    K.tt("dve", vbt[:, :].rearrange("p (h e) -> p h e", h=4), v3,
                         sml[:, 8:12].unsqueeze(2).to_broadcast([128, 4, 128]), ALU.mult, [qkv, sml], [vbt])
                    for (c0, dstT) in ((0, qnT), (512, knT)):
                        yield
                        bk = K.bank()
                        for h in range(4):
                            K.tr(bk[:, h * 128:(h + 1) * 128], qkv[:, c0 + h * 128:c0 + (h + 1) * 128], ident, [qkv, CT], [bk], inc=(h == 3))
                        K.cp("act", dstT[:, :, :], bk[:, :].rearrange("p (h e) -> p h e", h=4), [bk], [dstT])
                    if GSTAGE < 5:
                        return
                    for (msk, dst, scale, bcol) in (("posS", decL, -1.0, 24), ("negI", decTg, 1.0, 28)):
                        yield
                        bk = K.bank()
                        for h in range(4):
                            K.mm(bk[:, h * 128:(h + 1) * 128], ident, C(msk), True, False, [CT], [bk], inc=False)
                            K.mm(bk[:, h * 128:(h + 1) * 128], CT[0:8, sel_o + h * 128:sel_o + (h + 1) * 128], gcsT[:, :],
                                 False, True, [CT, gcsT], [bk], inc=(h == 3))
                        for h in range(4):
                            K.act(dst[:, h, :], bk[:, h * 128:(h + 1) * 128], AF.Exp, [bk, sml], [dst],
                                  scale=scale, bias=sml[:, bcol + h:bcol + h + 1])
                    if GSTAGE < 6:
                        return
                    yield
                    bk = K.bank()
                    for h in range(4):
                        K.mm(bk[:, h * 128:(h + 1) * 128], knT[:, h, :], knT[:, h, :], True, True, [knT], [bk], inc=(h == 3))
                    if GSTAGE == 60:
                        return
                    for h in range(4):
                        K.stt("dve", Bb[0][:, h, :], bk[:, h * 128:(h + 1) * 128], sml[:, 12 + h:13 + h], decL[:, h, :],
                              ALU.mult, ALU.mult, [bk, sml, decL], [Bb[0]])
                    if GSTAGE == 61:
                        return
                    yield
                    bk = K.bank()
                    for h in range(4):
                        K.tr(bk[:, h * 128:(h + 1) * 128], Bb[0][:, h, :], ident, [Bb[0], CT], [bk], inc=(h == 3))
                    if GSTAGE == 62:
                        return
                    K.cp("act", Aa[0][:, :, :], bk[:, :].rearrange("p (h e) -> p h e", h=4), [bk], [Aa[0]])
                    if GSTAGE == 63:
                        return
                    for h in range(4):
                        K.tt("dve", Pm[:, h, :], Aa[0][:, h, :], ident, ALU.add, [Aa[0], CT], [Pm])
                    if GSTAGE < 7:
                        return
                    cur = 0
                    for lev in range(1, 6):
                        nxt = 1 - cur
                        if lev <= 4:
                            yield
                            bk = K.bank()
                            for h in range(4):
                                K.mm(bk[:, h * 128:(h + 1) * 128], Bb[cur][:, h, :], Aa[cur][:, h, :], True, True, [Bb[cur], Aa[cur]], [bk], inc=(h == 3))
                            K.cp("act", Aa[nxt][:, :, :], bk[:, :].rearrange("p (h e) -> p h e", h=4), [bk], [Aa[nxt]])
                        yield
                        bk = K.bank()
                        for h in range(4):
                            K.mm(bk[:, h * 128:(h + 1) * 128], Aa[cur][:, h, :], Bb[cur][:, h, :], True, True, [Bb[cur], Aa[cur]], [bk], inc=(h == 3))
                        K.cp("dve", Bb[nxt][:, :, :], bk[:, :].rearrange("p (h e) -> p h e", h=4), [bk], [Bb[nxt]])
                        yield
                        bk = K.bank()
                        for h in range(4):
                            K.mm(bk[:, h * 128:(h + 1) * 128], Bb[nxt][:, h, :], Pm[:, h, :], True, True, [Bb[nxt], Pm], [bk], inc=(h == 3))
                        K.tt("dve", Pm[:, :, :], Pm[:, :, :], bk[:, :].rearrange("p (h e) -> p h e", h=4), ALU.add, [Pm, bk], [Pm])
                        cur = nxt
                    if GSTAGE < 8:
                        return
                    K.cp("act", TTb[:, :, :], Pm[:, :, :], [Pm], [TTb])
                    yield
                    bk = K.bank()
                    for h in range(4):
                        K.mm(bk[:, h * 128:(h + 1) * 128], TTb[:, h, :], vbt[:, h * 128:(h + 1) * 128], True, True, [TTb, vbt], [bk], inc=(h == 3))
                    K.cp("act", uu[:], bk[:, :], [bk], [uu])
                    yield
                    bk = K.bank()
                    for h in range(4):
                        K.mm(bk[:, h * 128:(h + 1) * 128], kbg[:, h * 128:(h + 1) * 128], TTb[:, h, :], True, True, [TTb, kbg], [bk], inc=(h == 3))
                    K.cp("act", wT[:, :, :], bk[:, :].rearrange("p (h e) -> p h e", h=4), [bk], [wT])
                    yield
                    bk = K.bank()
                    for h in range(4):
                        K.mm(bk[:, h * 128:(h + 1) * 128], knT[:, h, :], qnT[:, h, :], True, True, [knT, qnT], [bk], inc=(h == 3))
                    K.tt("dve", attnT[:, :, :], bk[:, :].rearrange("p (h e) -> p h e", h=4), decTg[:, :, :], ALU.mult, [bk, decTg], [attnT])
                    if GSTAGE < 9:
                        return
                    for hf in range(2):
                        yield
                        bk = K.bank()
                        for h in range(4):
                            K.mm(bk[:, h * 128:(h + 1) * 128], wT[:, h, :], Sgb[:, h * 128:(h + 1) * 128], True, True, [wT, Sgb], [bk], inc=(h == 3))
                        K.tt("dve", vn[hf][:], uu[:], bk[:, :], ALU.subtract, [uu, bk], [vn[hf]])
                        yield
                        bk = K.bank()
                        for h in range(4):
                            K.mm(bk[:, h * 128:(h + 1) * 128], qnT[:, h, :], Sgb[:, h * 128:(h + 1) * 128], True, True,
                                 [qnT, Sgb], [bk], inc=(h == 3))
                        K.ts("dve", sml[:, 104 + 4 * hf:108 + 4 * hf], sml[:, 36:40], C("bs%d" % hf)[:, 0:1], None, ALU.mult, None, [sml, CT], [sml])
                        dst = otmp if hf == 0 else uu
                        K.tt("dve", dst[:, :].rearrange("p (h e) -> p h e", h=4), bk[:, :].rearrange("p (h e) -> p h e", h=4),
                             sml[:, 104 + 4 * hf:108 + 4 * hf].unsqueeze(2).to_broadcast([128, 4, 128]), ALU.mult, [bk, sml], [dst])
                        if hf == 1:
                            K.tt("dve", otmp[:], otmp[:], uu[:], ALU.add, [otmp, uu], [otmp])
                        yield
                        bk = K.bank()
                        for h in range(4):
                            K.mm(bk[:, h * 128:(h + 1) * 128], kdm[hf][:, h * 128:(h + 1) * 128], vn[hf][:, h * 128:(h + 1) * 128],
                                 True, True, [kdm[hf], vn[hf]], [bk], inc=(h == 3))
                        K.tt("dve", Sg[:, :].rearrange("p (h e) -> p h e", h=4), Sg[:, :].rearrange("p (h e) -> p h e", h=4),
                             sml[:, 44 + 4 * hf:48 + 4 * hf].unsqueeze(2).to_broadcast([128, 4, 128]), ALU.mult, [Sg, sml], [Sg])
                        K.tt("dve", Sg[:], Sg[:], bk[:, :], ALU.add, [Sg, bk], [Sg])
                        K.cp("act", Sgb[:], Sg[:], [Sg], [Sgb])
                    K.ts("dve", vf[:], vn[0][:], C("bs0")[:, 0:1], None, ALU.mult, None, [vn[0], CT], [vf])
                    K.stt("dve", vf[:], vn[1][:], C("bs1")[:, 0:1], vf[:], ALU.mult, ALU.add, [vn[1], CT, vf], [vf])
                    yield
                    bk = K.bank()
                    for h in range(4):
                        K.mm(bk[:, h * 128:(h + 1) * 128], attnT[:, h, :], vf[:, h * 128:(h + 1) * 128], True, True, [attnT, vf], [bk], inc=(h == 3))
                    K.tt("dve", oo[:], otmp[:], bk[:, :], ALU.add, [otmp, bk], [oo])
                    K.act(otmp[:], oo[:], AF.Square, [oo], [otmp])
                    K.rsum(sm[:, 4:8], otmp[:, :].rearrange("p (h e) -> p h e", h=4), [otmp], [sm])
                    rstd_from_ss(sm[:, 4:8], sm[:, 4:8], sm, 128.0, EPS)
                    K.tt("dve", oo[:, :].rearrange("p (h e) -> p h e", h=4), oo[:, :].rearrange("p (h e) -> p h e", h=4),
                         sm[:, 4:8].unsqueeze(2).to_broadcast([128, 4, 128]), ALU.mult, [oo, sm], [oo])
                    silu_to(otmp[:], otmp, zt[:], zt, uu[:], uu)
                    K.tt("dve", oo[:], oo[:], otmp[:], ALU.mult, [oo, otmp], [oo])
                    K.dma("sp", s_o[t % 2], yc[t * 128:(t + 1) * 128, 1024:1536], oo[:], R=[oo], W=[yc_b[t][2]])
                    yield

                pipeline2(A, B)
                K.barrier()

            for pst in _phase("out"):
                ph = {"st": pst}
                common_alloc(ph, l, None, None)
                WB = K.sb(pst, "WBo", [128, 12, 1024], BF16)
                load_w(WB, WB, [w_out[l, k * 128:(k + 1) * 128, :] for k in range(12)], 12)
                yt = [K.sb(pst, f"yt{i}", [128, MIXW], F32) for i in range(2)]
                yTs = [K.sb(pst, f"yT{i}", [128, 12, 128], BF16) for i in range(2)]
                xo = [K.sb(pst, f"xo{i}", [128, D], F32) for i in range(2)]
                ynw = pfm(ph, "ynw")
                def A(t):
                    par = t % 2
                    yT = yTs[par]
                    xt = load_x(ph, l, t)
                    ytt = yt[t % 2]
                    K.dma("sp", s_a[t % 2], ytt[:], yc[t * 128:(t + 1) * 128, :], R=yc_b[t], W=[ytt])
                    for j0 in range(0, 12, 4):
                        bk = K.bank()
                        for j in range(4):
                            K.tr(bk[:, j * 128:(j + 1) * 128], ytt[:, (j0 + j) * 128:(j0 + j + 1) * 128], ident, [ytt, CT], [bk], inc=(j == 3))
                        K.tt("dve", yT[:, j0:j0 + 4, :], bk[:, :].rearrange("p (c e) -> p c e", c=4),
                             ynw[:, j0:j0 + 4].unsqueeze(2).to_broadcast([128, 4, 128]), ALU.mult, [bk, ph["pfm"]], [yT])
                    yield

                def B(t):
                    par = t % 2
                    yT = yTs[par]
                    xt = ph["xt"][par]
                    xot = xo[t % 2]
                    for n in range(2):
                        yield
                        bk = K.bank()
                        for k in range(12):
                            K.mm(bk[:, :], yT[:, k, :], WB[:, k, n * 512:(n + 1) * 512], k == 0, k == 11, [yT, WB], [bk], inc=(k == 11))
                        K.tt("dve", xot[:, n * 512:(n + 1) * 512], xt[:, n * 512:(n + 1) * 512], bk[:, :], ALU.add, [xt, bk], [xot])
                    K.dma("sp", s_o[t % 2], xb[t * 128:(t + 1) * 128, :], xot[:], R=[xot], W=[xb_b[t]])
                    yield

                pipeline(A, B, 1)
                K.barrier()

            for pst in _phase("mlp"):
                ph = {"st": pst}
                common_alloc(ph, l, None, None)
                WU = K.sb(pst, "WU", [128, 8, DFF], BF16)
                WD = K.sb(pst, "WD", [128, 32, D], BF16)
                load_w(WU, WU, [w_up[l, k * 128:(k + 1) * 128, :] for k in range(8)], 8)
                load_w(WD, WD, [w_down[l, k * 128:(k + 1) * 128, :] for k in range(32)], 32, s_w2)
                aT = K.sb(pst, "aT", [128, 32, 128], BF16)
                rl = K.sb(pst, "rl", [128, 512], F32)
                xo = [K.sb(pst, f"xo{i}", [128, D], F32) for i in range(2)]
                def A(t):
                    par = t % 2
                    xt = ph["xt"][t % 2]
                    K.dma("sp", s_x[t % 2], xt[:], xb[t * 128:(t + 1) * 128, :], R=[xb_b[t]], W=[xt])
                    norm_hT(ph, xt, pfm(ph, "mlpnw"), ph["hf"][par], ph["hT"][par], ph["smA"][par])
                    yield
                    yield

                def B(t):
                    par = t % 2
                    xt = ph["xt"][par]
                    for f0 in range(0, 32, 4):
                        yield
                        bk = K.bank()
                        for j in range(4):
                            f = f0 + j
                            for k in range(8):
                                K.mm(bk[:, j * 128:(j + 1) * 128], WU[:, k, f * 128:(f + 1) * 128], ph["hT"][par][:, k, :],
                                     k == 0, k == 7, [ph["hT"][par], WU], [bk], inc=(j == 3 and k == 7))
                        K.act(rl[:], bk[:, :], AF.Relu, [bk], [rl])
                        K.tt("dve", aT[:, f0:f0 + 4, :], rl[:, :].rearrange("p (c e) -> p c e", c=4),
                             rl[:, :].rearrange("p (c e) -> p c e", c=4), ALU.mult, [rl], [aT])
                    xot = xo[t % 2]
                    for n in range(2):
                        yield
                        bk = K.bank()
                        for k in range(32):
                            K.mm(bk[:, :], aT[:, k, :], WD[:, k, n * 512:(n + 1) * 512], k == 0, k == 31, [aT, WD], [bk], inc=(k == 31))
                        K.tt("dve", xot[:, n * 512:(n + 1) * 512], xt[:, n * 512:(n + 1) * 512], bk[:, :], ALU.add, [xt, bk], [xot])
                    K.dma("sp", s_o[t % 2], xb[t * 128:(t + 1) * 128, :], xot[:], R=[xot], W=[xb_b[t]])
                    yield

                pipeline(A, B, 3)
                K.barrier()

        for pst in _phase("final"):
            fnw = K.sb(pst, "fnw", [128, D], F32)
            K.dma("sp", s_pf, fnw[:], fnw_in, W=[fnw])
            xts = [K.sb(pst, f"fx{i}", [128, D], F32) for i in range(2)]
            hf2 = [K.sb(pst, f"fh{i}", [128, D], F32) for i in range(2)]
            sm = K.sb(pst, "fsm", [128, 8], F32)
            for t in range(NT):
                xt = xts[t % 2]
                hf = hf2[t % 2]
                K.dma("sp", s_x[t % 2], xt[:], xb[t * 128:(t + 1) * 128, :], R=[xb_b[t]], W=[xt])
                K.act(hf[:], xt[:], AF.Square, [xt], [hf, sm], accum_out=sm[:, 0:1])
                K.act(sm[:, 1:2], sm[:, 0:1], AF.Ln, [sm], [sm], scale=1.0 / D, bias=EPS)
                K.act(sm[:, 2:3], sm[:, 1:2], AF.Exp, [sm], [sm], scale=-0.5)
                K.stt("dve", hf[:], xt[:], sm[:, 2:3], fnw[:], ALU.mult, ALU.mult, [xt, sm, fnw], [hf])
                K.dma("sp", s_o[t % 2], y_out[t * 128:(t + 1) * 128, :], hf[:], R=[hf], W=[y_b[t]])
            K.barrier()
        build_program.stats = dict(K.ninst, nsem=K.nsem)
    return nc


def _run(inp, NT, DEPTH, debug=False, ncores=8):
    B = inp["x"].shape[0]
    pf, pt = _pack_params(inp, DEPTH)
    fnw = np.ascontiguousarray(np.broadcast_to(np.asarray(inp["final_norm_w"], np.float32)[None, :], (128, D)))
    nc = build_program(NT, DEPTH, debug)
    in_maps = []
    for c in range(ncores):
        b = c % B
        in_maps.append({
            "x": np.ascontiguousarray(inp["x"][b], dtype=np.float32),
            "pos": np.ascontiguousarray(np.asarray(inp["positions"][b], np.int32).reshape(NT, 128).T),
            "w_in": np.ascontiguousarray(inp["w_in"][:DEPTH], dtype=np.float32),
            "w_out": np.ascontiguousarray(inp["w_out"][:DEPTH], dtype=np.float32),
            "w_up": np.ascontiguousarray(inp["w_up"][:DEPTH], dtype=np.float32),
            "w_down": np.ascontiguousarray(inp["w_down"][:DEPTH], dtype=np.float32),
            "pf": pf, "pt": pt, "fnw": fnw, "consts": CONST_TABLE,
        })
    res = run_bass_kernel_spmd(nc, in_maps, core_ids=list(range(ncores)))
    return res


def kernel(**inputs):
    inp = {k: np.asarray(v) for k, v in inputs.items()}
    B, S, _ = inp["x"].shape
    res = _run(inp, S // 128, inp["w_in"].shape[0])
    out = np.stack([np.asarray(res.results[b]["y"], dtype=np.float32) for b in range(B)], axis=0)
    return out
```

```python
import math
from contextlib import ExitStack

import numpy as np
import concourse.bass as bass
import concourse.mybir as mybir
from concourse.bass_utils import run_bass_kernel_spmd

F32 = mybir.dt.float32
BF16 = mybir.dt.bfloat16
I32 = mybir.dt.int32
ALU = mybir.AluOpType
AF = mybir.ActivationFunctionType
AX = mybir.AxisListType

D = 1024
DIN = 5648
DFF = 4096
MIXW = 1536
EPS = 1e-6
EPOCH_LEN = 12000
NEG = -30000.0


class Buf:
    def __init__(self, name):
        self.name = name
        self.w = None
        self.r = []


class Prod:
    def __init__(self, K, name, handle):
        self.K = K
        self.name = name
        self.h = handle
        self.sems = []
        self.epoch = -1
        self.cnt = 0
        self.pending = False
        self.new_epoch()

    def new_epoch(self):
        self.sems.append(self.K.new_sem(f"{self.name}_e{len(self.sems)}"))
        self.epoch += 1
        self.cnt = 0


class T:
    def __init__(self, t, name):
        self.t = t
        self.b = Buf(name)

    def __getitem__(self, idx):
        return self.t[idx]


def _bufs(lst):
    out = []
    for x in lst:
        if isinstance(x, (list, tuple)):
            out.extend(_bufs(x))
        elif isinstance(x, T):
            out.append(x.b)
        else:
            out.append(x)
    return out


class Kern:
    def __init__(self, nc, stack):
        self.nc = nc
        self.stack = stack
        self.nsem = 0
        self.prods = {}
        for nm, h in (("pe", nc.tensor), ("act", nc.scalar), ("dve", nc.vector),
                      ("pool", nc.gpsimd), ("sp", nc.sync)):
            self.prods[nm] = Prod(self, nm, h)
        self.engines = ["pe", "act", "dve", "pool", "sp"]
        self.waited = {}
        self.ninst = {k: 0 for k in self.engines}
        self.banks = []
        self.cur_pool = 0
        self.pool_i = [0, 0]
        self.cost = 0.0

    def new_sem(self, name):
        self.nsem += 1
        return self.stack.enter_context(self.nc.semaphore(name))

    def dma_slot(self, name):
        p = Prod(self, "dma_" + name, None)
        self.prods[p.name] = p
        return p

    def sb(self, stack, name, shape, dtype):
        self.nsb = getattr(self, "nsb", 0) + 1
        name = f"s{self.nsb}_{name}"
        return T(stack.enter_context(self.nc.sbuf_tensor(name, list(shape), dtype)), name)

    def make_banks(self):
        for i in range(8):
            t = self.stack.enter_context(self.nc.psum_tensor(f"bank{i}", [128, 512], F32))
            self.banks.append(T(t, f"bank{i}"))

    def bank(self):
        p = self.cur_pool
        b = self.banks[4 * p + self.pool_i[p] % 4]
        self.pool_i[p] += 1
        return b

    def _need(self, eng, reads, writes):
        need = {}

        def add(tok):
            if tok is None:
                return
            p, ep, c = tok
            if p is eng and eng.name == "pe":
                return
            key = (p.name, ep)
            if need.get(key, (None, 0))[1] < c:
                need[key] = (p, c)

        for b in reads:
            add(b.w)
        for b in writes:
            add(b.w)
            for t in b.r:
                add(t)
        for (pname, ep), (p, c) in need.items():
            wk = (eng.name, pname, ep)
            if self.waited.get(wk, 0) >= c:
                continue
            if ep == p.epoch:
                assert c <= p.cnt, f"wait on un-issued inc: {eng.name} waits {pname} {c}>{p.cnt}"
            eng.h.wait_ge(p.sems[ep], c)
            self.waited[wk] = c

    def _mark(self, tok, reads, writes):
        for b in reads:
            b.r.append(tok)
            if len(b.r) > 48:
                best = {}
                for (p, ep, c) in b.r:
                    k = (p.name, ep)
                    if k not in best or best[k][2] < c:
                        best[k] = (p, ep, c)
                b.r = list(best.values())
        for b in writes:
            b.w = tok
            b.r = []

    def op(self, engname, fn, R=(), W=(), inc=True):
        reads, writes = _bufs(R), _bufs(W)
        eng = self.prods[engname]
        self._need(eng, reads, writes)
        inst = fn(eng.h)
        self.ninst[engname] += 1
        self.cost += COST.get(engname, 0.0)
        if inc:
            if eng.cnt + 1 > EPOCH_LEN:
                assert not eng.pending
                eng.new_epoch()
            eng.cnt += 1
            inst.then_inc(eng.sems[eng.epoch], 1)
            eng.pending = False
            tok = (eng, eng.epoch, eng.cnt)
        else:
            if eng.cnt + 1 > EPOCH_LEN and not eng.pending:
                eng.new_epoch()
            eng.pending = True
            tok = (eng, eng.epoch, eng.cnt + 1)
        self._mark(tok, reads, writes)
        return inst

    def dma(self, qname, slot, out, in_, R=(), W=(), **kw):
        reads, writes = _bufs(R), _bufs(W)
        q = self.prods[qname]
        self._need(q, reads, writes)
        inst = q.h.dma_start(out=out, in_=in_, **kw)
        self.ninst[qname] += 1
        if slot.cnt + 16 > EPOCH_LEN:
            slot.new_epoch()
        slot.cnt += 16
        inst.then_inc(slot.sems[slot.epoch], 16)
        tok = (slot, slot.epoch, slot.cnt)
        self._mark(tok, reads, writes)
        return inst

    def barrier(self):
        for en in self.engines:
            e = self.prods[en]
            for p in self.prods.values():
                assert not p.pending
                if p.cnt == 0:
                    continue
                wk = (e.name, p.name, p.epoch)
                if self.waited.get(wk, 0) >= p.cnt:
                    continue
                e.h.wait_ge(p.sems[p.epoch], p.cnt)
                self.waited[wk] = p.cnt

    def tt(self, eng, out, in0, in1, op, R, W):
        return self.op(eng, lambda h: h.tensor_tensor(out=out, in0=in0, in1=in1, op=op), R, W)

    def ts(self, eng, out, in0, s1, s2, op0, op1, R, W):
        if s2 is None:
            return self.op(eng, lambda h: h.tensor_scalar(out=out, in0=in0, scalar1=s1, scalar2=None, op0=op0), R, W)
        return self.op(eng, lambda h: h.tensor_scalar(out=out, in0=in0, scalar1=s1, scalar2=s2, op0=op0, op1=op1), R, W)

    def stt(self, eng, out, in0, scalar, in1, op0, op1, R, W):
        return self.op(eng, lambda h: h.scalar_tensor_tensor(out=out, in0=in0, scalar=scalar, in1=in1,
                                                             op0=op0, op1=op1), R, W)

    def act(self, out, in_, func, R, W, **kw):
        return self.op("act", lambda h: h.activation(out=out, in_=in_, func=func, **kw), R, W)

    def cp(self, eng, out, in_, R, W):
        if eng == "act":
            return self.op("act", lambda h: h.copy(out=out, in_=in_), R, W)
        return self.op(eng, lambda h: h.tensor_copy(out=out, in_=in_), R, W)

    def mm(self, out, lhsT, rhs, start, stop, R, W, inc):
        return self.op("pe", lambda h: h.matmul(out, lhsT=lhsT, rhs=rhs, start=start, stop=stop), R, W, inc=inc)

    def tr(self, out, in_, ident, R, W, inc):
        return self.op("pe", lambda h: h.transpose(out=out, in_=in_, identity=ident), R, W, inc=inc)

    def rsum(self, out, in_, R, W):
        return self.op("dve", lambda h: h.tensor_reduce(out=out, in_=in_, axis=AX.X, op=ALU.add), R, W)

    def recip(self, out, in_, R, W):
        return self.op("dve", lambda h: h.reciprocal(out=out, in_=in_), R, W)


C_OFF = {}


def _const_table():
    cols = []

    def add(name, arr):
        arr = np.asarray(arr, np.float32)
        full = np.zeros((128, arr.shape[1]), np.float32)
        full[:arr.shape[0]] = arr
        C_OFF[name] = (sum(c.shape[1] for c in cols), arr.shape[1])
        cols.append(full)

    i = np.arange(128)
    m, l = i[:, None], i[None, :]
    same = (m // 64) == (l // 64)
    add("ident", np.eye(128))
    add("tri", (m <= l))
    add("tribd", (m <= l) & same)
    add("ones", np.ones((128, 128)))
    add("blockones", same)
    add("bs0", np.broadcast_to(m < 64, (128, 128)))
    add("bs1", np.broadcast_to(m >= 64, (128, 128)))
    add("negssd", np.where(l >= m, 0.0, NEG))
    add("negI", np.where((l >= m) & same, 0.0, NEG))
    add("posS", np.where((m > l) & same, 0.0, -NEG))
    sel8 = np.zeros((8, 8 * 128))
    for h in range(8):
        sel8[h, h * 128:(h + 1) * 128] = 1.0
    add("sel8", sel8)
    gam = 1.0 - np.exp2(-5.0 - np.arange(4))
    lg = np.log(gam)
    maskP = np.zeros((128, 512))
    for h in range(4):
        maskP[:, h * 128:(h + 1) * 128] = np.where(l >= m, np.exp(-lg[h] * (m + 1.0)), 0.0)
    add("maskP", maskP)
    add("xiq", np.exp(lg[None, :] * (i[:, None] + 1.0)) * (128.0 ** -0.5))
    add("zeta", np.exp(lg[None, :] * (127.0 - i[:, None])))
    invf = 10000.0 ** (-np.arange(64, dtype=np.float32) / 64.0)
    add("invf", np.broadcast_to(invf[None, :].astype(np.float32), (128, 64)))
    return np.concatenate(cols, axis=1), [float(g ** 128) for g in gam]


CONST_TABLE, RET_G128 = _const_table()
NCONST = CONST_TABLE.shape[1]

PF = {"mixnw": (0, 8), "mlpnw": (8, 8), "ynw": (16, 12), "scw": (28, 32), "scb": (60, 8), "gcw": (68, 48)}
NPF = 116
PT = {"bias16": (0, 16), "alog": (16, 12), "dskip": (28, 8)}
NPT = 36


def _pack_params(inp, depth):
    pf = np.zeros((depth, 128, NPF), np.float32)
    pt = np.zeros((depth, 128, NPT), np.float32)
    for l in range(depth):
        pf[l, :, 0:8] = inp["mix_norm_w"][l].reshape(8, 128).T
        pf[l, :, 8:16] = inp["mlp_norm_w"][l].reshape(8, 128).T
        pf[l, :, 16:20] = inp["ret_norm_w"][l].reshape(4, 128).T
        pf[l, :, 20:24] = inp["ssd_norm_w"][l].reshape(4, 128).T
        pf[l, :, 24:28] = np.repeat(inp["gdn_norm_w"][l].reshape(128, 1), 4, axis=1)
        pf[l, :, 28:60] = inp["ssd_conv_w"][l].reshape(4, 8, 128).transpose(2, 1, 0).reshape(128, 32)
        pf[l, :, 60:68] = inp["ssd_conv_b"][l].reshape(8, 128).T
        pf[l, :, 68:116] = inp["gdn_conv_w"][l].reshape(4, 12, 128).transpose(2, 1, 0).reshape(128, 48)
        pt[l, :, 0:8] = inp["ssd_dt_bias"][l][None, :]
        pt[l, :, 12:16] = inp["gdn_dt_bias"][l][None, :]
        pt[l, :, 16:24] = inp["ssd_a_log"][l][None, :]
        pt[l, :, 24:28] = inp["gdn_a_log"][l][None, :]
        pt[l, :, 28:36] = inp["ssd_d"][l][None, :]
    return pf, pt


GSTAGE = 99
PIPE_FRAC = 0.5
COST = {"pe": 0.15, "act": 0.6, "dve": 0.6, "pool": 1.0, "sp": 0.0}
MAXSTEP = 10 ** 9
CONV_ENG = "pool"
PHASES = {"init", "ret", "ssd", "gdn", "out", "mlp", "final"}


def _phase(name):
    if name in PHASES:
        with ExitStack() as s:
            yield s


def build_program(NT, DEPTH, debug=False):
    S = NT * 128
    nc = bass.Bass("TRN2", target_bir_lowering=False)
    x_in = nc.dram_tensor("x", [S, D], F32, kind="ExternalInput").ap()
    pos_in = nc.dram_tensor("pos", [128, NT], I32, kind="ExternalInput").ap()
    w_in = nc.dram_tensor("w_in", [DEPTH, D, DIN], F32, kind="ExternalInput").ap()
    w_out = nc.dram_tensor("w_out", [DEPTH, MIXW, D], F32, kind="ExternalInput").ap()
    w_up = nc.dram_tensor("w_up", [DEPTH, D, DFF], F32, kind="ExternalInput").ap()
    w_down = nc.dram_tensor("w_down", [DEPTH, DFF, D], F32, kind="ExternalInput").ap()
    pf_in = nc.dram_tensor("pf", [DEPTH, 128, NPF], F32, kind="ExternalInput").ap()
    pt_in = nc.dram_tensor("pt", [DEPTH, 128, NPT], F32, kind="ExternalInput").ap()
    fnw_in = nc.dram_tensor("fnw", [128, D], F32, kind="ExternalInput").ap()
    const_in = nc.dram_tensor("consts", [128, NCONST], F32, kind="ExternalInput").ap()
    y_out = nc.dram_tensor("y", [S, D], F32, kind="ExternalOutput").ap()
    dk = "ExternalOutput" if debug else "Internal"
    xb = nc.dram_tensor("xb", [S, D], F32, kind=dk).ap()
    yc = nc.dram_tensor("yc", [S, MIXW], F32, kind=dk).ap()
    csd = nc.dram_tensor("csd", [128, NT, 128], F32, kind="Internal").ap()

    with ExitStack() as st:
        st.enter_context(nc.allow_low_precision("bf16 matmul operands, fp32 accumulation"))
        K = Kern(nc, st)
        K.make_banks()
        xb_b = [Buf(f"xb{t}") for t in range(NT)]
        yc_b = [[Buf(f"yc{t}_{j}") for j in range(3)] for t in range(NT)]
        y_b = [Buf(f"y{t}") for t in range(NT)]
        csd_b = Buf("csd")
        s_c = K.dma_slot("c")
        s_pf = K.dma_slot("pf")
        s_pt = K.dma_slot("pt")
        s_cs = K.dma_slot("cs")
        s_w = K.dma_slot("w")
        s_w2 = K.dma_slot("w2")
        s_x = [K.dma_slot(f"x{i}") for i in range(2)]
        s_a = [K.dma_slot(f"a{i}") for i in range(2)]
        s_o = [K.dma_slot(f"o{i}") for i in range(2)]

        CT = K.sb(st, "consts", [128, NCONST], F32)
        K.dma("sp", s_c, CT[:], const_in, W=[CT])

        def C(name, rows=128):
            o, n = C_OFF[name]
            return CT[0:rows, o:o + n]

        ident = C("ident")

        def xsrc(l, t):
            if l == 0:
                return x_in[t * 128:(t + 1) * 128, :], []
            return xb[t * 128:(t + 1) * 128, :], [xb_b[t]]

        def norm_hT(ph, xt, nwcols, hf, hT, sm):
            K.act(hf[:], xt[:], AF.Square, [xt], [hf, sm], accum_out=sm[:, 0:1])
            K.act(sm[:, 1:2], sm[:, 0:1], AF.Ln, [sm], [sm], scale=1.0 / D, bias=EPS)
            K.act(sm[:, 2:3], sm[:, 1:2], AF.Exp, [sm], [sm], scale=-0.5)
            K.act(hf[:], xt[:], AF.Copy, [xt, sm], [hf], scale=sm[:, 2:3])
            for b in range(2):
                bk = K.bank()
                for j in range(4):
                    k = b * 4 + j
                    K.tr(bk[:, j * 128:(j + 1) * 128], hf[:, k * 128:(k + 1) * 128], ident, [hf, CT], [bk], inc=(j == 3))
                K.tt("dve", hT[:, b * 4:b * 4 + 4, :], bk[:, :].rearrange("p (c e) -> p c e", c=4),
                     nwcols[:, b * 4:b * 4 + 4].unsqueeze(2).to_broadcast([128, 4, 128]), ALU.mult, [bk, ph["pfm"]], [hT])

        def proj_tm(hT, WB, c0, n, out_ap, outT, eng):
            bk = K.bank()
            for k in range(8):
                K.mm(bk[:, 0:n], hT[:, k, :], WB[:, k, c0:c0 + n], k == 0, k == 7, [hT, WB], [bk], inc=(k == 7))
            K.cp(eng, out_ap, bk[:, 0:n], [bk], [outT])

        def proj_fm(hT, WB, c0, nchunk, cx, j0):
            bk = K.bank()
            for j in range(nchunk):
                for k in range(8):
                    K.mm(bk[:, j * 128:(j + 1) * 128], WB[:, k, c0 + j * 128:c0 + (j + 1) * 128], hT[:, k, :],
                         k == 0, k == 7, [hT, WB], [bk], inc=(j == nchunk - 1 and k == 7))
            K.cp("act", cx[:, j0:j0 + nchunk, 3:131],
                 bk[:, 0:nchunk * 128].rearrange("p (c e) -> p c e", c=nchunk), [bk], [cx])

        def conv_silu(cx, cxn, cw, cb, nch, acc, tmps):
            for c0 in range(0, nch, 4):
                a = acc[:, c0:c0 + 4, :]
                for j in range(1, 4):
                    wj = cw[:, c0:c0 + 4, j:j + 1].to_broadcast([128, 4, 128])
                    K.tt(CONV_ENG, tmps[j - 1][:, 0:4, :], cx[:, c0:c0 + 4, j:j + 128], wj, ALU.mult, [cx, ph_cur["pfm"]], [tmps[j - 1]])
                w0 = cw[:, c0:c0 + 4, 0:1].to_broadcast([128, 4, 128])
                K.tt("dve", a, cx[:, c0:c0 + 4, 0:128], w0, ALU.mult, [cx, ph_cur["pfm"]], [acc])
                for j in range(1, 4):
                    K.tt("dve", a, a, tmps[j - 1][:, 0:4, :], ALU.add, [acc, tmps[j - 1]], [acc])
                if cb is not None:
                    K.tt("dve", a, a, cb[:, c0:c0 + 4].unsqueeze(2).to_broadcast([128, 4, 128]), ALU.add,
                         [acc, ph_cur["pfm"]], [acc])
                t0 = tmps[0]
                K.act(t0[:, 0:4, :], a, AF.Exp, [acc], [t0], scale=-1.0)
                K.act(t0[:, 0:4, :], t0[:, 0:4, :], AF.Ln, [t0], [t0], bias=1.0)
                K.act(t0[:, 0:4, :], t0[:, 0:4, :], AF.Exp, [t0], [t0], scale=-1.0)
                K.tt("dve", a, a, t0[:, 0:4, :], ALU.mult, [acc, t0], [acc])
                yield
            K.cp("act", cxn[:, :, 0:3], cx[:, :, 128:131], [cx], [cxn])

        def silu_to(out_ap, outT, in_ap, inT, tmp_ap, tmpT):
            K.act(tmp_ap, in_ap, AF.Exp, [inT], [tmpT], scale=-1.0)
            K.act(tmp_ap, tmp_ap, AF.Ln, [tmpT], [tmpT], bias=1.0)
            K.act(tmp_ap, tmp_ap, AF.Exp, [tmpT], [tmpT], scale=-1.0)
            K.tt("dve", out_ap, in_ap, tmp_ap, ALU.mult, [inT, tmpT], [outT])

        def rstd_from_ss(out_ap, ss_ap, smT, n, eps):
            K.act(out_ap, ss_ap, AF.Ln, [smT], [smT], scale=1.0 / n, bias=eps)
            K.act(out_ap, out_ap, AF.Exp, [smT], [smT], scale=-0.5)

        def softplus16(sp, z, sm2, n, smlT):
            K.act(sm2[:, 0:n], z, AF.Abs, [sm2, smlT], [sm2])
            K.act(sm2[:, 0:n], sm2[:, 0:n], AF.Exp, [sm2], [sm2], scale=-1.0)
            K.act(sm2[:, 0:n], sm2[:, 0:n], AF.Ln, [sm2], [sm2], bias=1.0)
            K.ts("dve", sp, z, 0.0, None, ALU.max, None, [smlT], [smlT])
            K.tt("dve", sp, sp, sm2[:, 0:n], ALU.add, [smlT, sm2], [smlT])

        def load_w(WB_ap, WBT, src_ap, nk, slot=None):
            for k in range(nk):
                K.dma("pool", slot or s_w, WB_ap[:, k, :], src_ap[k], W=[WBT])

        ph_cur = {}

        def step(g, pool):
            K.cur_pool = pool
            try:
                next(g)
                return True
            except StopIteration:
                return False
            finally:
                K.cur_pool = 0

        def pipeline(A, B, ratio):
            ga = A(0)
            while step(ga, 0):
                pass
            for t in range(NT):
                gb = B(t)
                ga = A(t + 1) if t + 1 < NT else iter(())
                doneA = False
                i = 0
                while step(gb, 1):
                    i += 1
                    if not doneA and i % ratio == 0:
                        doneA = not step(ga, 0)
                while not doneA:
                    doneA = not step(ga, 0)

        def pipeline2(A, B):
            def G(t):
                i = 0
                for gen in (A(t), B(t)):
                    for _ in gen:
                        i += 1
                        if i >= MAXSTEP:
                            return
                        yield

            def run_step(st):
                c0 = K.cost
                ok = step(st["g"], st["par"])
                st["clk"] += K.cost - c0
                return ok

            c0 = K.cost
            g = G(0)
            while step(g, 0):
                pass
            tile_cost = max(K.cost - c0, 1e-6)
            period = tile_cost * PIPE_FRAC
            streams = []
            t_next = 1
            last_start = 0.0
            while t_next < NT or streams:
                if len(streams) < 2 and t_next < NT:
                    start = last_start + period if streams else 0.0
                    if not streams or min(s_["clk"] for s_ in streams) >= start:
                        streams.append({"g": G(t_next), "par": t_next % 2, "clk": start})
                        last_start = start
                        t_next += 1
                        continue
                st = min(streams, key=lambda s_: s_["clk"])
                if not run_step(st):
                    streams.remove(st)

        def common_alloc(ph, l, wcols, wname):
            ph_cur.clear()
            ph["pfm"] = K.sb(ph["st"], "pfm", [128, NPF], F32)
            ph["ptm"] = K.sb(ph["st"], "ptm", [128, NPT], F32)
            K.dma("sp", s_pf, ph["pfm"][:], pf_in[l], W=[ph["pfm"]])
            K.dma("sp", s_pt, ph["ptm"][:], pt_in[l], W=[ph["ptm"]])
            ph["xt"] = [K.sb(ph["st"], f"xt{i}", [128, D], F32) for i in range(2)]
            ph["hf"] = [K.sb(ph["st"], f"hf{i}", [128, D], F32) for i in range(2)]
            ph["hT"] = [K.sb(ph["st"], f"hT{i}", [128, 8, 128], BF16) for i in range(2)]
            ph["smA"] = [K.sb(ph["st"], f"smA{i}", [128, 8], F32) for i in range(2)]
            ph["raw"] = [K.sb(ph["st"], f"raw{i}", [128, 16], F32) for i in range(2)]
            ph["sm"] = [K.sb(ph["st"], f"sm{i}", [128, 8], F32) for i in range(2)]
            ph["small"] = [K.sb(ph["st"], f"small{i}", [128, 128], F32) for i in range(2)]
            ph["sm2"] = [K.sb(ph["st"], f"sm2{i}", [128, 16], F32) for i in range(2)]
            for sm_ in ph["small"]:
                K.op("dve", lambda h: h.memset(sm_[:], 0.0), [], [sm_])
            ph_cur.update(ph)

        def pfm(ph, name):
            o, n = PF[name]
            return ph["pfm"][:, o:o + n]

        def ptm(ph, name):
            o, n = PT[name]
            return ph["ptm"][:, o:o + n]

        def load_x(ph, l, t):
            src, rb = xsrc(l, t)
            xt = ph["xt"][t % 2]
            K.dma("sp", s_x[t % 2], xt[:], src, R=rb, W=[xt])
            return xt

        for pst in _phase("init"):
            posi = K.sb(pst, "posi", [128, NT], I32)
            posf = K.sb(pst, "posf", [128, NT], F32)
            ang = K.sb(pst, "ang", [128, NT, 64], F32)
            kk = K.sb(pst, "kk", [128, NT, 64], F32)
            cs = K.sb(pst, "cs", [128, NT, 128], F32)
            K.dma("sp", s_pf, posi[:], pos_in, W=[posi])
            K.cp("dve", posf[:], posi[:], [posi], [posf])
            K.tt("dve", ang[:], posf[:, :].unsqueeze(2).to_broadcast([128, NT, 64]),
                 C("invf").unsqueeze(1).to_broadcast([128, NT, 64]), ALU.mult, [posf, CT], [ang])
            MAGIC = 12582912.0
            TWO_PI = 2.0 * math.pi
            for which, shift in ((1, 0.0), (0, math.pi / 2.0)):
                if shift != 0.0:
                    K.ts("dve", ang[:], ang[:], shift, None, ALU.add, None, [ang], [ang])
                K.ts("dve", kk[:], ang[:], 1.0 / TWO_PI, MAGIC, ALU.mult, ALU.add, [ang], [kk])
                K.ts("dve", kk[:], kk[:], MAGIC, None, ALU.subtract, None, [kk], [kk])
                K.stt("dve", kk[:], kk[:], -TWO_PI, ang[:], ALU.mult, ALU.add, [kk, ang], [kk])
                K.ts("dve", kk[:], kk[:], math.pi, -math.pi, ALU.min, ALU.max, [kk], [kk])
                K.act(cs[:, :, which * 64:(which + 1) * 64], kk[:], AF.Sin, [kk], [cs])
            K.dma("sp", s_cs, csd, cs[:], R=[cs], W=[csd_b])
            K.barrier()

        for l in range(DEPTH):
            for pst in _phase("ret"):
                ph = {"st": pst}
                common_alloc(ph, l, None, None)
                WB = K.sb(pst, "WBr", [128, 8, 2048], BF16)
                load_w(WB, WB, [w_in[l, k * 128:(k + 1) * 128, 0:2048] for k in range(8)], 8)
                cs = K.sb(pst, "cs", [128, NT, 128], F32)
                K.dma("sp", s_cs, cs[:], csd, R=[csd_b], W=[cs])
                tms = [K.sb(pst, f"tm{i}", [128, 2048], F32) for i in range(2)]
                qr_s = [K.sb(pst, "qr", [128, 512], F32) for _p in range(2)]
                kr_s = [K.sb(pst, "kr", [128, 512], F32) for _p in range(2)]
                tA_s = [K.sb(pst, "tA", [128, 512], F32) for _p in range(2)]
                qxT_s = [K.sb(pst, "qxT", [128, 4, 128], BF16) for _p in range(2)]
                krT_s = [K.sb(pst, "krT", [128, 4, 128], BF16) for _p in range(2)]
                krb_s = [K.sb(pst, "krb", [128, 512], BF16) for _p in range(2)]
                vb_s = [K.sb(pst, "vb", [128, 512], BF16) for _p in range(2)]
                vz_s = [K.sb(pst, "vz", [128, 512], BF16) for _p in range(2)]
                smT_s = [K.sb(pst, "smT", [128, 4, 128], BF16) for _p in range(2)]
                Sr = K.sb(pst, "Sr", [128, 512], F32)
                Srb = K.sb(pst, "Srb", [128, 512], BF16)
                yo_s = [K.sb(pst, "yo", [128, 512], F32) for _p in range(2)]
                sm = ph["sm"][0]
                K.op("dve", lambda h: h.memset(Sr[:], 0.0), [], [Sr])
                K.op("dve", lambda h: h.memset(Srb[:], 0.0), [], [Srb])
                def A(t):
                    par = t % 2
                    qr = qr_s[par]
                    kr = kr_s[par]
                    tA = tA_s[par]
                    qxT = qxT_s[par]
                    krT = krT_s[par]
                    krb = krb_s[par]
                    vb = vb_s[par]
                    vz = vz_s[par]
                    smT = smT_s[par]
                    yo = yo_s[par]
                    sm = ph["sm"][par]
                    sml = ph["small"][par]
                    tm = tms[par]
                    xt = load_x(ph, l, t)
                    norm_hT(ph, xt, pfm(ph, "mixnw"), ph["hf"][par], ph["hT"][par], ph["smA"][par])
                    yield
                    for g in range(4):
                        proj_tm(ph["hT"][par], WB, g * 512, 512, tm[:, g * 512:(g + 1) * 512], tm, "act" if g % 2 else "dve")
                        yield
                    yield

                def B(t):
                    par = t % 2
                    qr = qr_s[par]
                    kr = kr_s[par]
                    tA = tA_s[par]
                    qxT = qxT_s[par]
                    krT = krT_s[par]
                    krb = krb_s[par]
                    vb = vb_s[par]
                    vz = vz_s[par]
                    smT = smT_s[par]
                    yo = yo_s[par]
                    sm = ph["sm"][par]
                    sml = ph["small"][par]
                    tm = tms[par]
                    cosb = cs[:, t, 0:64].unsqueeze(1).to_broadcast([128, 4, 64])
                    sinb = cs[:, t, 64:128].unsqueeze(1).to_broadcast([128, 4, 64])
                    for (src0, dst) in ((0, qr), (512, kr)):
                        v4 = tm[:, src0:src0 + 512].rearrange("p (h t e) -> p h t e", h=4, t=2)
                        d4 = dst[:, :].rearrange("p (h t e) -> p h t e", h=4, t=2)
                        a4 = tA[:, 0:256].rearrange("p (h e) -> p h e", h=4)
                        t1, t2 = v4[:, :, 0, :], v4[:, :, 1, :]
                        K.tt("dve", d4[:, :, 0, :], t1, cosb, ALU.mult, [tm, cs], [dst])
                        K.tt("dve", a4, t2, sinb, ALU.mult, [tm, cs], [tA])
                        K.tt("dve", d4[:, :, 0, :], d4[:, :, 0, :], a4, ALU.subtract, [dst, tA], [dst])
                        K.tt("dve", d4[:, :, 1, :], t2, cosb, ALU.mult, [tm, cs], [dst])
                        K.tt("dve", a4, t1, sinb, ALU.mult, [tm, cs], [tA])
                        K.tt("dve", d4[:, :, 1, :], d4[:, :, 1, :], a4, ALU.add, [dst, tA], [dst])
                    K.tt("dve", qr[:, :].rearrange("p (h e) -> p h e", h=4), qr[:, :].rearrange("p (h e) -> p h e", h=4),
                         C("xiq").unsqueeze(2).to_broadcast([128, 4, 128]), ALU.mult, [qr, CT], [qr])
                    for (src, dstT) in ((qr, qxT), (kr, krT)):
                        yield
                        bk = K.bank()
                        for h in range(4):
                            K.tr(bk[:, h * 128:(h + 1) * 128], src[:, h * 128:(h + 1) * 128], ident, [src, CT], [bk], inc=(h == 3))
                        K.cp("act", dstT[:, :, :], bk[:, :].rearrange("p (h e) -> p h e", h=4), [bk], [dstT])
                    K.cp("act", krb[:], kr[:], [kr], [krb])
                    K.cp("act", vb[:], tm[:, 1024:1536], [tm], [vb])
                    K.tt("dve", vz[:, :].rearrange("p (h e) -> p h e", h=4), tm[:, 1024:1536].rearrange("p (h e) -> p h e", h=4),
                         C("zeta").unsqueeze(2).to_broadcast([128, 4, 128]), ALU.mult, [tm, CT], [vz])
                    yield
                    bk = K.bank()
                    for h in range(4):
                        K.mm(bk[:, h * 128:(h + 1) * 128], krT[:, h, :], qxT[:, h, :], True, True, [krT, qxT], [bk], inc=(h == 3))
                    K.tt("dve", smT[:, :, :], bk[:, :].rearrange("p (h e) -> p h e", h=4),
                         C("maskP").rearrange("p (h e) -> p h e", h=4), ALU.mult, [bk, CT], [smT])
                    yield
                    by = K.bank()
                    for h in range(4):
                        K.mm(by[:, h * 128:(h + 1) * 128], smT[:, h, :], vb[:, h * 128:(h + 1) * 128], True, False, [smT, vb], [by], inc=False)
                        K.mm(by[:, h * 128:(h + 1) * 128], qxT[:, h, :], Srb[:, h * 128:(h + 1) * 128], False, True, [qxT, Srb], [by], inc=(h == 3))
                    yield
                    bs = K.bank()
                    for h in range(4):
                        K.mm(bs[:, h * 128:(h + 1) * 128], krb[:, h * 128:(h + 1) * 128], vz[:, h * 128:(h + 1) * 128], True, True, [krb, vz], [bs], inc=(h == 3))
                    for h in range(4):
                        K.stt("dve", Sr[:, h * 128:(h + 1) * 128], Sr[:, h * 128:(h + 1) * 128], RET_G128[h],
                              bs[:, h * 128:(h + 1) * 128], ALU.mult, ALU.add, [Sr, bs], [Sr])
                    K.cp("act", Srb[:], Sr[:], [Sr], [Srb])
                    K.act(tA[:], by[:, :], AF.Square, [by], [tA])
                    K.rsum(sm[:, 4:8], tA[:, :].rearrange("p (h e) -> p h e", h=4), [tA], [sm])
                    rstd_from_ss(sm[:, 4:8], sm[:, 4:8], sm, 128.0, EPS)
                    K.tt("dve", yo[:, :].rearrange("p (h e) -> p h e", h=4), by[:, :].rearrange("p (h e) -> p h e", h=4),
                         sm[:, 4:8].unsqueeze(2).to_broadcast([128, 4, 128]), ALU.mult, [by, sm], [yo])
                    silu_to(tA[:], tA, tm[:, 1536:2048], tm, qr[:], qr)
                    K.tt("dve", yo[:], yo[:], tA[:], ALU.mult, [yo, tA], [yo])
                    K.dma("sp", s_o[t % 2], yc[t * 128:(t + 1) * 128, 0:512], yo[:], R=[yo], W=[yc_b[t][0]])
                    yield

                pipeline2(A, B)
                K.barrier()

            for pst in _phase("ssd"):
                ph = {"st": pst}
                common_alloc(ph, l, None, None)
                NW = 1544
                WB = K.sb(pst, "WBs", [128, 8, NW], BF16)
                load_w(WB, WB, [w_in[l, k * 128:(k + 1) * 128, 2048:3592] for k in range(8)], 8)
                zts = [K.sb(pst, f"zt{i}", [128, 512], F32) for i in range(2)]
                cxs = [K.sb(pst, f"cxs{i}", [128, 8, 131], F32) for i in range(2)]
                xc_s = [K.sb(pst, "xc", [128, 8, 128], F32) for _p in range(2)]
                tmps_s = [[K.sb(pst, f"ctmp{i}", [128, 4, 128], F32) for i in range(3)] for _p in range(2)]
                xs_tm_s = [K.sb(pst, "xs_tm", [128, 512], F32) for _p in range(2)]
                bm_tm_s = [K.sb(pst, "bm_tm", [128, 256], BF16) for _p in range(2)]
                bcT_s = [K.sb(pst, "bcT", [128, 4, 128], BF16) for _p in range(2)]
                decT_s = [K.sb(pst, "decT", [128, 8, 128], F32) for _p in range(2)]
                GT_s = [K.sb(pst, "GT", [128, 8, 128], BF16) for _p in range(2)]
                xdt_s = [K.sb(pst, "xdt", [128, 512], BF16) for _p in range(2)]
                xdte_s = [K.sb(pst, "xdte", [128, 512], BF16) for _p in range(2)]
                t1_s = [K.sb(pst, "t1", [128, 512], F32) for _p in range(2)]
                t2_s = [K.sb(pst, "t2", [128, 512], F32) for _p in range(2)]
                Ss = K.sb(pst, "Ss", [128, 512], F32)
                Ssb = K.sb(pst, "Ssb", [128, 512], BF16)
                acsT_s = [K.sb(pst, "acsT", [8, 128], F32) for _p in range(2)]
                sml = ph["small"][0]
                sm = ph["sm"][0]
                K.op("dve", lambda h: h.memset(Ss[:], 0.0), [], [Ss])
                K.op("dve", lambda h: h.memset(Ssb[:], 0.0), [], [Ssb])
                for cx_ in cxs:
                    K.op("dve", lambda h: h.memset(cx_[:], 0.0), [], [cx_])
                K.act(sml[:, 16:24], ptm(ph, "alog")[:, 0:8], AF.Exp, [ph["ptm"]], [sml])
                K.ts("dve", sml[:, 16:24], sml[:, 16:24], -1.0, None, ALU.mult, None, [sml], [sml])
                scw = pfm(ph, "scw").rearrange("p (c j) -> p c j", c=8)
                K.cp("dve", ph["small"][1][:], ph["small"][0][:], [ph["small"][0]], [ph["small"][1]])
                def A(t):
                    par = t % 2
                    xc = xc_s[par]
                    tmps = tmps_s[par]
                    xs_tm = xs_tm_s[par]
                    bm_tm = bm_tm_s[par]
                    bcT = bcT_s[par]
                    decT = decT_s[par]
                    GT = GT_s[par]
                    xdt = xdt_s[par]
                    xdte = xdte_s[par]
                    t1 = t1_s[par]
                    t2 = t2_s[par]
                    acsT = acsT_s[par]
                    sm = ph["sm"][par]
                    sml = ph["small"][par]
                    zt = zts[par]
                    cx = cxs[par]
                    raw = ph["raw"][par]
                    xt = load_x(ph, l, t)
                    norm_hT(ph, xt, pfm(ph, "mixnw"), ph["hf"][par], ph["hT"][par], ph["smA"][par])
                    yield
                    proj_tm(ph["hT"][par], WB, 0, 512, zt[:], zt, "act")
                    yield
                    proj_tm(ph["hT"][par], WB, 1536, 8, raw[:, 0:8], raw, "dve")
                    yield
                    proj_fm(ph["hT"][par], WB, 512, 4, cx, 0)
                    yield
                    proj_fm(ph["hT"][par], WB, 1024, 4, cx, 4)
                    yield
                    yield

                def B(t):
                    par = t % 2
                    xc = xc_s[par]
                    tmps = tmps_s[par]
                    xs_tm = xs_tm_s[par]
                    bm_tm = bm_tm_s[par]
                    bcT = bcT_s[par]
                    decT = decT_s[par]
                    GT = GT_s[par]
                    xdt = xdt_s[par]
                    xdte = xdte_s[par]
                    t1 = t1_s[par]
                    t2 = t2_s[par]
                    acsT = acsT_s[par]
                    sm = ph["sm"][par]
                    sml = ph["small"][par]
                    zt = zts[par]
                    cx = cxs[par]
                    cxn = cxs[1 - par]
                    raw = ph["raw"][par]
                    yield from conv_silu(cx, cxn, scw, pfm(ph, "scb"), 8, xc, tmps)
                    K.tt("dve", sml[:, 0:8], raw[:, 0:8], ptm(ph, "bias16")[:, 0:8], ALU.add, [raw, ph["ptm"]], [sml])
                    softplus16(sml[:, 88:96], sml[:, 0:8], ph["sm2"][par], 8, sml)
                    K.tt("dve", sml[:, 24:32], sml[:, 88:96], sml[:, 16:24], ALU.mult, [sml], [sml])
                    yield
                    bk = K.bank()
                    K.mm(bk[:, 0:8], C("tri"), sml[:, 24:32], True, True, [CT, sml], [bk], inc=False)
                    K.mm(bk[:, 8:16], C("ones"), sml[:, 24:32], True, True, [CT, sml], [bk], inc=False)
                    K.mm(bk[0:8, 128:256], sml[:, 24:32], C("tri"), True, True, [CT, sml], [bk], inc=True)
                    K.cp("dve", sml[:, 32:40], bk[:, 0:8], [bk], [sml])
                    K.ts("dve", sml[:, 40:48], bk[:, 0:8], -1.0, None, ALU.mult, None, [bk], [sml])
                    K.cp("dve", sml[:, 56:64], bk[:, 8:16], [bk], [sml])
                    K.cp("act", acsT[:, :], bk[0:8, 128:256], [bk], [acsT])
                    K.act(sml[:, 48:56], sml[:, 32:40], AF.Exp, [sml], [sml])
                    K.tt("dve", sml[:, 64:72], sml[:, 56:64], sml[:, 32:40], ALU.subtract, [sml], [sml])
                    K.act(sml[:, 64:72], sml[:, 64:72], AF.Exp, [sml], [sml])
                    K.act(sml[:, 72:80], sml[:, 56:64], AF.Exp, [sml], [sml])
                    K.tt("dve", sml[:, 80:88], sml[:, 88:96], sml[:, 64:72], ALU.mult, [sml], [sml])
                    for hb in range(2):
                        yield
                        bk = K.bank()
                        for j in range(4):
                            h = hb * 4 + j
                            K.mm(bk[:, j * 128:(j + 1) * 128], ident, C("negssd"), True, False, [CT], [bk], inc=False)
                            K.mm(bk[:, j * 128:(j + 1) * 128], CT[0:8, C_OFF["sel8"][0] + h * 128:C_OFF["sel8"][0] + (h + 1) * 128],
                                 acsT[:, :], False, True, [CT, acsT], [bk], inc=(j == 3))
                        for j in range(4):
                            h = hb * 4 + j
                            K.act(decT[:, h, :], bk[:, j * 128:(j + 1) * 128], AF.Exp, [bk, sml], [decT], bias=sml[:, 40 + h:41 + h])
                    yield
                    bk = K.bank()
                    for c in range(4):
                        K.tr(bk[:, c * 128:(c + 1) * 128], xc[:, c, :], ident, [xc, CT], [bk], inc=(c == 3))
                    K.cp("act", xs_tm[:], bk[:, :], [bk], [xs_tm])
                    yield
                    bk = K.bank()
                    for g in range(2):
                        K.tr(bk[:, g * 128:(g + 1) * 128], xc[:, 4 + g, :], ident, [xc, CT], [bk], inc=(g == 1))
                    K.cp("act", bm_tm[:], bk[:, 0:256], [bk], [bm_tm])
                    K.cp("dve", bcT[:, :, :], xc[:, 4:8, :], [xc], [bcT])
                    yield
                    bk = K.bank()
                    for g in range(2):
                        K.mm(bk[:, g * 128:(g + 1) * 128], bcT[:, g, :], bcT[:, 2 + g, :], True, True, [bcT], [bk], inc=(g == 1))
                    for g in range(2):
                        K.tt("dve", GT[:, 4 * g:4 * g + 4, :], decT[:, 4 * g:4 * g + 4, :],
                             bk[:, g * 128:(g + 1) * 128].unsqueeze(1).to_broadcast([128, 4, 128]), ALU.mult, [decT, bk], [GT])
                    xs3 = xs_tm[:, :].rearrange("p (h e) -> p h e", h=8)
                    K.tt("dve", xdt[:, :].rearrange("p (h e) -> p h e", h=8), xs3,
                         sml[:, 88:96].unsqueeze(2).to_broadcast([128, 8, 64]), ALU.mult, [xs_tm, sml], [xdt])
                    K.tt("dve", xdte[:, :].rearrange("p (h e) -> p h e", h=8), xs3,
                         sml[:, 80:88].unsqueeze(2).to_broadcast([128, 8, 64]), ALU.mult, [xs_tm, sml], [xdte])
                    yield
                    by = K.bank()
                    for h in range(8):
                        K.mm(by[:, h * 64:(h + 1) * 64], GT[:, h, :], xdt[:, h * 64:(h + 1) * 64], True, True, [GT, xdt], [by], inc=(h == 7))
                    yield
                    bc = K.bank()
                    for g in range(2):
                        K.mm(bc[:, g * 256:(g + 1) * 256], bcT[:, 2 + g, :], Ssb[:, g * 256:(g + 1) * 256], True, True, [bcT, Ssb], [bc], inc=(g == 1))
                    yield
                    bn = K.bank()
                    for g in range(2):
                        K.mm(bn[:, g * 256:(g + 1) * 256], bm_tm[:, g * 128:(g + 1) * 128], xdte[:, g * 256:(g + 1) * 256], True, True, [bm_tm, xdte], [bn], inc=(g == 1))
                    K.tt("dve", t1[:, :].rearrange("p (h e) -> p h e", h=8), bc[:, :].rearrange("p (h e) -> p h e", h=8),
                         sml[:, 48:56].unsqueeze(2).to_broadcast([128, 8, 64]), ALU.mult, [bc, sml], [t1])
                    K.tt("dve", t1[:], t1[:], by[:, :], ALU.add, [t1, by], [t1])
                    K.tt("dve", t2[:, :].rearrange("p (h e) -> p h e", h=8), xs3,
                         ptm(ph, "dskip").unsqueeze(2).to_broadcast([128, 8, 64]), ALU.mult, [xs_tm, ph["ptm"]], [t2])
                    K.tt("dve", t1[:], t1[:], t2[:], ALU.add, [t1, t2], [t1])
                    K.tt("dve", Ss[:, :].rearrange("p (h e) -> p h e", h=8), Ss[:, :].rearrange("p (h e) -> p h e", h=8),
                         sml[:, 72:80].unsqueeze(2).to_broadcast([128, 8, 64]), ALU.mult, [Ss, sml], [Ss])
                    K.tt("dve", Ss[:], Ss[:], bn[:, :], ALU.add, [Ss, bn], [Ss])
                    K.cp("act", Ssb[:], Ss[:], [Ss], [Ssb])
                    silu_to(t2[:], t2, zt[:], zt, xs_tm[:], xs_tm)
                    K.tt("dve", t1[:], t1[:], t2[:], ALU.mult, [t1, t2], [t1])
                    K.act(t2[:], t1[:], AF.Square, [t1], [t2])
                    K.rsum(sm[:, 4:6], t2[:, :].rearrange("p (g e) -> p g e", g=2), [t2], [sm])
                    rstd_from_ss(sm[:, 4:6], sm[:, 4:6], sm, 256.0, EPS)
                    K.tt("dve", t1[:, :].rearrange("p (g e) -> p g e", g=2), t1[:, :].rearrange("p (g e) -> p g e", g=2),
                         sm[:, 4:6].unsqueeze(2).to_broadcast([128, 2, 256]), ALU.mult, [t1, sm], [t1])
                    K.dma("sp", s_o[t % 2], yc[t * 128:(t + 1) * 128, 512:1024], t1[:], R=[t1], W=[yc_b[t][1]])
                    yield

                pipeline(A, B, 4)
                K.barrier()

            for pst in _phase("gdn"):
                ph = {"st": pst}
                common_alloc(ph, l, None, None)
                NW = 2056
                WB = K.sb(pst, "WBg", [128, 8, NW], BF16)
                load_w(WB, WB, [w_in[l, k * 128:(k + 1) * 128, 3592:5648] for k in range(8)], 8)
                zts = [K.sb(pst, f"zt{i}", [128, 512], F32) for i in range(2)]
                cxs = [K.sb(pst, f"cxg{i}", [128, 12, 131], F32) for i in range(2)]
                gc_s = [K.sb(pst, "gc", [128, 12, 128], F32) for _p in range(2)]
                tmps_s = [[K.sb(pst, f"ctmp{i}", [128, 4, 128], F32) for i in range(3)] for _p in range(2)]
                qkv_s = [K.sb(pst, "qkv", [128, 1536], F32) for _p in range(2)]
                sq_s = [K.sb(pst, "sq", [128, 1024], F32) for _p in range(2)]
                kdm_s = [[K.sb(pst, f"kdm{i}", [128, 512], BF16) for i in range(2)] for _p in range(2)]
                vn_s = [[K.sb(pst, f"vn{i}", [128, 512], BF16) for i in range(2)] for _p in range(2)]
                vf_s = [K.sb(pst, "vf", [128, 512], BF16) for _p in range(2)]
                qnT_s = [K.sb(pst, "qnT", [128, 4, 128], BF16) for _p in range(2)]
                knT_s = [K.sb(pst, "knT", [128, 4, 128], BF16) for _p in range(2)]
                decL_s = [K.sb(pst, "decL", [128, 4, 128], F32) for _p in range(2)]
                decTg_s = [K.sb(pst, "decTg", [128, 4, 128], F32) for _p in range(2)]
                Aa_s = [[K.sb(pst, f"A{i}", [128, 4, 128], F32) for i in range(2)] for _p in range(2)]
                Bb_s = [[K.sb(pst, f"B{i}", [128, 4, 128], F32) for i in range(2)] for _p in range(2)]
                Pm_s = [K.sb(pst, "Pm", [128, 4, 128], F32) for _p in range(2)]
                TTb_s = [K.sb(pst, "TTb", [128, 4, 128], BF16) for _p in range(2)]
                vbt_s = [K.sb(pst, "vbt", [128, 512], BF16) for _p in range(2)]
                kbg_s = [K.sb(pst, "kbg", [128, 512], BF16) for _p in range(2)]
                uu_s = [K.sb(pst, "uu", [128, 512], F32) for _p in range(2)]
                wT_s = [K.sb(pst, "wT", [128, 4, 128], BF16) for _p in range(2)]
                attnT_s = [K.sb(pst, "attnT", [128, 4, 128], BF16) for _p in range(2)]
                otmp_s = [K.sb(pst, "otmp", [128, 512], F32) for _p in range(2)]
                oo_s = [K.sb(pst, "oo", [128, 512], F32) for _p in range(2)]
                Sg = K.sb(pst, "Sg", [128, 512], F32)
                Sgb = K.sb(pst, "Sgb", [128, 512], BF16)
                gcsT_s = [K.sb(pst, "gcsT", [8, 128], F32) for _p in range(2)]
                sml = ph["small"][0]
                sm = ph["sm"][0]
                K.op("dve", lambda h: h.memset(Sg[:], 0.0), [], [Sg])
                K.op("dve", lambda h: h.memset(Sgb[:], 0.0), [], [Sgb])
                for cx_ in cxs:
                    K.op("dve", lambda h: h.memset(cx_[:], 0.0), [], [cx_])
                K.op("dve", lambda h: h.memset(sml[:], 0.0), [], [sml])
                K.act(sml[:, 16:20], ptm(ph, "alog")[:, 8:12], AF.Exp, [ph["ptm"]], [sml])
                K.ts("dve", sml[:, 16:20], sml[:, 16:20], -1.0, None, ALU.mult, None, [sml], [sml])
                gcw = pfm(ph, "gcw").rearrange("p (c j) -> p c j", c=12)
                sel_o = C_OFF["sel8"][0]
                K.cp("dve", ph["small"][1][:], ph["small"][0][:], [ph["small"][0]], [ph["small"][1]])
                def A(t):
                    par = t % 2
                    gc = gc_s[par]
                    tmps = tmps_s[par]
                    qkv = qkv_s[par]
                    sq = sq_s[par]
                    kdm = kdm_s[par]
                    vn = vn_s[par]
                    vf = vf_s[par]
                    qnT = qnT_s[par]
                    knT = knT_s[par]
                    decL = decL_s[par]
                    decTg = decTg_s[par]
                    Aa = Aa_s[par]
                    Bb = Bb_s[par]
                    Pm = Pm_s[par]
                    TTb = TTb_s[par]
                    vbt = vbt_s[par]
                    kbg = kbg_s[par]
                    uu = uu_s[par]
                    wT = wT_s[par]
                    attnT = attnT_s[par]
                    otmp = otmp_s[par]
                    oo = oo_s[par]
                    gcsT = gcsT_s[par]
                    sm = ph["sm"][par]
                    sml = ph["small"][par]
                    zt = zts[par]
                    cx = cxs[par]
                    raw = ph["raw"][par]
                    xt = load_x(ph, l, t)
                    norm_hT(ph, xt, pfm(ph, "mixnw"), ph["hf"][par], ph["hT"][par], ph["smA"][par])
                    yield
                    proj_tm(ph["hT"][par], WB, 1536, 512, zt[:], zt, "act")
                    yield
                    proj_tm(ph["hT"][par], WB, 2048, 8, raw[:, 0:8], raw, "dve")
                    yield
                    for j0 in range(0, 12, 4):
                        proj_fm(ph["hT"][par], WB, j0 * 128, 4, cx, j0)
                        yield
                    yield

                def B(t):
                    par = t % 2
                    gc = gc_s[par]
                    tmps = tmps_s[par]
                    qkv = qkv_s[par]
                    sq = sq_s[par]
                    kdm = kdm_s[par]
                    vn = vn_s[par]
                    vf = vf_s[par]
                    qnT = qnT_s[par]
                    knT = knT_s[par]
                    decL = decL_s[par]
                    decTg = decTg_s[par]
                    Aa = Aa_s[par]
                    Bb = Bb_s[par]
                    Pm = Pm_s[par]
                    TTb = TTb_s[par]
                    vbt = vbt_s[par]
                    kbg = kbg_s[par]
                    uu = uu_s[par]
                    wT = wT_s[par]
                    attnT = attnT_s[par]
                    otmp = otmp_s[par]
                    oo = oo_s[par]
                    gcsT = gcsT_s[par]
                    sm = ph["sm"][par]
                    sml = ph["small"][par]
                    zt = zts[par]
                    cx = cxs[par]
                    cxn = cxs[1 - par]
                    raw = ph["raw"][par]
                    yield from conv_silu(cx, cxn, gcw, None, 12, gc, tmps)
                    if GSTAGE < 2:
                        return
                    for j0 in range(0, 12, 4):
                        yield
                        bk = K.bank()
                        for j in range(4):
                            K.tr(bk[:, j * 128:(j + 1) * 128], gc[:, j0 + j, :], ident, [gc, CT], [bk], inc=(j == 3))
                        K.cp("act" if j0 == 4 else "dve", qkv[:, j0 * 128:(j0 + 4) * 128], bk[:, :], [bk], [qkv])
                    K.act(sq[:], qkv[:, 0:1024], AF.Square, [qkv], [sq])
                    K.rsum(sml[:, 64:72], sq[:, :].rearrange("p (h e) -> p h e", h=8), [sq], [sml])
                    K.act(sml[:, 52:60], sml[:, 64:72], AF.Ln, [sml], [sml], bias=EPS)
                    K.act(sml[:, 52:60], sml[:, 52:60], AF.Exp, [sml], [sml], scale=-0.5)
                    K.ts("dve", sml[:, 52:56], sml[:, 52:56], 128.0 ** -0.5, None, ALU.mult, None, [sml], [sml])
                    K.tt("dve", qkv[:, 0:1024].rearrange("p (h e) -> p h e", h=8), qkv[:, 0:1024].rearrange("p (h e) -> p h e", h=8),
                         sml[:, 52:60].unsqueeze(2).to_broadcast([128, 8, 128]), ALU.mult, [qkv, sml], [qkv])
                    if GSTAGE < 3:
                        return
                    K.act(sml[:, 8:12], raw[:, 0:4], AF.Exp, [raw], [sml], scale=-1.0)
                    K.ts("dve", sml[:, 8:12], sml[:, 8:12], 1.0, None, ALU.add, None, [sml], [sml])
                    K.recip(sml[:, 8:12], sml[:, 8:12], [sml], [sml])
                    K.ts("dve", sml[:, 12:16], sml[:, 8:12], -1.0, None, ALU.mult, None, [sml], [sml])
                    K.tt("dve", sml[:, 4:8], raw[:, 4:8], ptm(ph, "bias16")[:, 12:16], ALU.add, [raw, ph["ptm"]], [sml])
                    softplus16(sml[:, 76:80], sml[:, 4:8], ph["sm2"][par], 4, sml)
                    K.tt("dve", sml[:, 20:24], sml[:, 76:80], sml[:, 16:20], ALU.mult, [sml], [sml])
                    yield
                    bk = K.bank()
                    K.mm(bk[:, 0:4], C("tribd"), sml[:, 20:24], True, True, [CT, sml], [bk], inc=False)
                    K.mm(bk[:, 4:8], C("blockones"), sml[:, 20:24], True, True, [CT, sml], [bk], inc=False)
                    K.mm(bk[:, 8:12], C("bs0"), sml[:, 20:24], True, True, [CT, sml], [bk], inc=False)
                    K.mm(bk[:, 12:16], C("bs1"), sml[:, 20:24], True, True, [CT, sml], [bk], inc=False)
                    K.mm(bk[0:8, 128:256], sml[:, 20:28], C("tribd"), True, True, [CT, sml], [bk], inc=True)
                    K.cp("dve", sml[:, 24:28], bk[:, 0:4], [bk], [sml])
                    K.ts("dve", sml[:, 28:32], bk[:, 0:4], -1.0, None, ALU.mult, None, [bk], [sml])
                    K.cp("dve", sml[:, 32:36], bk[:, 4:8], [bk], [sml])
                    K.act(sml[:, 44:52], bk[:, 8:16], AF.Exp, [bk], [sml])
                    K.cp("act", gcsT[:, :], bk[0:8, 128:256], [bk], [gcsT])
                    K.act(sml[:, 36:40], sml[:, 24:28], AF.Exp, [sml], [sml])
                    K.tt("dve", sml[:, 40:44], sml[:, 32:36], sml[:, 24:28], ALU.subtract, [sml], [sml])
                    K.act(sml[:, 40:44], sml[:, 40:44], AF.Exp, [sml], [sml])
                    K.tt("dve", sml[:, 60:64], sml[:, 8:12], sml[:, 36:40], ALU.mult, [sml], [sml])
                    if GSTAGE < 4:
                        return
                    qn3 = qkv[:, 0:512].rearrange("p (h e) -> p h e", h=4)
                    kn3 = qkv[:, 512:1024].rearrange("p (h e) -> p h e", h=4)
                    v3 = qkv[:, 1024:1536].rearrange("p (h e) -> p h e", h=4)
                    for i in range(2):
                        K.ts("dve", sml[:, 96 + 4 * i:100 + 4 * i], sml[:, 40:44], C("bs%d" % i)[:, 0:1], None, ALU.mult, None, [sml, CT], [sml])
                        K.tt("dve", kdm[i][:, :].rearrange("p (h e) -> p h e", h=4), kn3,
                             sml[:, 96 + 4 * i:100 + 4 * i].unsqueeze(2).to_broadcast([128, 4, 128]), ALU.mult, [qkv, sml], [kdm[i]])
                    K.tt("dve", kbg[:, :].rearrange("p (h e) -> p h e", h=4), kn3,
                         sml[:, 60:64].unsqueeze(2).to_broadcast([128, 4, 128]), ALU.mult, [qkv, sml], [kbg])
                    K.tt("dve", vbt[:, :].rearrange("p (h e) -> p h e", h=4), v3,
                         sml[:, 8:12].unsqueeze(2).to_broadcast([128, 4, 128]), ALU.mult, [qkv, sml], [vbt])
                    for (c0, dstT) in ((0, qnT), (512, knT)):
                        yield
                        bk = K.bank()
                        for h in range(4):
                            K.tr(bk[:, h * 128:(h + 1) * 128], qkv[:, c0 + h * 128:c0 + (h + 1) * 128], ident, [qkv, CT], [bk], inc=(h == 3))
                        K.cp("act", dstT[:, :, :], bk[:, :].rearrange("p (h e) -> p h e", h=4), [bk], [dstT])
                    if GSTAGE < 5:
                        return
                    for (msk, dst, scale, bcol) in (("posS", decL, -1.0, 24), ("negI", decTg, 1.0, 28)):
                        yield
                        bk = K.bank()
                        for h in range(4):
                            K.mm(bk[:, h * 128:(h + 1) * 128], ident, C(msk), True, False, [CT], [bk], inc=False)
                            K.mm(bk[:, h * 128:(h + 1) * 128], CT[0:8, sel_o + h * 128:sel_o + (h + 1) * 128], gcsT[:, :],
                                 False, True, [CT, gcsT], [bk], inc=(h == 3))
                        for h in range(4):
                            K.act(dst[:, h, :], bk[:, h * 128:(h + 1) * 128], AF.Exp, [bk, sml], [dst],
                                  scale=scale, bias=sml[:, bcol + h:bcol + h + 1])
                    if GSTAGE < 6:
                        return
                    yield
                    bk = K.bank()
                    for h in range(4):
                        K.mm(bk[:, h * 128:(h + 1) * 128], knT[:, h, :], knT[:, h, :], True, True, [knT], [bk], inc=(h == 3))
                    if GSTAGE == 60:
                        return
                    for h in range(4):
                        K.stt("dve", Bb[0][:, h, :], bk[:, h * 128:(h + 1) * 128], sml[:, 12 + h:13 + h], decL[:, h, :],
                              ALU.mult, ALU.mult, [bk, sml, decL], [Bb[0]])
                    if GSTAGE == 61:
                        return
                    yield
                    bk = K.bank()
                    for h in range(4):
                        K.tr(bk[:, h * 128:(h + 1) * 128], Bb[0][:, h, :], ident, [Bb[0], CT], [bk], inc=(h == 3))
                    if GSTAGE == 62:
                        return
                    K.cp("act", Aa[0][:, :, :], bk[:, :].rearrange("p (h e) -> p h e", h=4), [bk], [Aa[0]])
                    if GSTAGE == 63:
                        return
                    for h in range(4):
                        K.tt("dve", Pm[:, h, :], Aa[0][:, h, :], ident, ALU.add, [Aa[0], CT], [Pm])
                    if GSTAGE < 7:
                        return
                    cur = 0
                    for lev in range(1, 6):
                        nxt = 1 - cur
                        if lev <= 4:
                            yield
                            bk = K.bank()
                            for h in range(4):
                                K.mm(bk[:, h * 128:(h + 1) * 128], Bb[cur][:, h, :], Aa[cur][:, h, :], True, True, [Bb[cur], Aa[cur]], [bk], inc=(h == 3))
                            K.cp("act", Aa[nxt][:, :, :], bk[:, :].rearrange("p (h e) -> p h e", h=4), [bk], [Aa[nxt]])
                        yield
                        bk = K.bank()
                        for h in range(4):
                            K.mm(bk[:, h * 128:(h + 1) * 128], Aa[cur][:, h, :], Bb[cur][:, h, :], True, True, [Bb[cur], Aa[cur]], [bk], inc=(h == 3))
                        K.cp("dve", Bb[nxt][:, :, :], bk[:, :].rearrange("p (h e) -> p h e", h=4), [bk], [Bb[nxt]])
                        yield
                        bk = K.bank()
                        for h in range(4):
                            K.mm(bk[:, h * 128:(h + 1) * 128], Bb[nxt][:, h, :], Pm[:, h, :], True, True, [Bb[nxt], Pm], [bk], inc=(h == 3))
                        K.tt("dve", Pm[:, :, :], Pm[:, :, :], bk[:, :].rearrange("p (h e) -> p h e", h=4), ALU.add, [Pm, bk], [Pm])
                        cur = nxt
                    if GSTAGE < 8:
                        return
                    K.cp("act", TTb[:, :, :], Pm[:, :, :], [Pm], [TTb])
                    yield
                    bk = K.bank()
                    for h in range(4):
                        K.mm(bk[:, h * 128:(h + 1) * 128], TTb[:, h, :], vbt[:, h * 128:(h + 1) * 128], True, True, [TTb, vbt], [bk], inc=(h == 3))
                    K.cp("act", uu[:], bk[:, :], [bk], [uu])
                    yield
                    bk = K.bank()
                    for h in range(4):
                        K.mm(bk[:, h * 128:(h + 1) * 128], kbg[:, h * 128:(h + 1) * 128], TTb[:, h, :], True, True, [TTb, kbg], [bk], inc=(h == 3))
                    K.cp("act", wT[:, :, :], bk[:, :].rearrange("p (h e) -> p h e", h=4), [bk], [wT])
                    yield
                    bk = K.bank()
                    for h in range(4):
                        K.mm(bk[:, h * 128:(h + 1) * 128], knT[:, h, :], qnT[:, h, :], True, True, [knT, qnT], [bk], inc=(h == 3))
                    K.tt("dve", attnT[:, :, :], bk[:, :].rearrange("p (h e) -> p h e", h=4), decTg[:, :, :], ALU.mult, [bk, decTg], [attnT])
                    if GSTAGE < 9:
                        return
                    for hf in range(2):
                        yield
                        bk = K.bank()
                        for h in range(4):
                            K.mm(bk[:, h * 128:(h + 1) * 128], wT[:, h, :], Sgb[:, h * 128:(h + 1) * 128], True, True, [wT, Sgb], [bk], inc=(h == 3))
                        K.tt("dve", vn[hf][:], uu[:], bk[:, :], ALU.subtract, [uu, bk], [vn[hf]])
                        yield
                        bk = K.bank()
                        for h in range(4):
                            K.mm(bk[:, h * 128:(h + 1) * 128], qnT[:, h, :], Sgb[:, h * 128:(h + 1) * 128], True, True,
                                 [qnT, Sgb], [bk], inc=(h == 3))
                        K.ts("dve", sml[:, 104 + 4 * hf:108 + 4 * hf], sml[:, 36:40], C("bs%d" % hf)[:, 0:1], None, ALU.mult, None, [sml, CT], [sml])
                        dst = otmp if hf == 0 else uu
                        K.tt("dve", dst[:, :].rearrange("p (h e) -> p h e", h=4), bk[:, :].rearrange("p (h e) -> p h e", h=4),
                             sml[:, 104 + 4 * hf:108 + 4 * hf].unsqueeze(2).to_broadcast([128, 4, 128]), ALU.mult, [bk, sml], [dst])
                        if hf == 1:
                            K.tt("dve", otmp[:], otmp[:], uu[:], ALU.add, [otmp, uu], [otmp])
                        yield
                        bk = K.bank()
                        for h in range(4):
                            K.mm(bk[:, h * 128:(h + 1) * 128], kdm[hf][:, h * 128:(h + 1) * 128], vn[hf][:, h * 128:(h + 1) * 128],
                                 True, True, [kdm[hf], vn[hf]], [bk], inc=(h == 3))
                        K.tt("dve", Sg[:, :].rearrange("p (h e) -> p h e", h=4), Sg[:, :].rearrange("p (h e) -> p h e", h=4),
                             sml[:, 44 + 4 * hf:48 + 4 * hf].unsqueeze(2).to_broadcast([128, 4, 128]), ALU.mult, [Sg, sml], [Sg])
                        K.tt("dve", Sg[:], Sg[:], bk[:, :], ALU.add, [Sg, bk], [Sg])
                        K.cp("act", Sgb[:], Sg[:], [Sg], [Sgb])
                    K.ts("dve", vf[:], vn[0][:], C("bs0")[:, 0:1], None, ALU.mult, None, [vn[0], CT], [vf])
                    K.stt("dve", vf[:], vn[1][:], C("bs1")[:, 0:1], vf[:], ALU.mult, ALU.add, [vn[1], CT, vf], [vf])
                    yield
                    bk = K.bank()
                    for h in range(4):
                        K.mm(bk[:, h * 128:(h + 1) * 128], attnT[:, h, :], vf[:, h * 128:(h + 1) * 128], True, True, [attnT, vf], [bk], inc=(h == 3))
                    K.tt("dve", oo[:], otmp[:], bk[:, :], ALU.add, [otmp, bk], [oo])
                    K.act(otmp[:], oo[:], AF.Square, [oo], [otmp])
                    K.rsum(sm[:, 4:8], otmp[:, :].rearrange("p (h e) -> p h e", h=4), [otmp], [sm])
                    rstd_from_ss(sm[:, 4:8], sm[:, 4:8], sm, 128.0, EPS)
                    K.tt("dve", oo[:, :].rearrange("p (h e) -> p h e", h=4), oo[:, :].rearrange("p (h e) -> p h e", h=4),
                         sm[:, 4:8].unsqueeze(2).to_broadcast([128, 4, 128]), ALU.mult, [oo, sm], [oo])
                    silu_to(otmp[:], otmp, zt[:], zt, uu[:], uu)
                    K.tt("dve", oo[:], oo[:], otmp[:], ALU.mult, [oo, otmp], [oo])
                    K.dma("sp", s_o[t % 2], yc[t * 128:(t + 1) * 128, 1024:1536], oo[:], R=[oo], W=[yc_b[t][2]])
                    yield

                pipeline2(A, B)
                K.barrier()

            for pst in _phase("out"):
                ph = {"st": pst}
                common_alloc(ph, l, None, None)
                WB = K.sb(pst, "WBo", [128, 12, 1024], BF16)
                load_w(WB, WB, [w_out[l, k * 128:(k + 1) * 128, :] for k in range(12)], 12)
                yt = [K.sb(pst, f"yt{i}", [128, MIXW], F32) for i in range(2)]
                yTs = [K.sb(pst, f"yT{i}", [128, 12, 128], BF16) for i in range(2)]
                xo = [K.sb(pst, f"xo{i}", [128, D], F32) for i in range(2)]
                ynw = pfm(ph, "ynw")
                def A(t):
                    par = t % 2
                    yT = yTs[par]
                    xt = load_x(ph, l, t)
                    ytt = yt[t % 2]
                    K.dma("sp", s_a[t % 2], ytt[:], yc[t * 128:(t + 1) * 128, :], R=yc_b[t], W=[ytt])
                    for j0 in range(0, 12, 4):
                        bk = K.bank()
                        for j in range(4):
                            K.tr(bk[:, j * 128:(j + 1) * 128], ytt[:, (j0 + j) * 128:(j0 + j + 1) * 128], ident, [ytt, CT], [bk], inc=(j == 3))
                        K.tt("dve", yT[:, j0:j0 + 4, :], bk[:, :].rearrange("p (c e) -> p c e", c=4),
                             ynw[:, j0:j0 + 4].unsqueeze(2).to_broadcast([128, 4, 128]), ALU.mult, [bk, ph["pfm"]], [yT])
                    yield

                def B(t):
                    par = t % 2
                    yT = yTs[par]
                    xt = ph["xt"][par]
                    xot = xo[t % 2]
                    for n in range(2):
                        yield
                        bk = K.bank()
                        for k in range(12):
                            K.mm(bk[:, :], yT[:, k, :], WB[:, k, n * 512:(n + 1) * 512], k == 0, k == 11, [yT, WB], [bk], inc=(k == 11))
                        K.tt("dve", xot[:, n * 512:(n + 1) * 512], xt[:, n * 512:(n + 1) * 512], bk[:, :], ALU.add, [xt, bk], [xot])
                    K.dma("sp", s_o[t % 2], xb[t * 128:(t + 1) * 128, :], xot[:], R=[xot], W=[xb_b[t]])
                    yield

                pipeline(A, B, 1)
                K.barrier()

            for pst in _phase("mlp"):
                ph = {"st": pst}
                common_alloc(ph, l, None, None)
                WU = K.sb(pst, "WU", [128, 8, DFF], BF16)
                WD = K.sb(pst, "WD", [128, 32, D], BF16)
                load_w(WU, WU, [w_up[l, k * 128:(k + 1) * 128, :] for k in range(8)], 8)
                load_w(WD, WD, [w_down[l, k * 128:(k + 1) * 128, :] for k in range(32)], 32, s_w2)
                aT = K.sb(pst, "aT", [128, 32, 128], BF16)
                rl = K.sb(pst, "rl", [128, 512], F32)
                xo = [K.sb(pst, f"xo{i}", [128, D], F32) for i in range(2)]
                def A(t):
                    par = t % 2
                    xt = ph["xt"][t % 2]
                    K.dma("sp", s_x[t % 2], xt[:], xb[t * 128:(t + 1) * 128, :], R=[xb_b[t]], W=[xt])
                    norm_hT(ph, xt, pfm(ph, "mlpnw"), ph["hf"][par], ph["hT"][par], ph["smA"][par])
                    yield
                    yield

                def B(t):
                    par = t % 2
                    xt = ph["xt"][par]
                    for f0 in range(0, 32, 4):
                        yield
                        bk = K.bank()
                        for j in range(4):
                            f = f0 + j
                            for k in range(8):
                                K.mm(bk[:, j * 128:(j + 1) * 128], WU[:, k, f * 128:(f + 1) * 128], ph["hT"][par][:, k, :],
                                     k == 0, k == 7, [ph["hT"][par], WU], [bk], inc=(j == 3 and k == 7))
                        K.act(rl[:], bk[:, :], AF.Relu, [bk], [rl])
                        K.tt("dve", aT[:, f0:f0 + 4, :], rl[:, :].rearrange("p (c e) -> p c e", c=4),
                             rl[:, :].rearrange("p (c e) -> p c e", c=4), ALU.mult, [rl], [aT])
                    xot = xo[t % 2]
                    for n in range(2):
                        yield
                        bk = K.bank()
                        for k in range(32):
                            K.mm(bk[:, :], aT[:, k, :], WD[:, k, n * 512:(n + 1) * 512], k == 0, k == 31, [aT, WD], [bk], inc=(k == 31))
                        K.tt("dve", xot[:, n * 512:(n + 1) * 512], xt[:, n * 512:(n + 1) * 512], bk[:, :], ALU.add, [xt, bk], [xot])
                    K.dma("sp", s_o[t % 2], xb[t * 128:(t + 1) * 128, :], xot[:], R=[xot], W=[xb_b[t]])
                    yield

                pipeline(A, B, 3)
                K.barrier()

        for pst in _phase("final"):
            fnw = K.sb(pst, "fnw", [128, D], F32)
            K.dma("sp", s_pf, fnw[:], fnw_in, W=[fnw])
            xts = [K.sb(pst, f"fx{i}", [128, D], F32) for i in range(2)]
            hf2 = [K.sb(pst, f"fh{i}", [128, D], F32) for i in range(2)]
            sm = K.sb(pst, "fsm", [128, 8], F32)
            for t in range(NT):
                xt = xts[t % 2]
                hf = hf2[t % 2]
                K.dma("sp", s_x[t % 2], xt[:], xb[t * 128:(t + 1) * 128, :], R=[xb_b[t]], W=[xt])
                K.act(hf[:], xt[:], AF.Square, [xt], [hf, sm], accum_out=sm[:, 0:1])
                K.act(sm[:, 1:2], sm[:, 0:1], AF.Ln, [sm], [sm], scale=1.0 / D, bias=EPS)
                K.act(sm[:, 2:3], sm[:, 1:2], AF.Exp, [sm], [sm], scale=-0.5)
                K.stt("dve", hf[:], xt[:], sm[:, 2:3], fnw[:], ALU.mult, ALU.mult, [xt, sm, fnw], [hf])
                K.dma("sp", s_o[t % 2], y_out[t * 128:(t + 1) * 128, :], hf[:], R=[hf], W=[y_b[t]])
            K.barrier()
        build_program.stats = dict(K.ninst, nsem=K.nsem)
    return nc


def _run(inp, NT, DEPTH, debug=False, ncores=8):
    B = inp["x"].shape[0]
    pf, pt = _pack_params(inp, DEPTH)
    fnw = np.ascontiguousarray(np.broadcast_to(np.asarray(inp["final_norm_w"], np.float32)[None, :], (128, D)))
    nc = build_program(NT, DEPTH, debug)
    in_maps = []
    for c in range(ncores):
        b = c % B
        in_maps.append({
            "x": np.ascontiguousarray(inp["x"][b], dtype=np.float32),
            "pos": np.ascontiguousarray(np.asarray(inp["positions"][b], np.int32).reshape(NT, 128).T),
            "w_in": np.ascontiguousarray(inp["w_in"][:DEPTH], dtype=np.float32),
            "w_out": np.ascontiguousarray(inp["w_out"][:DEPTH], dtype=np.float32),
            "w_up": np.ascontiguousarray(inp["w_up"][:DEPTH], dtype=np.float32),
            "w_down": np.ascontiguousarray(inp["w_down"][:DEPTH], dtype=np.float32),
            "pf": pf, "pt": pt, "fnw": fnw, "consts": CONST_TABLE,
        })
    res = run_bass_kernel_spmd(nc, in_maps, core_ids=list(range(ncores)))
    return res


def kernel(**inputs):
    inp = {k: np.asarray(v) for k, v in inputs.items()}
    B, S, _ = inp["x"].shape
    res = _run(inp, S // 128, inp["w_in"].shape[0])
    out = np.stack([np.asarray(res.results[b]["y"], dtype=np.float32) for b in range(B)], axis=0)
    return out
```
